# Optimizing a Trainium2 kernel written in Bass

```python
import math
import jax, jax.numpy as jnp
from jax import lax
import numpy as np

D_MODEL = 1024
BATCH = 2
SEQ = 8192
DEPTH = 4

N_MIXERS = 4
HEAD_DIM = 64
N_HEADS = D_MODEL // HEAD_DIM
ROT_DIM = HEAD_DIM // 4
ROPE_THETA = 500000.0
Q_BLOCK = 128
DIL_CONFIGS = ((128, 1), (512, 4), (2048, 16))
N_HEADS_DIL = 8
DIL_GROUP_WIDTH = N_HEADS_DIL * HEAD_DIM
DIL_IN_WIDTH = len(DIL_CONFIGS) * 3 * DIL_GROUP_WIDTH
IDX_HEADS = 8
IDX_DIM = 64
TOPK_TOKENS = 256
DSA_IN_WIDTH = 3 * D_MODEL + IDX_HEADS * IDX_DIM + IDX_DIM + IDX_HEADS
MOBA_BLOCK = 256
MOBA_TOPK = 3
MOBA_Q_CHUNK = 32
D_FF = 4 * D_MODEL
PLE_DIM = 256
EPS = 1e-6

kernel_name = 'hybrid_sb_dilated_dsa_moba_trunk'


def _n_layers_of(mixer):
    return len(range(mixer, DEPTH, N_MIXERS))


def rms_norm(x, g):
    x32 = x.astype(jnp.float32)
    y = x32 * lax.rsqrt(jnp.mean(x32 * x32, axis=-1, keepdims=True) + EPS)
    return (y * g.astype(jnp.float32)).astype(x.dtype)


def rope_tables(positions):
    inv_freq = ROPE_THETA ** (-jnp.arange(0, ROT_DIM, 2, dtype=jnp.float32) / ROT_DIM)
    ang = positions.astype(jnp.float32)[..., None] * inv_freq
    return jnp.cos(ang)[:, :, None, :], jnp.sin(ang)[:, :, None, :]


def apply_partial_rope(t, cos, sin):
    half = ROT_DIM // 2
    t1 = t[..., :half].astype(jnp.float32)
    t2 = t[..., half:ROT_DIM].astype(jnp.float32)
    rot = jnp.concatenate([t1 * cos - t2 * sin, t2 * cos + t1 * sin], axis=-1).astype(t.dtype)
    return jnp.concatenate([rot, t[..., ROT_DIM:]], axis=-1)


def stick_breaking_attention(x, w_in, w_out):
    B, S, _ = x.shape
    q, k, v = jnp.split(x @ w_in, 3, axis=-1)
    q = q.reshape(B, S, N_HEADS, HEAD_DIM)
    k = k.reshape(B, S, N_HEADS, HEAD_DIM)
    v = v.reshape(B, S, N_HEADS, HEAD_DIM)
    scale = HEAD_DIM ** -0.5
    outs = []
    for blk in range(S // Q_BLOCK):
        t0 = blk * Q_BLOCK
        L = t0 + Q_BLOCK
        z = jnp.einsum('bqhd,bkhd->bhqk', q[:, t0:L], k[:, :L]).astype(jnp.float32) * scale
        t_idx = t0 + jnp.arange(Q_BLOCK)[:, None]
        s_idx = jnp.arange(L)[None, :]
        past = s_idx < t_idx
        log_fail = jnp.where(past, jax.nn.log_sigmoid(-z), 0.0)
        later = lax.cumsum(log_fail, axis=3, reverse=True) - log_fail
        a = jnp.where(past, jnp.exp(jax.nn.log_sigmoid(z) + later), 0.0)
        outs.append(jnp.einsum('bhqk,bkhd->bqhd', a.astype(v.dtype), v[:, :L]))
    o = jnp.concatenate(outs, axis=1).reshape(B, S, N_HEADS * HEAD_DIM)
    return o @ w_out


def banded_attention(q, k, v, n_back):
    N, L, H, Dh = q.shape
    nb = L // Q_BLOCK
    qb = q.reshape(N, nb, Q_BLOCK, H, Dh)
    kb = k.reshape(N, nb, Q_BLOCK, H, Dh)
    vb = v.reshape(N, nb, Q_BLOCK, H, Dh)
    zero = jnp.zeros_like(kb[:, :1])
    k2 = jnp.concatenate([jnp.concatenate([zero, kb[:, :-1]], axis=1), kb], axis=2)
    v2 = jnp.concatenate([jnp.concatenate([zero, vb[:, :-1]], axis=1), vb], axis=2)
    s = jnp.einsum('nbqhd,nbkhd->nbhqk', qb, k2).astype(jnp.float32) * (Dh ** -0.5)
    dist = Q_BLOCK + jnp.arange(Q_BLOCK)[:, None] - jnp.arange(2 * Q_BLOCK)[None, :]
    band = (dist >= 0) & (dist <= n_back)
    first_pad = (jnp.arange(nb) == 0)[:, None, None] & (jnp.arange(2 * Q_BLOCK) < Q_BLOCK)[None, None, :]
    mask = band[None] & ~first_pad
    s = jnp.where(mask[None, :, None], s, -jnp.inf)
    lse = jax.nn.logsumexp(s, axis=-1)
    prob = jnp.exp(s - lse[..., None])
    o = jnp.einsum('nbhqk,nbkhd->nbqhd', prob.astype(v.dtype), v2)
    return o.reshape(N, L, H, Dh), lse.transpose(0, 1, 3, 2).reshape(N, L, H)


def dilated_attention(x, w_in, w_out, cos, sin):
    B, S, _ = x.shape
    H = N_HEADS_DIL
    G = len(DIL_CONFIGS)
    proj = (x @ w_in).reshape(B, S, G, 3, H, HEAD_DIM)
    outs, lses = [], []
    for g, (window, dil) in enumerate(DIL_CONFIGS):
        q = apply_partial_rope(proj[:, :, g, 0], cos, sin)
        k = apply_partial_rope(proj[:, :, g, 1], cos, sin)
        v = proj[:, :, g, 2]
        L = S // dil
        Lp = -(-L // Q_BLOCK) * Q_BLOCK

        def strided(t):
            t = t.reshape(B, L, dil, H, HEAD_DIM).transpose(0, 2, 1, 3, 4).reshape(B * dil, L, H, HEAD_DIM)
            return jnp.pad(t, ((0, 0), (0, Lp - L), (0, 0), (0, 0)))

        o, lse = banded_attention(strided(q), strided(k), strided(v), window // dil)
        o = o[:, :L].reshape(B, dil, L, H, HEAD_DIM).transpose(0, 2, 1, 3, 4).reshape(B, S, H, HEAD_DIM)
        lse = lse[:, :L].reshape(B, dil, L, H).transpose(0, 2, 1, 3).reshape(B, S, H)
        outs.append(o)
        lses.append(lse)
    alpha = jax.nn.softmax(jnp.stack(lses, axis=0), axis=0)
    o = jnp.einsum('gbsh,gbshd->bshd', alpha, jnp.stack(outs, axis=0).astype(jnp.float32)).astype(x.dtype)
    return o.reshape(B, S, H * HEAD_DIM) @ w_out


def dsa_attention(x, w_in, w_out, cos, sin):
    B, S, _ = x.shape
    D = N_HEADS * HEAD_DIM
    cuts = [D, 2 * D, 3 * D, 3 * D + IDX_HEADS * IDX_DIM, 3 * D + IDX_HEADS * IDX_DIM + IDX_DIM]
    q, k, v, qi, ki, wi = jnp.split(x @ w_in, cuts, axis=-1)
    q = apply_partial_rope(q.reshape(B, S, N_HEADS, HEAD_DIM), cos, sin)
    k = apply_partial_rope(k.reshape(B, S, N_HEADS, HEAD_DIM), cos, sin)
    v = v.reshape(B, S, N_HEADS, HEAD_DIM)
    qi = apply_partial_rope(qi.reshape(B, S, IDX_HEADS, IDX_DIM), cos, sin)
    ki = apply_partial_rope(ki.reshape(B, S, 1, IDX_DIM), cos, sin)[:, :, 0]
    wi = wi.astype(jnp.float32) * (IDX_HEADS ** -0.5)
    topk = min(TOPK_TOKENS, S // 4)
    scale = HEAD_DIM ** -0.5
    key_pos = jnp.arange(S)

    def block(bi):
        t0 = bi * Q_BLOCK
        t_pos = t0 + jnp.arange(Q_BLOCK)
        qb = lax.dynamic_slice_in_dim(q, t0, Q_BLOCK, axis=1)
        qib = lax.dynamic_slice_in_dim(qi, t0, Q_BLOCK, axis=1)
        wib = lax.dynamic_slice_in_dim(wi, t0, Q_BLOCK, axis=1)
        rel = jax.nn.relu(jnp.einsum('bqhd,bkd->bqhk', qib, ki).astype(jnp.float32))
        score = jnp.einsum('bqh,bqhk->bqk', wib, rel)
        causal = key_pos[None, :] <= t_pos[:, None]
        score = jnp.where(causal[None], score, -jnp.inf)
        _, idx = lax.top_k(score, topk)
        valid = idx <= t_pos[None, :, None]
        k_sel = jax.vmap(lambda kk, ii: kk[ii])(k, idx)
        v_sel = jax.vmap(lambda vv, ii: vv[ii])(v, idx)
        s = jnp.einsum('bqhd,bqkhd->bqhk', qb, k_sel).astype(jnp.float32) * scale
        s = jnp.where(valid[:, :, None, :], s, -jnp.inf)
        prob = jax.nn.softmax(s, axis=-1)
        return jnp.einsum('bqhk,bqkhd->bqhd', prob.astype(v.dtype), v_sel)

    o = lax.map(block, jnp.arange(S // Q_BLOCK))
    o = o.transpose(1, 0, 2, 3, 4).reshape(B, S, D)
    return o @ w_out


def moba_attention(x, w_in, w_out, cos, sin):
    B, S, _ = x.shape
    H = N_HEADS
    q, k, v = jnp.split(x @ w_in, 3, axis=-1)
    q = apply_partial_rope(q.reshape(B, S, H, HEAD_DIM), cos, sin)
    k = apply_partial_rope(k.reshape(B, S, H, HEAD_DIM), cos, sin)
    v = v.reshape(B, S, H, HEAD_DIM)
    nb = -(-S // MOBA_BLOCK)
    Sp = nb * MOBA_BLOCK
    pad = ((0, 0), (0, Sp - S), (0, 0), (0, 0))
    kb = jnp.pad(k, pad).reshape(B, nb, MOBA_BLOCK, H, HEAD_DIM).transpose(0, 3, 1, 2, 4)
    vb = jnp.pad(v, pad).reshape(B, nb, MOBA_BLOCK, H, HEAD_DIM).transpose(0, 3, 1, 2, 4)
    k_mean = jnp.mean(kb.astype(jnp.float32), axis=3)
    topk = min(MOBA_TOPK, nb - 1)
    scale = HEAD_DIM ** -0.5
    b_ix = jnp.arange(B)[:, None, None, None]
    h_ix = jnp.arange(H)[None, None, :, None]

    def chunk(ci):
        t0 = ci * MOBA_Q_CHUNK
        t_pos = t0 + jnp.arange(MOBA_Q_CHUNK)
        cur = t0 // MOBA_BLOCK
        qc = lax.dynamic_slice_in_dim(q, t0, MOBA_Q_CHUNK, axis=1)
        k_own = lax.dynamic_index_in_dim(kb, cur, axis=2, keepdims=False)
        v_own = lax.dynamic_index_in_dim(vb, cur, axis=2, keepdims=False)
        own_pos = cur * MOBA_BLOCK + jnp.arange(MOBA_BLOCK)
        s_own = jnp.einsum('bqhd,bhkd->bqhk', qc, k_own).astype(jnp.float32) * scale
        s_own = jnp.where((own_pos[None, :] <= t_pos[:, None])[None, :, None, :], s_own, -jnp.inf)
        if topk == 0:
            prob = jax.nn.softmax(s_own, axis=-1)
            return jnp.einsum('bqhk,bhkd->bqhd', prob.astype(v.dtype), v_own)
        gate = jnp.einsum('bqhd,bhnd->bqhn', qc.astype(jnp.float32), k_mean)
        gate = jnp.where(jnp.arange(nb) < cur, gate, -jnp.inf)
        _, sel = lax.top_k(gate, topk)
        valid = sel < cur
        k_sel = kb[b_ix, h_ix, sel]
        v_sel = vb[b_ix, h_ix, sel]
        s_sel = jnp.einsum('bqhd,bqhjkd->bqhjk', qc, k_sel).astype(jnp.float32) * scale
        s_sel = jnp.where(valid[..., None], s_sel, -jnp.inf).reshape(B, MOBA_Q_CHUNK, H, topk * MOBA_BLOCK)
        prob = jax.nn.softmax(jnp.concatenate([s_sel, s_own], axis=-1), axis=-1).astype(v.dtype)
        p_sel = prob[..., :topk * MOBA_BLOCK].reshape(B, MOBA_Q_CHUNK, H, topk, MOBA_BLOCK)
        p_own = prob[..., topk * MOBA_BLOCK:]
        return (jnp.einsum('bqhjk,bqhjkd->bqhd', p_sel, v_sel)
                + jnp.einsum('bqhk,bhkd->bqhd', p_own, v_own))

    o = lax.map(chunk, jnp.arange(S // MOBA_Q_CHUNK))
    o = o.transpose(1, 0, 2, 3, 4).reshape(B, S, H * HEAD_DIM)
    return o @ w_out


def setup_inputs(seed: int = 0) -> dict:
    key = jax.random.key(seed)
    ks = iter(jax.random.split(key, 32))

    def dense(shape, fan_in):
        return jax.random.normal(next(ks), shape, jnp.float32) * (fan_in ** -0.5)

    def gain(shape):
        return 1.0 + 0.05 * jax.random.normal(next(ks), shape, jnp.float32)

    nA, nB, nC, nD = (_n_layers_of(m) for m in range(N_MIXERS))
    x = jax.random.normal(next(ks), (BATCH, SEQ, D_MODEL), jnp.float32)
    p = jax.random.normal(next(ks), (DEPTH, BATCH, SEQ, PLE_DIM), jnp.float32)
    offsets = jax.random.randint(next(ks), (BATCH, 1), 0, 4096, dtype=jnp.int32)
    positions = offsets + jnp.arange(SEQ, dtype=jnp.int32)[None, :]
    return {
        'x': x,
        'p': p,
        'positions': positions,
        'w_in_sb': dense((nA, D_MODEL, 3 * D_MODEL), D_MODEL),
        'w_out_sb': dense((nA, D_MODEL, D_MODEL), D_MODEL),
        'w_in_dil': dense((nB, D_MODEL, DIL_IN_WIDTH), D_MODEL),
        'w_out_dil': dense((nB, DIL_GROUP_WIDTH, D_MODEL), DIL_GROUP_WIDTH),
        'w_in_dsa': dense((nC, D_MODEL, DSA_IN_WIDTH), D_MODEL),
        'w_out_dsa': dense((nC, D_MODEL, D_MODEL), D_MODEL),
        'w_in_moba': dense((nD, D_MODEL, 3 * D_MODEL), D_MODEL),
        'w_out_moba': dense((nD, D_MODEL, D_MODEL), D_MODEL),
        'g_mix_pre': gain((DEPTH, D_MODEL)),
        'g_mix_post': gain((DEPTH, D_MODEL)),
        'g_ffn_pre': gain((DEPTH, D_MODEL)),
        'g_ffn_post': gain((DEPTH, D_MODEL)),
        'w_ff_in': dense((DEPTH, D_MODEL, D_FF), D_MODEL),
        'w_ff_out': dense((DEPTH, D_FF, D_MODEL), D_FF),
        'g_ple': gain((DEPTH, D_MODEL)),
        'w_ple_gate': dense((DEPTH, D_MODEL, D_MODEL), D_MODEL),
        'w_ple': dense((DEPTH, PLE_DIM, D_MODEL), PLE_DIM),
    }


def reference(x, p, positions, w_in_sb, w_out_sb, w_in_dil, w_out_dil, w_in_dsa, w_out_dsa,
              w_in_moba, w_out_moba, g_mix_pre, g_mix_post, g_ffn_pre, g_ffn_post,
              w_ff_in, w_ff_out, g_ple, w_ple_gate, w_ple):
    cos, sin = rope_tables(positions)
    h = x
    for i in range(DEPTH):
        mixer, j = i % N_MIXERS, i // N_MIXERS
        u = rms_norm(h, g_mix_pre[i])
        if mixer == 0:
            y = stick_breaking_attention(u, w_in_sb[j], w_out_sb[j])
        elif mixer == 1:
            y = dilated_attention(u, w_in_dil[j], w_out_dil[j], cos, sin)
        elif mixer == 2:
            y = dsa_attention(u, w_in_dsa[j], w_out_dsa[j], cos, sin)
        else:
            y = moba_attention(u, w_in_moba[j], w_out_moba[j], cos, sin)
        h = h + rms_norm(y, g_mix_post[i])
        u = rms_norm(h, g_ffn_pre[i])
        f = jnp.square(jax.nn.relu(u @ w_ff_in[i])) @ w_ff_out[i]
        h = h + rms_norm(f, g_ffn_post[i])
        gate = jax.nn.sigmoid(rms_norm(h, g_ple[i]) @ w_ple_gate[i])
        h = h + (p[i] @ w_ple[i]) * gate
    return h
```

```python
import contextlib
import math
import numpy as np
import ml_dtypes
import concourse.bass as bass
import concourse.mybir as mybir
from concourse.bass_utils import run_bass_kernel_spmd

F32 = mybir.dt.float32
BF16 = mybir.dt.bfloat16
I32 = mybir.dt.int32
AF = mybir.ActivationFunctionType
ALU = mybir.AluOpType
AX = mybir.AxisListType

SEM_LIMIT = 30000
DMA_POOL = 12
NEG = -1.0e30
EPS = 1e-6
D = 1024
TOK = 2048
NT = 16

LAYERS = [
    dict(name="sb", nq=16, nk=16, nv=1024, rope=False, qs=16, no=1024),
    dict(name="dil", nq=24, nk=24, nv=1536, rope=True, qs=24, no=512),
    dict(name="dsa", nq=24, nk=18, nv=1032, rope=True, qs=16, no=1024),
    dict(name="moba", nq=16, nk=16, nv=1024, rope=True, qs=16, no=1024),
]
DIL_CFG = ((128, 1), (512, 4), (2048, 16))


class Res:
    __slots__ = ("name", "lastw", "readers")

    def __init__(self, name):
        self.name = name
        self.lastw = None
        self.readers = []


class Op:
    __slots__ = ("eng", "fn", "deps", "signal", "sem", "tick", "dma", "idx", "slot_prev")

    def __init__(self, eng, fn, dma):
        self.eng = eng
        self.fn = fn
        self.deps = []
        self.signal = False
        self.sem = None
        self.tick = 0
        self.dma = dma
        self.slot_prev = None


class Prog:
    ENGS = ("pe", "act", "dve", "pool", "sp")

    def __init__(self, nc, stack):
        self.nc = nc
        self.stack = stack
        self.ops = []
        self.by_eng = {e: [] for e in self.ENGS}
        self.nres = 0
        self.sb_off = 0
        self.ntens = 0
        self.dma_since = []

    ARENA = 204800

    def init_arena(self):
        a = self.stack.enter_context(self.nc.sbuf_tensor("arena", [128, self.ARENA // 2], BF16))
        self.h16 = a
        self.h32 = a.bitcast(F32)
        self.hi32 = a.bitcast(I32)

    def alloc_bytes(self, n):
        n = (n + 31) // 32 * 32
        off = self.sb_off
        self.sb_off += n
        assert self.sb_off <= self.ARENA, f"SBUF overflow {self.sb_off}"
        return off

    def mark(self):
        return self.sb_off

    def release(self, mark):
        self.sb_off = mark

    def res(self, name=None):
        self.nres += 1
        return Res(name or f"r{self.nres}")

    def op(self, eng, fn, reads=(), writes=(), dma=False, acc=False):
        o = Op(eng, fn, dma)
        o.idx = len(self.ops)
        deps = set()
        for r in reads:
            if r.lastw is not None:
                deps.add(r.lastw)
        for w in writes:
            if w.lastw is not None:
                lw = self.ops[w.lastw]
                if not (acc and lw.eng == "pe" and eng == "pe" and not lw.dma):
                    deps.add(w.lastw)
            for rd in w.readers:
                deps.add(rd)
        deps.discard(o.idx)
        o.deps = sorted(deps)
        best = {}
        keep = []
        for d in deps:
            po = self.ops[d]
            if po.dma or po.fn is None:
                keep.append(d)
            elif po.eng not in best or best[po.eng] < d:
                best[po.eng] = d
        o.deps = sorted(keep + list(best.values()))
        for r in reads:
            if not dma:
                r.readers = [x for x in r.readers if self.ops[x].dma or self.ops[x].eng != eng]
            r.readers.append(o.idx)
        for w in writes:
            w.lastw = o.idx
            w.readers = []
        self.ops.append(o)
        self.by_eng[eng].append(o)
        if dma:
            self.dma_since.append(o.idx)
        return o

    def wait_only(self, eng, deps):
        o = Op(eng, None, False)
        o.idx = len(self.ops)
        o.deps = sorted(set(deps))
        self.ops.append(o)
        self.by_eng[eng].append(o)
        return o

    def barrier(self):
        deps = list(self.dma_since)
        for e in self.ENGS:
            for o in reversed(self.by_eng[e]):
                if o.fn is not None and not o.dma:
                    deps.append(o.idx)
                    break
        for e in self.ENGS:
            self.wait_only(e, deps)
        self.dma_since = []

    def dma(self, eng, out, in_, reads=(), writes=()):
        return self.op(eng, lambda e: e.dma_start(out=out, in_=in_), reads, writes, dma=True)

    def mm(self, out, lhsT, rhs, start, stop, reads=(), writes=(), acc=False):
        return self.op("pe", lambda e: e.matmul(out=out, lhsT=lhsT, rhs=rhs, start=start, stop=stop),
                       reads, writes, acc=acc)

    def tr(self, out, in_, ident, reads=(), writes=(), acc=False):
        return self.op("pe", lambda e: e.transpose(out=out, in_=in_, identity=ident), reads, writes, acc=acc)

    def act(self, out, in_, func, reads=(), writes=(), scale=1.0, bias=0.0, accum_out=None):
        def fn(e):
            kw = {}
            if accum_out is not None:
                kw["accum_out"] = accum_out
            return e.activation(out=out, in_=in_, func=func, bias=bias, scale=scale, **kw)
        return self.op("act", fn, reads, writes)

    def tt(self, eng, out, in0, in1, op, reads=(), writes=()):
        return self.op(eng, lambda e: e.tensor_tensor(out=out, in0=in0, in1=in1, op=op), reads, writes)

    def ts(self, eng, out, in0, s1, s2, op0, op1=None, reads=(), writes=(), accum_out=None):
        def fn(e):
            kw = {}
            if accum_out is not None:
                kw["accum_out"] = accum_out
            if op1 is None:
                return e.tensor_scalar(out=out, in0=in0, scalar1=s1, scalar2=None, op0=op0, **kw)
            return e.tensor_scalar(out=out, in0=in0, scalar1=s1, scalar2=s2, op0=op0, op1=op1, **kw)
        return self.op(eng, fn, reads, writes)

    def stt(self, eng, out, in0, scalar, in1, op0, op1, reads=(), writes=()):
        return self.op(eng, lambda e: e.scalar_tensor_tensor(out=out, in0=in0, scalar=scalar, in1=in1,
                                                             op0=op0, op1=op1), reads, writes)

    def copy(self, eng, out, in_, reads=(), writes=()):
        if eng == "act":
            return self.act(out, in_, AF.Copy, reads, writes)
        return self.op(eng, lambda e: e.tensor_copy(out=out, in_=in_), reads, writes)

    def memset(self, eng, ap, val, writes=()):
        return self.op(eng, lambda e: e.memset(ap, val), (), writes)

    def emit(self):
        nc = self.nc
        ops = self.ops
        for o in ops:
            for d in o.deps:
                ops[d].signal = True
        sems = []

        def new_sem(nm):
            s = self.stack.enter_context(nc.semaphore(nm))
            sems.append(s)
            return s

        for e in self.ENGS:
            cur = None
            cnt = 0
            slots = []
            nslot = 0
            for o in self.by_eng[e]:
                if o.fn is None:
                    continue
                if o.dma:
                    o.signal = True
                    si = nslot % DMA_POOL
                    if si >= len(slots):
                        slots.append([new_sem(f"d_{e}_{si}"), 0, None])
                    s = slots[si]
                    nslot += 1
                    o.slot_prev = s[2]
                    s[1] += 16
                    if s[1] > SEM_LIMIT:
                        s[0] = new_sem(f"d_{e}_x{o.idx}")
                        s[1] = 16
                    o.sem, o.tick = s[0], s[1]
                    s[2] = o
                elif o.signal:
                    if cur is None or cnt >= SEM_LIMIT:
                        cur = new_sem(f"c_{e}_{o.idx}")
                        cnt = 0
                    cnt += 1
                    o.sem, o.tick = cur, cnt
        self.n_sems = len(sems)

        def emit_engine(ename, eobj):
            seen = {}
            for o in self.by_eng[ename]:
                need = {}
                plist = [ops[d] for d in o.deps]
                if o.dma and o.slot_prev is not None:
                    plist.append(o.slot_prev)
                for p in plist:
                    if p.sem is None:
                        continue
                    k = id(p.sem)
                    if seen.get(k, 0) >= p.tick:
                        continue
                    if k not in need or need[k][1] < p.tick:
                        need[k] = (p.sem, p.tick)
                for k, (s, v) in need.items():
                    eobj.wait_ge(s, v)
                    seen[k] = v
                if o.fn is None:
                    continue
                ins = o.fn(eobj)
                if o.signal:
                    ins.then_inc(o.sem, 16 if o.dma else 1)

        with nc.Block() as block:
            @block.sync
            def _(e):
                emit_engine("sp", e)

            @block.tensor
            def _(e):
                emit_engine("pe", e)

            @block.scalar
            def _(e):
                emit_engine("act", e)

            @block.vector
            def _(e):
                emit_engine("dve", e)

            @block.gpsimd
            def _(e):
                emit_engine("pool", e)


class Buf:
    def __init__(self, P, shape, dtype, name=None):
        self.shape = list(shape)
        self.dtype = dtype
        self.row = int(np.prod(shape[1:]))
        esz = 2 if dtype == BF16 else 4
        off = P.alloc_bytes(self.row * esz)
        self.h = P.h16 if dtype == BF16 else (P.h32 if dtype == F32 else P.hi32)
        self.base = off // esz
        self.ps = P.ARENA // esz
        self.r = P.res(name)
        pat = [[self.ps, shape[0]]]
        st = self.row
        for d in shape[1:]:
            st //= d
            pat.append([st, d])
        self.t = bass.AP(self.h, self.base, pat)

    def __getitem__(self, k):
        return self.t[k]

    def ap(self, off, free, p0=0, npart=None):
        if npart is None:
            npart = self.shape[0] - p0
        return bass.AP(self.h, self.base + p0 * self.ps + off, [[self.ps, npart]] + [list(x) for x in free])


class Ctx:
    def __init__(self, nc, stack):
        self.nc = nc
        self.P = Prog(nc, stack)
        P = self.P
        P.init_arena()
        self.ps = []
        for i in range(7):
            t = stack.enter_context(nc.psum_tensor(f"psf{i}", [128, 512], F32))
            self.ps.append((t, P.res(f"psf{i}")))
        self.pb = []
        t = stack.enter_context(nc.psum_tensor("psb0", [128, 1024], BF16))
        r = P.res("psb0")
        self.pb = [(t, r), (t, r)]
        self.dram = {}
        self.outs = []

    def din(self, name, shape, dtype):
        t = self.nc.dram_tensor(name, list(shape), dtype, kind="ExternalInput").ap()
        self.dram[name] = (t, self.P.res(name))
        return t

    def dout(self, name, shape, dtype):
        t = self.nc.dram_tensor(name, list(shape), dtype, kind="ExternalOutput").ap()
        self.dram[name] = (t, self.P.res(name))
        self.outs.append(name)
        return t

    def dint(self, name, shape, dtype):
        t = self.nc.dram_tensor(name, list(shape), dtype, kind="Internal").ap()
        self.dram[name] = (t, self.P.res(name))
        return t

    def consts(self):
        P = self.P
        self.ident = Buf(P, [128, 128], BF16, "ident")
        self.ones16 = Buf(P, [128, 128], BF16, "ones16")
        self.ones32 = Buf(P, [128, 64], F32, "ones32")
        self.cinfo = Buf(P, [128, 4], F32, "cinfo")
        self.d0 = Buf(P, [128, 128], F32, "d0")
        self.d0t = Buf(P, [128, 128], F32, "d0t")
        tmpi = Buf(P, [128, 128], I32, "tmpi")
        P.memset("pool", self.ident[:], 0.0, writes=[self.ident.r])
        P.op("pool", lambda e: e.affine_select(out=self.ident[:], in_=self.ident[:], pattern=[[-1, 128]],
                                               compare_op=ALU.not_equal, fill=1.0, base=0, channel_multiplier=1),
             reads=[self.ident.r], writes=[self.ident.r])
        P.memset("pool", self.ones16[:], 1.0, writes=[self.ones16.r])
        P.memset("pool", self.ones32[:], 1.0, writes=[self.ones32.r])
        ci, cr = self.dram["cinfo"]
        P.dma("sp", self.cinfo[:], ci[:, :], reads=[cr], writes=[self.cinfo.r])
        P.op("pool", lambda e: e.iota(tmpi[:], pattern=[[1, 128]], base=0, channel_multiplier=-1), writes=[tmpi.r])
        P.copy("dve", self.d0[:], tmpi[:], reads=[tmpi.r], writes=[self.d0.r])
        P.op("pool", lambda e: e.iota(tmpi[:], pattern=[[-1, 128]], base=0, channel_multiplier=1),
             reads=[self.d0.r], writes=[tmpi.r])
        P.copy("dve", self.d0t[:], tmpi[:], reads=[tmpi.r], writes=[self.d0t.r])


def load_weight(cx, dst, src, K, N, stage):
    P = cx.P
    src_ap, src_r = src
    i = 0
    SW = stage[0].shape[1]
    for kc in range((K + 127) // 128):
        rows = min(128, K - kc * 128)
        for n0 in range(0, N, SW):
            n = min(SW, N - n0)
            st = stage[i % 2]
            i += 1
            P.dma("sp", st[0:rows, 0:n], src_ap[kc * 128:kc * 128 + rows, n0:n0 + n], reads=[src_r], writes=[st.r])
            P.copy("pool", dst[0:rows, kc, n0:n0 + n], st[0:rows, 0:n], reads=[st.r], writes=[dst.r])


def load_gain(cx, name, row):
    P = cx.P
    g = Buf(P, [128, D], F32, name)
    ap, r = cx.dram[name]
    P.dma("sp", g[:], ap[row:row + 1, :].partition_broadcast(128), reads=[r], writes=[g.r])
    return g


def rms_rstd(cx, src_ap, src_res, junk, sm, col):
    P = cx.P
    P.act(junk[:], src_ap, AF.Square, reads=src_res, writes=[junk.r, sm.r], accum_out=sm[:, col:col + 1])
    P.act(sm[:, col + 1:col + 2], sm[:, col:col + 1], AF.Sqrt, reads=[sm.r], writes=[sm.r], scale=1.0 / D, bias=EPS)
    P.op("dve", lambda e: e.reciprocal(out=sm[:, col + 2:col + 3], in_=sm[:, col + 1:col + 2]),
         reads=[sm.r], writes=[sm.r])
    return sm[:, col + 2:col + 3]


def transpose_rows(cx, dstT, src, nchunks, k, src_reads):
    P = cx.P
    for c0 in range(0, nchunks, 8):
        n = min(8, nchunks - c0)
        pbt, pbr = cx.pb[(k + c0 // 8) % 2]
        for c in range(n):
            P.tr(pbt[:, c * 128:(c + 1) * 128], src[:, (c0 + c) * 128:(c0 + c + 1) * 128], cx.ident[:],
                 reads=list(src_reads) + [cx.ident.r], writes=[pbr], acc=(c > 0))
        eng = "act" if (c0 // 8) % 2 == 0 else "dve"
        P.copy(eng, dstT.ap(c0 * 128, [[1, n * 128]]), pbt[:, 0:n * 128], reads=[pbr], writes=[dstT.r])


def phase_pre(cx, li, ple_li):
    P = cx.P
    mark = P.mark()
    do_ple = ple_li is not None
    do_in = li is not None
    stage = [Buf(P, [128, 1024], F32, "stage") for _ in range(2)]
    hb = [Buf(P, [128, D], F32, "hs") for _ in range(2)]
    sm = Buf(P, [128, 16], F32, "sm")
    ub = Buf(P, [128, D], BF16, "u")
    junk = ub
    uT = Buf(P, [128, 8, 128], BF16, "uT")
    h_ap, h_r = cx.dram["h_in"]
    if do_ple:
        wg = Buf(P, [128, 8, D], BF16, "wg")
        wp = Buf(P, [128, 2, D], BF16, "wp")
        load_weight(cx, wg, cx.dram["w_ple_gate"], D, D, stage)
        load_weight(cx, wp, cx.dram["w_ple"], 256, D, stage)
        gple = load_gain(cx, "g_ple", 0)
        pt = [Buf(P, [128, 256], F32, "p") for _ in range(2)]
        pbf = Buf(P, [128, 256], BF16, "pbf")
        pT = Buf(P, [128, 2, 128], BF16, "pT")
        gate = Buf(P, [128, D], F32, "gate")
        tmp = Buf(P, [128, D], F32, "tmp")
        p_ap, p_r = cx.dram["p"]
        ho_ap, ho_r = cx.dram["h_out"]
    if do_in:
        L = LAYERS[li]
        nT = L["nq"] + L["nk"]
        TC = nT * 64
        NV = L["nv"]
        NW = TC + NV
        win = Buf(P, [128, 8, NW], BF16, "win")
        load_weight(cx, win, cx.dram["w_in"], D, NW, stage)
        gpre = load_gain(cx, "g_mix_pre", 0)
        tq = Buf(P, [128, TC], F32, "tq")
        tqb = Buf(P, [128, TC], BF16, "tqb")
        tTs = [Buf(P, [128, nT // 2, 128], BF16, "tTs") for _ in range(2)]
        tvs = [Buf(P, [128, NV], BF16, "tvs") for _ in range(2)]
        qT_ap, qT_r = cx.dram["qT_o"]
        kT_ap, kT_r = cx.dram["kT_o"]
        tv_ap, tv_r = cx.dram["tv_o"]
        if L["name"] == "dsa":
            wis = Buf(P, [128, 8], F32, "wis")
            wi_ap, wi_r = cx.dram["wi_o"]
        if L["rope"]:
            cs = Buf(P, [128, 2, NT, 8], F32, "cs")
            posi = Buf(P, [128, NT], I32, "posi")
            posf = Buf(P, [128, NT], F32, "posf")
            invf = Buf(P, [128, 8], F32, "invf")
            ang = Buf(P, [128, 2, NT, 8], F32, "ang")
            kk = Buf(P, [128, 2, NT, 8], F32, "kk")
            kki = Buf(P, [128, 2, NT, 8], I32, "kki")
            rt = Buf(P, [128, 4, nT * 8], F32, "ropetmp")
            pa, pr = cx.dram["pos"]
            ia, ir = cx.dram["invf"]
            P.dma("sp", posi[:], pa[:, :], reads=[pr], writes=[posi.r])
            P.dma("sp", invf[:], ia[:, :], reads=[ir], writes=[invf.r])
            P.copy("dve", posf[:], posi[:], reads=[posi.r], writes=[posf.r])
            NA = NT * 8
            P.tt("dve", ang.ap(NA, [[8, NT], [1, 8]]), posf.ap(0, [[1, NT], [0, 8]]), invf.ap(0, [[0, NT], [1, 8]]),
                 ALU.mult, reads=[posf.r, invf.r], writes=[ang.r])
            P.ts("dve", ang.ap(0, [[1, NA]]), ang.ap(NA, [[1, NA]]), math.pi / 2, None, ALU.add,
                 reads=[ang.r], writes=[ang.r])
            P.ts("dve", kk.ap(0, [[1, 2 * NA]]), ang.ap(0, [[1, 2 * NA]]), 1.0 / (2 * math.pi), None, ALU.mult,
                 reads=[ang.r], writes=[kk.r])
            P.copy("dve", kki.ap(0, [[1, 2 * NA]]), kk.ap(0, [[1, 2 * NA]]), reads=[kk.r], writes=[kki.r])
            P.copy("dve", kk.ap(0, [[1, 2 * NA]]), kki.ap(0, [[1, 2 * NA]]), reads=[kki.r], writes=[kk.r])
            P.stt("dve", ang.ap(0, [[1, 2 * NA]]), kk.ap(0, [[1, 2 * NA]]), -2 * math.pi, ang.ap(0, [[1, 2 * NA]]),
                  ALU.mult, ALU.add, reads=[kk.r, ang.r], writes=[ang.r])
            P.ts("dve", kk.ap(0, [[1, 2 * NA]]), ang.ap(0, [[1, 2 * NA]]), math.pi, -2 * math.pi, ALU.is_gt, ALU.mult,
                 reads=[ang.r], writes=[kk.r])
            P.tt("dve", ang.ap(0, [[1, 2 * NA]]), ang.ap(0, [[1, 2 * NA]]), kk.ap(0, [[1, 2 * NA]]), ALU.add,
                 reads=[ang.r, kk.r], writes=[ang.r])
            P.ts("dve", kk.ap(0, [[1, 2 * NA]]), ang.ap(0, [[1, 2 * NA]]), -math.pi, 2 * math.pi, ALU.is_lt, ALU.mult,
                 reads=[ang.r], writes=[kk.r])
            P.tt("dve", ang.ap(0, [[1, 2 * NA]]), ang.ap(0, [[1, 2 * NA]]), kk.ap(0, [[1, 2 * NA]]), ALU.add,
                 reads=[ang.r, kk.r], writes=[ang.r])
            P.act(cs.ap(0, [[1, 2 * NA]]), ang.ap(0, [[1, 2 * NA]]), AF.Sin, reads=[ang.r], writes=[cs.r])

    for m in range(NT):
        hs = hb[m % 2]
        P.dma("sp", hs[:], h_ap[m * 128:(m + 1) * 128, :], reads=[h_r], writes=[hs.r])
        if do_ple:
            pp = pt[m % 2]
            P.dma("sp", pp[:], p_ap[m * 128:(m + 1) * 128, :], reads=[p_r], writes=[pp.r])
            rstd = rms_rstd(cx, hs[:], [hs.r], junk, sm, 0)
            P.stt("dve", ub[:], hs[:], rstd, gple[:], ALU.mult, ALU.mult, reads=[hs.r, sm.r, gple.r], writes=[ub.r])
            transpose_rows(cx, uT, ub, 8, m, [ub.r])
            (g0, g0r), (g1, g1r) = cx.ps[0], cx.ps[1]
            for n, (gt, gr) in enumerate(((g0, g0r), (g1, g1r))):
                for c in range(8):
                    P.mm(gt[:, :], uT[:, c, :], wg[:, c, n * 512:(n + 1) * 512], c == 0, c == 7,
                         reads=[uT.r, wg.r], writes=[gr], acc=(c > 0))
                P.act(gate[:, n * 512:(n + 1) * 512], gt[:, :], AF.Sigmoid, reads=[gr], writes=[gate.r])
            P.copy("pool", pbf[:], pp[:], reads=[pp.r], writes=[pbf.r])
            transpose_rows(cx, pT, pbf, 2, m + 1, [pbf.r])
            for n in range(2):
                wt, wr = cx.ps[2 + n]
                for c in range(2):
                    P.mm(wt[:, :], pT[:, c, :], wp[:, c, n * 512:(n + 1) * 512], c == 0, c == 1,
                         reads=[pT.r, wp.r], writes=[wr], acc=(c > 0))
                P.tt("dve", tmp[:, n * 512:(n + 1) * 512], wt[:, :], gate[:, n * 512:(n + 1) * 512], ALU.mult,
                     reads=[wr, gate.r], writes=[tmp.r])
            P.tt("pool", hs[:], hs[:], tmp[:], ALU.add, reads=[hs.r, tmp.r], writes=[hs.r])
            P.dma("pool", ho_ap[m * 128:(m + 1) * 128, :], hs[:], reads=[hs.r], writes=[ho_r])
        if not do_in:
            continue
        rstd = rms_rstd(cx, hs[:], [hs.r], junk, sm, 4)
        P.stt("dve", ub[:], hs[:], rstd, gpre[:], ALU.mult, ALU.mult, reads=[hs.r, sm.r, gpre.r], writes=[ub.r])
        transpose_rows(cx, uT, ub, 8, m, [ub.r])
        tv_s = tvs[m % 2]
        nchunk = 0
        chunks = [(a, min(512, TC - a)) for a in range(0, TC, 512)] + \
                 [(TC + a, min(512, NV - a)) for a in range(0, NV, 512)]
        for (c0, n) in chunks:
            pst, psr = cx.ps[nchunk % 4]
            nchunk += 1
            for c in range(8):
                P.mm(pst[:, 0:n], uT[:, c, :], win[:, c, c0:c0 + n], c == 0, c == 7,
                     reads=[uT.r, win.r], writes=[psr], acc=(c > 0))
            if c0 < TC:
                qcols = L["qs"] * 64
                a = c0
                while a < c0 + n:
                    if a < qcols:
                        b = min(c0 + n, qcols)
                        sc = 0.125
                    else:
                        b = c0 + n
                        sc = 1.0
                    P.act(tq[:, a:b], pst[:, a - c0:b - c0], AF.Copy, reads=[psr], writes=[tq.r], scale=sc)
                    a = b
            else:
                v0 = c0 - TC
                if L["name"] == "dsa" and v0 + n > 1024:
                    nn = 1024 - v0
                    if nn > 0:
                        P.copy("dve", tv_s[:, v0:v0 + nn], pst[:, 0:nn], reads=[psr], writes=[tv_s.r])
                    P.copy("dve", wis[:, :], pst[:, nn:nn + 8], reads=[psr], writes=[wis.r])
                    P.copy("dve", tv_s[:, 1024:1032], wis[:, :], reads=[wis.r], writes=[tv_s.r])
                    P.dma("pool", wi_ap[m * 128:(m + 1) * 128, :], wis[:, :], reads=[wis.r], writes=[wi_r])
                else:
                    P.copy("dve", tv_s[:, v0:v0 + n], pst[:, 0:n], reads=[psr], writes=[tv_s.r])
        if L["rope"]:
            H8 = nT * 8
            x1 = tq.ap(0, [[64, nT], [1, 8]])
            x2 = tq.ap(8, [[64, nT], [1, 8]])
            cosb = cs.ap(m * 8, [[0, nT], [1, 8]])
            sinb = cs.ap(NT * 8 + m * 8, [[0, nT], [1, 8]])
            ta = rt.ap(0, [[8, nT], [1, 8]])
            tb = rt.ap(H8, [[8, nT], [1, 8]])
            tc = rt.ap(2 * H8, [[8, nT], [1, 8]])
            td = rt.ap(3 * H8, [[8, nT], [1, 8]])
            P.tt("dve", ta, x1, cosb, ALU.mult, reads=[tq.r, cs.r], writes=[rt.r])
            P.tt("pool", tb, x2, sinb, ALU.mult, reads=[tq.r, cs.r], writes=[rt.r])
            P.tt("dve", tc, x2, cosb, ALU.mult, reads=[tq.r, cs.r], writes=[rt.r])
            P.tt("pool", td, x1, sinb, ALU.mult, reads=[tq.r, cs.r], writes=[rt.r])
            P.tt("dve", x1, ta, tb, ALU.subtract, reads=[rt.r], writes=[tq.r])
            P.tt("dve", x2, tc, td, ALU.add, reads=[rt.r], writes=[tq.r])
        P.copy("pool", tqb[:], tq[:], reads=[tq.r], writes=[tqb.r])
        tT = tTs[m % 2]
        transpose_rows(cx, tT, tqb, nT // 2, m, [tqb.r])
        nqc = L["nq"] // 2
        nkc = L["nk"] // 2
        P.dma("pool", qT_ap[:, m * 128:(m + 1) * 128].rearrange("(c p) t -> p c t", p=128), tT[:, 0:nqc, :],
              reads=[tT.r], writes=[qT_r])
        P.dma("pool", kT_ap[:, m * 128:(m + 1) * 128].rearrange("(c p) t -> p c t", p=128), tT[:, nqc:nqc + nkc, :],
              reads=[tT.r], writes=[kT_r])
        P.dma("pool", tv_ap[m * 128:(m + 1) * 128, :], tv_s[:], reads=[tv_s.r], writes=[tv_r])
    P.barrier()
    P.release(mark)


def phase_post(cx, li):
    P = cx.P
    L = LAYERS[li]
    mark = P.mark()
    NOC = L["no"] // 128
    stage = [Buf(P, [128, 1024], F32, "stage") for _ in range(2)]
    wo = Buf(P, [128, NOC, D], BF16, "wo")
    w1 = Buf(P, [128, 8, 4096], BF16, "w1")
    w2 = Buf(P, [128, 32, D], BF16, "w2")
    load_weight(cx, wo, cx.dram["w_out"], L["no"], D, stage)
    load_weight(cx, w1, cx.dram["w_ff_in"], D, 4096, stage)
    load_weight(cx, w2, cx.dram["w_ff_out"], 4096, D, stage)
    gpost = load_gain(cx, "g_mix_post", 0)
    gfpre = load_gain(cx, "g_ffn_pre", 0)
    gfpost = load_gain(cx, "g_ffn_post", 0)
    hb = [Buf(P, [128, D], F32, "hs") for _ in range(2)]
    ob = [Buf(P, [128, NOC, 128], BF16, "oT") for _ in range(2)]
    sm = Buf(P, [128, 16], F32, "sm")
    tmp = Buf(P, [128, D], F32, "tmp")
    ub = Buf(P, [128, D], BF16, "u")
    junk = ub
    uT = Buf(P, [128, 8, 128], BF16, "uT")
    rl = [Buf(P, [128, 512], F32, "relu") for _ in range(2)]
    aT = Buf(P, [128, 32, 128], BF16, "aT")
    h_ap, h_r = cx.dram["h_res"]
    ho_ap, ho_r = cx.dram["h_mid"]
    oT_ap, oT_r = cx.dram["oT"]
    for m in range(NT):
        hs = hb[m % 2]
        ot = ob[m % 2]
        P.dma("sp", hs[:], h_ap[m * 128:(m + 1) * 128, :], reads=[h_r], writes=[hs.r])
        P.dma("sp", ot[:], oT_ap[:, m * 128:(m + 1) * 128].rearrange("(c p) t -> p c t", p=128),
              reads=[oT_r], writes=[ot.r])
        y = (cx.ps[0], cx.ps[1])
        for n in range(2):
            yt, yr = y[n]
            for c in range(NOC):
                P.mm(yt[:, :], ot[:, c, :], wo[:, c, n * 512:(n + 1) * 512], c == 0, c == NOC - 1,
                     reads=[ot.r, wo.r], writes=[yr], acc=(c > 0))
        P.act(junk[:, 0:512], y[0][0][:, :], AF.Square, reads=[y[0][1]], writes=[junk.r, sm.r], accum_out=sm[:, 0:1])
        P.act(junk[:, 512:1024], y[1][0][:, :], AF.Square, reads=[y[1][1]], writes=[junk.r, sm.r], accum_out=sm[:, 1:2])
        P.tt("dve", sm[:, 2:3], sm[:, 0:1], sm[:, 1:2], ALU.add, reads=[sm.r], writes=[sm.r])
        P.act(sm[:, 3:4], sm[:, 2:3], AF.Sqrt, reads=[sm.r], writes=[sm.r], scale=1.0 / D, bias=EPS)
        P.op("dve", lambda e: e.reciprocal(out=sm[:, 4:5], in_=sm[:, 3:4]), reads=[sm.r], writes=[sm.r])
        for n in range(2):
            P.stt("dve", tmp[:, n * 512:(n + 1) * 512], y[n][0][:, :], sm[:, 4:5], gpost[:, n * 512:(n + 1) * 512],
                  ALU.mult, ALU.mult, reads=[y[n][1], sm.r, gpost.r], writes=[tmp.r])
        P.tt("pool", hs[:], hs[:], tmp[:], ALU.add, reads=[hs.r, tmp.r], writes=[hs.r])
        rstd = rms_rstd(cx, hs[:], [hs.r], junk, sm, 5)
        P.stt("dve", ub[:], hs[:], rstd, gfpre[:], ALU.mult, ALU.mult, reads=[hs.r, sm.r, gfpre.r], writes=[ub.r])
        transpose_rows(cx, uT, ub, 8, m, [ub.r])
        for f4 in range(8):
            pt_, pr_ = cx.ps[2 + f4 % 2]
            for f in range(4):
                ff = f4 * 4 + f
                for c in range(8):
                    P.mm(pt_[:, f * 128:(f + 1) * 128], w1[:, c, ff * 128:(ff + 1) * 128], uT[:, c, :], c == 0, c == 7,
                         reads=[w1.r, uT.r], writes=[pr_], acc=(c > 0 or f > 0))
            r_ = rl[f4 % 2]
            P.act(r_[:], pt_[:, :], AF.Relu, reads=[pr_], writes=[r_.r])
            P.tt("pool", aT.ap(f4 * 512, [[1, 512]]), r_[:], r_[:], ALU.mult, reads=[r_.r], writes=[aT.r])
        fo = (cx.ps[4], cx.ps[5])
        for n in range(2):
            ft, fr = fo[n]
            for ff in range(32):
                P.mm(ft[:, :], aT[:, ff, :], w2[:, ff, n * 512:(n + 1) * 512], ff == 0, ff == 31,
                     reads=[aT.r, w2.r], writes=[fr], acc=(ff > 0))
        P.act(junk[:, 0:512], fo[0][0][:, :], AF.Square, reads=[fo[0][1]], writes=[junk.r, sm.r], accum_out=sm[:, 8:9])
        P.act(junk[:, 512:1024], fo[1][0][:, :], AF.Square, reads=[fo[1][1]], writes=[junk.r, sm.r], accum_out=sm[:, 9:10])
        P.tt("dve", sm[:, 10:11], sm[:, 8:9], sm[:, 9:10], ALU.add, reads=[sm.r], writes=[sm.r])
        P.act(sm[:, 11:12], sm[:, 10:11], AF.Sqrt, reads=[sm.r], writes=[sm.r], scale=1.0 / D, bias=EPS)
        P.op("dve", lambda e: e.reciprocal(out=sm[:, 12:13], in_=sm[:, 11:12]), reads=[sm.r], writes=[sm.r])
        for n in range(2):
            P.stt("dve", tmp[:, n * 512:(n + 1) * 512], fo[n][0][:, :], sm[:, 12:13], gfpost[:, n * 512:(n + 1) * 512],
                  ALU.mult, ALU.mult, reads=[fo[n][1], sm.r, gfpost.r], writes=[tmp.r])
        P.tt("pool", hs[:], hs[:], tmp[:], ALU.add, reads=[hs.r, tmp.r], writes=[hs.r])
        P.dma("pool", ho_ap[m * 128:(m + 1) * 128, :], hs[:], reads=[hs.r], writes=[ho_r])
    P.barrier()
    P.release(mark)


def run_streams(factories, nslots=2):
    pending = list(factories)
    active = []
    free = list(range(nslots))
    while pending or active:
        while pending and free:
            s = free.pop(0)
            active.append((pending.pop(0)(s), s))
        for item in list(active):
            g, s = item
            try:
                next(g)
            except StopIteration:
                active.remove(item)
                free.append(s)


class KV:
    def __init__(self, cx):
        P = cx.P
        self.Ks = Buf(P, [128, 2, 4, 2048], BF16, "Ks")
        self.Vs = Buf(P, [128, 4, 16, 4, 80], BF16, "Vs")
        self.Qz = [Buf(P, [128, 2, 2048], BF16, f"Qz{i}") for i in range(2)]
        P.memset("pool", self.Vs[:], 1.0, writes=[self.Vs.r])
        for q in self.Qz:
            P.memset("pool", q[:], 0.0, writes=[q.r])

    def load(self, cx, qrow0, krow0, vcol0):
        P = cx.P
        q_ap, q_r = cx.dram["qT"]
        k_ap, k_r = cx.dram["kTg"]
        v_ap, v_r = cx.dram["vg"]
        qsrc = q_ap[qrow0:qrow0 + 256, :].rearrange("(c p) t -> p c t", p=128)
        for par in range(2):
            P.dma("sp", self.Qz[par][par * 64:(par + 1) * 64, :, :], qsrc[par * 64:(par + 1) * 64, :, :],
                  reads=[q_r], writes=[self.Qz[par].r])
        for ch in range(2):
            P.dma("sp", self.Ks[:, ch, :, :],
                  k_ap[:, krow0 + ch * 128:krow0 + (ch + 1) * 128, :].rearrange("r p t -> p r t"),
                  reads=[k_r], writes=[self.Ks.r])
        if vcol0 is not None:
            for rk in range(4):
                for h in range(4):
                    P.dma("sp", self.Vs[:, rk, :, h, 0:64],
                          v_ap[rk, :, vcol0 + h * 64:vcol0 + (h + 1) * 64].rearrange("(m p) d -> p m d", p=128),
                          reads=[v_r], writes=[self.Vs.r])

    def qk(self, cx, A, Ar, rank, ml, m, stop=True):
        P = cx.P
        for h in range(4):
            qz = self.Qz[h % 2]
            P.mm(A[:, h * 128:(h + 1) * 128], self.Ks[:, h // 2, rank, ml * 128:(ml + 1) * 128],
                 qz[:, h // 2, m * 128:(m + 1) * 128], h == 0, stop and h == 3,
                 reads=[self.Ks.r, qz.r], writes=[Ar], acc=(h > 0))

    def av(self, cx, O, Or, pt, rank, ml, first, last, nrow):
        P = cx.P
        for h in range(4):
            P.mm(O[0:nrow, h * 128:(h + 1) * 128], self.Vs[:, rank, ml, h, 0:nrow], pt[:, h * 128:(h + 1) * 128],
                 first and h == 0, last and h == 3, reads=[self.Vs.r, pt.r], writes=[Or], acc=(not first or h > 0))


def causal_masks(cx, strict):
    P = cx.P
    M = Buf(P, [128, 4, 128], BF16, "cmask")
    for r in range(4):
        thr = 128 * r + (0.5 if strict else -0.5)
        P.ts("dve", M[:, r, :], cx.d0[:], cx.cinfo[:, 0:1], thr, ALU.add, ALU.is_gt,
             reads=[cx.d0.r, cx.cinfo.r], writes=[M.r])
    return M


def store_oT(cx, oTs, hrow0, m):
    P = cx.P
    o_ap, o_r = cx.dram["oT"]
    P.dma("pool", o_ap[hrow0:hrow0 + 256, m * 128:(m + 1) * 128].rearrange("(h d) t -> d h t", d=64),
          oTs[0:64, :, :], reads=[oTs.r], writes=[o_r])


def phase_att_sb(cx):
    P = cx.P
    mark = P.mark()
    kv = KV(cx)
    M = causal_masks(cx, True)
    NIU = Buf(P, [128, 128], BF16, "niu")
    P.ts("dve", NIU[:], cx.d0t[:], -0.5, -1.0, ALU.is_gt, ALU.mult, reads=[cx.d0t.r], writes=[NIU.r])
    slots = []
    for s in range(2):
        slots.append(dict(
            E=Buf(P, [128, 512], F32, "E"), SP=Buf(P, [128, 512], BF16, "SP"), ARG=Buf(P, [128, 512], F32, "ARG"),
            C=Buf(P, [128, 512], F32, "C"), PT=Buf(P, [128, 512], BF16, "PT"), OT=Buf(P, [64, 4, 128], BF16, "OT"),
            A=cx.ps[s], A2=cx.ps[2 + s], B=cx.ps[6], O=cx.ps[4 + s]))

    def stream(hg, m):
        def gen(s):
            sl = slots[s]
            (A, Ar), (A2, A2r), (B, Br), (O, Or) = sl["A"], sl["A2"], sl["B"], sl["O"]
            E, SP, ARG, C, PT, OT = sl["E"], sl["SP"], sl["ARG"], sl["C"], sl["PT"], sl["OT"]
            kbs = list(range(4 * m + 3, -1, -1))
            lvl = DBG.get("lvl", 99)
            for i, kb in enumerate(kbs):
                rank, ml, r = kb % 4, kb // 4, kb - 4 * m
                last = i == len(kbs) - 1
                if lvl < 1:
                    continue
                kv.qk(cx, A, Ar, rank, ml, m, stop=True)
                yield
                if lvl < 2:
                    continue
                P.act(E[:], A[:, :], AF.Exp, reads=[Ar], writes=[E.r])
                yield
                if lvl < 3:
                    continue
                P.act(SP[:], E[:], AF.Ln, reads=[E.r], writes=[SP.r], bias=1.0)
                if r >= 0:
                    P.tt(DBG.get("maskeng", "pool"), SP.ap(0, [[128, 4], [1, 128]]), SP.ap(0, [[128, 4], [1, 128]]),
                         M.ap(r * 128, [[0, 4], [1, 128]]), ALU.mult, reads=[SP.r, M.r], writes=[SP.r])
                yield
                if lvl < 4:
                    continue
                kv.qk(cx, A2, A2r, rank, ml, m, stop=False)
                P.mm(A2[:, :], NIU[:], SP[:], False, True, reads=[NIU.r, SP.r], writes=[A2r], acc=True)
                if not last:
                    P.mm(B[:, :], cx.ones16[:], SP[:], True, True, reads=[cx.ones16.r, SP.r], writes=[Br])
                if lvl < 5:
                    continue
                if i == 0:
                    P.copy("dve", ARG[:], A2[:, :], reads=[A2r], writes=[ARG.r])
                    if not last:
                        P.copy("dve", C[:], B[:, :], reads=[Br], writes=[C.r])
                else:
                    P.tt("dve", ARG[:], A2[:, :], C[:], ALU.subtract, reads=[A2r, C.r], writes=[ARG.r])
                    if not last:
                        P.tt("dve", C[:], C[:], B[:, :], ALU.add, reads=[C.r, Br], writes=[C.r])
                yield
                if lvl < 6:
                    continue
                P.act(PT[:], ARG[:], AF.Exp, reads=[ARG.r], writes=[PT.r])
                if r >= 0:
                    P.tt(DBG.get("maskeng", "pool"), PT.ap(0, [[128, 4], [1, 128]]), PT.ap(0, [[128, 4], [1, 128]]),
                         M.ap(r * 128, [[0, 4], [1, 128]]), ALU.mult, reads=[PT.r, M.r], writes=[PT.r])
                yield
                if lvl < 7:
                    continue
                kv.av(cx, O, Or, PT, rank, ml, i == 0, last, 64)
                yield
            if lvl >= 8:
                P.copy("act", OT.ap(0, [[1, 512]], 0, 64), O[0:64, :], reads=[Or], writes=[OT.r])
                store_oT(cx, OT, hg * 256, m)
            yield
        return gen

    for hg in range(DBG.get("hg", 4)):
        kv.load(cx, hg * 256, hg * 256, hg * 256)
        run_streams([stream(hg, m) for m in range(DBG.get("m", NT))])
    P.barrier()
    P.release(mark)


def softmax_finish(cx, src, src_off, RD, OT, hrow0, m):
    P = cx.P
    Bc, Bcr = cx.ps[4]
    P.op("dve", lambda e: e.reciprocal(out=RD.ap(0, [[1, 512]], 64, 1), in_=src.ap(src_off, [[1, 512]], 64, 1)),
         reads=[src.r], writes=[RD.r])
    P.mm(Bc[0:64, :], cx.ones32[64:65, 0:64], RD.ap(0, [[1, 512]], 64, 1), True, True,
         reads=[cx.ones32.r, RD.r], writes=[Bcr])
    P.tt("dve", OT.ap(0, [[1, 512]], 0, 64), src.ap(src_off, [[1, 512]], 0, 64), Bc[0:64, :], ALU.mult,
         reads=[src.r, Bcr], writes=[OT.r])
    store_oT(cx, OT, hrow0, m)


def phase_att_dil(cx):
    P = cx.P
    mark = P.mark()
    kv = KV(cx)
    RLO = (-1, -4, -16)
    idx = {}
    for g in range(3):
        for r in range(RLO[g], 4):
            idx[(g, r)] = len(idx)
    MD = Buf(P, [128, len(idx), 128], BF16, "MD")
    modm = [None, Buf(P, [128, 128], F32, "modm1"), Buf(P, [128, 128], F32, "modm2")]
    t1 = Buf(P, [128, 128], F32, "t1")
    t2 = Buf(P, [128, 128], F32, "t2")
    ti = Buf(P, [128, 128], I32, "ti")
    for g in (1, 2):
        dil = DIL_CFG[g][1]
        P.ts("dve", t1[:], cx.d0[:], 128.0, 1.0 / dil, ALU.add, ALU.mult, reads=[cx.d0.r], writes=[t1.r])
        P.copy("dve", ti[:], t1[:], reads=[t1.r], writes=[ti.r])
        P.copy("dve", t1[:], ti[:], reads=[ti.r], writes=[t1.r])
        P.ts("dve", t1[:], t1[:], float(dil), None, ALU.mult, reads=[t1.r], writes=[t1.r])
        P.ts("dve", t2[:], cx.d0[:], 128.0, None, ALU.add, reads=[cx.d0.r], writes=[t2.r])
        P.tt("dve", modm[g][:], t1[:], t2[:], ALU.is_equal, reads=[t1.r, t2.r], writes=[modm[g].r])
    for (g, r), ix in idx.items():
        W = DIL_CFG[g][0]
        P.ts("dve", t1[:], cx.d0[:], cx.cinfo[:, 0:1], 128.0 * r - 0.5, ALU.add, ALU.is_gt,
             reads=[cx.d0.r, cx.cinfo.r], writes=[t1.r])
        P.ts("dve", t2[:], cx.d0[:], cx.cinfo[:, 0:1], 128.0 * r + W + 0.5, ALU.add, ALU.is_lt,
             reads=[cx.d0.r, cx.cinfo.r], writes=[t2.r])
        if g == 0:
            P.tt("dve", MD[:, ix, :], t1[:], t2[:], ALU.mult, reads=[t1.r, t2.r], writes=[MD.r])
        else:
            P.tt("dve", t1[:], t1[:], t2[:], ALU.mult, reads=[t1.r, t2.r], writes=[t1.r])
            P.tt("dve", MD[:, ix, :], t1[:], modm[g][:], ALU.mult, reads=[t1.r, modm[g].r], writes=[MD.r])
    ACC = Buf(P, [65, NT, 512], F32, "ACC")
    RD = Buf(P, [65, 512], F32, "RD")
    slots = [dict(PT=Buf(P, [128, 512], BF16, "PT"), OT=Buf(P, [64, 4, 128], BF16, "OT"),
                  A=cx.ps[s], O=cx.ps[2 + s]) for s in range(2)]

    def stream(hg, g, m):
        def gen(s):
            sl = slots[s]
            (A, Ar), (O, Or) = sl["A"], sl["O"]
            PT, OT = sl["PT"], sl["OT"]
            kbs = [4 * m + r for r in range(RLO[g], 4) if 4 * m + r >= 0]
            for i, kb in enumerate(kbs):
                rank, ml, r = kb % 4, kb // 4, kb - 4 * m
                kv.qk(cx, A, Ar, rank, ml, m)
                yield
                P.act(PT[:], A[:, :], AF.Exp, reads=[Ar], writes=[PT.r])
                P.tt("dve", PT.ap(0, [[128, 4], [1, 128]]), PT.ap(0, [[128, 4], [1, 128]]),
                     MD.ap(idx[(g, r)] * 128, [[0, 4], [1, 128]]), ALU.mult, reads=[PT.r, MD.r], writes=[PT.r])
                yield
                kv.av(cx, O, Or, PT, rank, ml, i == 0, i == len(kbs) - 1, 65)
                yield
            if g == 0:
                P.copy("act", ACC.ap(m * 512, [[1, 512]]), O[0:65, :], reads=[Or], writes=[ACC.r])
            else:
                P.tt("dve", ACC.ap(m * 512, [[1, 512]]), ACC.ap(m * 512, [[1, 512]]), O[0:65, :], ALU.add,
                     reads=[ACC.r, Or], writes=[ACC.r])
            if g == 2:
                softmax_finish(cx, ACC, m * 512, RD, OT, hg * 256, m)
            yield
        return gen

    for hg in range(DBG.get("hg", 2)):
        for g in range(3):
            row0 = (g * 8 + hg * 4) * 64
            kv.load(cx, row0, row0, row0)
            run_streams([stream(hg, g, m) for m in range(DBG.get("m", NT))])
    P.barrier()
    P.release(mark)


def phase_att_dsa(cx):
    P = cx.P
    ms_ap, ms_r = cx.dram["mscr"]
    NM = DBG.get("m", NT)
    mark = P.mark()
    q_ap, q_r = cx.dram["qT"]
    k_ap, k_r = cx.dram["kTg"]
    w_ap, w_r = cx.dram["wi"]
    Qiz = [Buf(P, [128, 4, 2048], BF16, f"Qiz{i}") for i in range(2)]
    Ki2 = Buf(P, [128, 4, 2048], BF16, "Ki2")
    WI = Buf(P, [128, NT, 8], F32, "WI")
    SC = Buf(P, [128, 4, 2048], F32, "SC")
    MK = Buf(P, [128, 4, 2048], BF16, "MK")
    Rb = [Buf(P, [128, 512], F32, "Rb") for _ in range(2)]
    MTs = [Buf(P, [128, 8, 128], BF16, "MTs") for _ in range(2)]
    NEGM = Buf(P, [128, 4, 128], F32, "NEGM")
    sm = Buf(P, [128, 16], F32, "bsm")
    qsrc = q_ap[1024:1536, :].rearrange("(c p) t -> p c t", p=128)
    for par in range(2):
        P.memset("pool", Qiz[par][:], 0.0, writes=[Qiz[par].r])
        P.dma("sp", Qiz[par][par * 64:(par + 1) * 64, :, :], qsrc[par * 64:(par + 1) * 64, :, :],
              reads=[q_r], writes=[Qiz[par].r])
        P.dma("sp", Ki2[par * 64:(par + 1) * 64, :, :], k_ap[:, 1024:1088, :].rearrange("r p t -> p r t"),
              reads=[k_r], writes=[Ki2.r])
    P.dma("sp", WI[:], w_ap[:, :].rearrange("(m p) h -> p m h", p=128), reads=[w_r], writes=[WI.r])
    for r in range(4):
        P.ts("dve", NEGM[:, r, :], cx.d0t[:], cx.cinfo[:, 0:1], 128.0 * r - 0.5, ALU.add, ALU.is_lt,
             reads=[cx.d0t.r, cx.cinfo.r], writes=[NEGM.r])
        P.ts("dve", NEGM[:, r, :], NEGM[:, r, :], NEG, None, ALU.mult, reads=[NEGM.r], writes=[NEGM.r])
    nmm = 0
    ntr = 0
    for m in range(NM):
        L = (m + 1) * 128
        for rank in range(4):
            for c0 in range(0, L, 512):
                n = min(512, L - c0)
                for ih in range(8):
                    St, Sr = cx.ps[nmm % 4]
                    rb = Rb[nmm % 2]
                    nmm += 1
                    P.mm(St[:, 0:n], Qiz[ih % 2][:, ih // 2, m * 128:(m + 1) * 128], Ki2[:, rank, c0:c0 + n], True, True,
                         reads=[Qiz[ih % 2].r, Ki2.r], writes=[Sr])
                    P.act(rb[:, 0:n], St[:, 0:n], AF.Relu, reads=[Sr], writes=[rb.r])
                    if ih == 0:
                        P.ts("dve", SC[:, rank, c0:c0 + n], rb[:, 0:n], WI[:, m, 0:1], None, ALU.mult,
                             reads=[rb.r, WI.r], writes=[SC.r])
                    else:
                        P.stt("dve", SC[:, rank, c0:c0 + n], rb[:, 0:n], WI[:, m, ih:ih + 1], SC[:, rank, c0:c0 + n],
                              ALU.mult, ALU.add, reads=[rb.r, WI.r, SC.r], writes=[SC.r])
        scv = SC.ap(0, [[2048, 4], [1, L]])
        P.op("dve", lambda e, scv=scv: e.tensor_reduce(out=sm[:, 0:1], in_=scv, axis=AX.XY, op=ALU.max),
             reads=[SC.r], writes=[sm.r])
        P.op("dve", lambda e, scv=scv: e.tensor_reduce(out=sm[:, 1:2], in_=scv, axis=AX.XY, op=ALU.min),
             reads=[SC.r], writes=[sm.r])
        P.ts("dve", sm[:, 1:2], sm[:, 1:2], -1.0, None, ALU.mult, reads=[sm.r], writes=[sm.r])
        P.tt("dve", sm[:, 2:3], sm[:, 0:1], sm[:, 1:2], ALU.max, reads=[sm.r], writes=[sm.r])
        P.ts("dve", sm[:, 3:4], sm[:, 2:3], 1.0, None, ALU.add, reads=[sm.r], writes=[sm.r])
        P.ts("dve", sm[:, 4:5], sm[:, 3:4], -1.0, None, ALU.mult, reads=[sm.r], writes=[sm.r])
        for rank in range(4):
            P.tt("dve", SC[:, rank, m * 128:(m + 1) * 128], SC[:, rank, m * 128:(m + 1) * 128], NEGM[:, rank, :], ALU.add,
                 reads=[SC.r, NEGM.r], writes=[SC.r])
        mkv = MK.ap(0, [[2048, 4], [1, L]])
        hi, lo, mid, cnt, ge, d1 = (sm[:, 3:4], sm[:, 4:5], sm[:, 5:6], sm[:, 6:7], sm[:, 7:8], sm[:, 8:9])
        for it in range(DBG.get("bis", 20)):
            P.ts("dve", mid, lo, hi, 0.5, ALU.add, ALU.mult, reads=[sm.r], writes=[sm.r])
            P.ts("dve", mkv, scv, mid, 0.0, ALU.is_ge, ALU.add, reads=[SC.r, sm.r], writes=[MK.r, sm.r], accum_out=cnt)
            P.ts("dve", ge, cnt, 255.5, None, ALU.is_gt, reads=[sm.r], writes=[sm.r])
            P.tt("dve", d1, mid, lo, ALU.subtract, reads=[sm.r], writes=[sm.r])
            P.stt("dve", lo, d1, ge, lo, ALU.mult, ALU.add, reads=[sm.r], writes=[sm.r])
            P.tt("dve", d1, hi, mid, ALU.subtract, reads=[sm.r], writes=[sm.r])
            P.stt("dve", hi, d1, ge, mid, ALU.mult, ALU.add, reads=[sm.r], writes=[sm.r])
        P.ts("dve", mkv, scv, lo, None, ALU.is_ge, reads=[SC.r, sm.r], writes=[MK.r])
        for rank in range(4):
            for b0 in range(0, m + 1, 8):
                nb = min(8, m + 1 - b0)
                pbt, pbr = cx.pb[0]
                mts = MTs[ntr % 2]
                ntr += 1
                for j in range(nb):
                    P.tr(pbt[:, j * 128:(j + 1) * 128], MK[:, rank, (b0 + j) * 128:(b0 + j + 1) * 128], cx.ident[:],
                         reads=[MK.r, cx.ident.r], writes=[pbr], acc=(j > 0))
                P.copy("act", mts.ap(0, [[1, nb * 128]]), pbt[:, 0:nb * 128], reads=[pbr], writes=[mts.r])
                P.dma("pool", ms_ap[m, :, rank * 16 + b0:rank * 16 + b0 + nb, :], mts[:, 0:nb, :],
                      reads=[mts.r], writes=[ms_r])
    P.barrier()
    P.release(mark)
    mark = P.mark()
    kv = KV(cx)
    RD = Buf(P, [65, 512], F32, "RD")
    slots = [dict(PT=Buf(P, [128, 512], BF16, "PT"), OT=Buf(P, [64, 4, 128], BF16, "OT"),
                  ON=Buf(P, [65, 512], F32, "ON"), MT=Buf(P, [128, 4, 2048], BF16, "MT"),
                  A=cx.ps[s], O=cx.ps[2 + s]) for s in range(2)]

    def stream(hg, m):
        def gen(s):
            sl = slots[s]
            (A, Ar), (O, Or) = sl["A"], sl["O"]
            PT, OT, ON, MT = sl["PT"], sl["OT"], sl["ON"], sl["MT"]
            L = (m + 1) * 128
            P.dma("sp", MT[:, :, 0:L], ms_ap[m, :, :, :].rearrange("s (r b) t -> s r (b t)", r=4)[:, :, 0:L],
                  reads=[ms_r], writes=[MT.r])
            steps = [(rank, ml) for ml in range(m + 1) for rank in range(4)]
            for i, (rank, ml) in enumerate(steps):
                kv.qk(cx, A, Ar, rank, ml, m)
                yield
                P.act(PT[:], A[:, :], AF.Exp, reads=[Ar], writes=[PT.r])
                P.tt("dve" if i % 2 == 0 else "pool", PT.ap(0, [[128, 4], [1, 128]]), PT.ap(0, [[128, 4], [1, 128]]),
                     MT.ap(rank * 2048 + ml * 128, [[0, 4], [1, 128]]), ALU.mult, reads=[PT.r, MT.r], writes=[PT.r])
                yield
                kv.av(cx, O, Or, PT, rank, ml, i == 0, i == len(steps) - 1, 65)
                yield
            P.copy("act", ON[:], O[0:65, :], reads=[Or], writes=[ON.r])
            softmax_finish(cx, ON, 0, RD, OT, hg * 256, m)
            yield
        return gen

    for hg in range(DBG.get("hg", 4)):
        kv.load(cx, hg * 256, hg * 256, hg * 256)
        run_streams([stream(hg, m) for m in range(NM)])
    P.barrier()
    P.release(mark)


def phase_att_moba(cx):
    P = cx.P
    mark = P.mark()
    NM = DBG.get("m", NT)
    kv = KV(cx)
    Mle = causal_masks(cx, False)
    o_ap, o_r = cx.dram["oT"]
    kmT = Buf(P, [128, 2, 32], BF16, "kmT")
    KS = Buf(P, [128, 4, 16], F32, "KS")
    KM = Buf(P, [128, 32], F32, "KM")
    iotaI = Buf(P, [128, 32], I32, "iotaI")
    iotaN = Buf(P, [128, 32], F32, "iotaN")
    P.op("pool", lambda e: e.iota(iotaI[:], pattern=[[1, 32]], base=0, channel_multiplier=0), writes=[iotaI.r])
    P.copy("dve", iotaN[:], iotaI[:], reads=[iotaI.r], writes=[iotaN.r])
    chalf = cx.cinfo[:, 1:2]
    slots = []
    for s in range(2):
        slots.append(dict(
            PT=[Buf(P, [128, 512], BF16, "PT") for _ in range(2)], VAL=Buf(P, [128, 32], F32, "VAL"),
            EQ=Buf(P, [128, 32], F32, "EQ"), GN=Buf(P, [128, 32], F32, "GN"), GS=Buf(P, [128, 4, 32], F32, "GS"),
            M8=Buf(P, [128, 32], F32, "M8"), SEL=Buf(P, [128, 4, 32], F32, "SEL"), TMP=Buf(P, [128, 4, 65], F32, "TMP"),
            ACC=Buf(P, [128, 4, 65], F32, "ACC"), RC=Buf(P, [128, 4], F32, "RC"), OK=Buf(P, [128, 256], BF16, "OK"),
            OTt=Buf(P, [128, 2, 128], BF16, "OTt"), A=cx.ps[s], ON=cx.ps[2 + s], G=cx.ps[5 + s]))

    def stream(hg, m):
        def gen(s):
            sl = slots[s]
            (A, Ar), (ON, ONr), (G, Gr) = sl["A"], sl["ON"], sl["G"]
            PTs, VAL, EQ, GN, GS, M8, SEL, TMP, ACC, RC, OK, OTt = (sl[k] for k in (
                "PT", "VAL", "EQ", "GN", "GS", "M8", "SEL", "TMP", "ACC", "RC", "OK", "OTt"))
            for h in range(4):
                qz = kv.Qz[h % 2]
                P.mm(G[:, h * 32:(h + 1) * 32], qz[:, h // 2, m * 128:(m + 1) * 128], kmT[:, h // 2, :], h == 0, h == 3,
                     reads=[qz.r, kmT.r], writes=[Gr], acc=(h > 0))
            P.ts("dve", VAL[:], iotaN[:], -2.0 * m, chalf, ALU.add, ALU.is_lt, reads=[iotaN.r, cx.cinfo.r], writes=[VAL.r])
            P.ts("dve", EQ[:], iotaN[:], -2.0 * m, chalf, ALU.add, ALU.is_equal, reads=[iotaN.r, cx.cinfo.r], writes=[EQ.r])
            P.ts("dve", GN[:], VAL[:], -NEG, NEG, ALU.mult, ALU.add, reads=[VAL.r], writes=[GN.r])
            P.tt("dve", GS.ap(0, [[32, 4], [1, 32]]), G[:, 0:128].rearrange("p (h n) -> p h n", h=4),
                 GN.ap(0, [[0, 4], [1, 32]]), ALU.add, reads=[Gr, GN.r], writes=[GS.r])
            for h in range(4):
                P.op("dve", lambda e, h=h: e.max(out=M8[:, h * 8:(h + 1) * 8], in_=GS[:, h, :]), reads=[GS.r], writes=[M8.r])
            for h in range(4):
                P.ts("dve", SEL[:, h, :], GS[:, h, :], M8[:, h * 8 + 2:h * 8 + 3], None, ALU.is_ge,
                     reads=[GS.r, M8.r], writes=[SEL.r])
            P.tt("dve", SEL.ap(0, [[32, 4], [1, 32]]), SEL.ap(0, [[32, 4], [1, 32]]), VAL.ap(0, [[0, 4], [1, 32]]), ALU.mult,
                 reads=[SEL.r, VAL.r], writes=[SEL.r])
            P.tt("dve", SEL.ap(0, [[32, 4], [1, 32]]), SEL.ap(0, [[32, 4], [1, 32]]), EQ.ap(0, [[0, 4], [1, 32]]), ALU.add,
                 reads=[SEL.r, EQ.r], writes=[SEL.r])
            yield
            nblk = 2 * m + 2
            for n in range(nblk):
                for kbi in range(2):
                    kb = 2 * n + kbi
                    rank, ml, r = kb % 4, kb // 4, kb - 4 * m
                    PT = PTs[kbi]
                    kv.qk(cx, A, Ar, rank, ml, m)
                    yield
                    P.act(PT[:], A[:, :], AF.Exp, reads=[Ar], writes=[PT.r])
                    if r >= 0:
                        P.tt("dve", PT.ap(0, [[128, 4], [1, 128]]), PT.ap(0, [[128, 4], [1, 128]]),
                             Mle.ap(r * 128, [[0, 4], [1, 128]]), ALU.mult, reads=[PT.r, Mle.r], writes=[PT.r])
                    yield
                    for h in range(4):
                        P.mm(ON[:, h * 65:(h + 1) * 65], PT[:, h * 128:(h + 1) * 128], kv.Vs[:, rank, ml, h, 0:65],
                             kbi == 0 and h == 0, kbi == 1 and h == 3, reads=[PT.r, kv.Vs.r], writes=[ONr],
                             acc=(kbi > 0 or h > 0))
                    yield
                onv = ON[:, 0:260].rearrange("p (h d) -> p h d", h=4)
                selb = SEL.ap(n, [[32, 4], [0, 65]])
                if n == 0:
                    P.tt("dve", ACC.ap(0, [[65, 4], [1, 65]]), onv, selb, ALU.mult, reads=[ONr, SEL.r], writes=[ACC.r])
                else:
                    P.tt("dve", TMP.ap(0, [[65, 4], [1, 65]]), onv, selb, ALU.mult, reads=[ONr, SEL.r], writes=[TMP.r])
                    P.tt("pool", ACC[:], ACC[:], TMP[:], ALU.add, reads=[ACC.r, TMP.r], writes=[ACC.r])
            P.op("dve", lambda e: e.reciprocal(out=RC[:], in_=ACC.ap(64, [[65, 4]])), reads=[ACC.r], writes=[RC.r])
            P.tt("dve", OK.ap(0, [[64, 4], [1, 64]]), ACC.ap(0, [[65, 4], [1, 64]]), RC.ap(0, [[1, 4], [0, 64]]), ALU.mult,
                 reads=[ACC.r, RC.r], writes=[OK.r])
            transpose_rows(cx, OTt, OK, 2, 0, [OK.r])
            P.dma("pool", o_ap[hg * 256:(hg + 1) * 256, m * 128:(m + 1) * 128].rearrange("(c p) t -> p c t", p=128),
                  OTt[:], reads=[OTt.r], writes=[o_r])
            yield
        return gen

    for hg in range(DBG.get("hg", 4)):
        kv.load(cx, hg * 256, hg * 256, hg * 256)
        for ch in range(2):
            P.op("dve", lambda e, ch=ch: e.tensor_reduce(out=KS.ap(0, [[1, 64]]), in_=kv.Ks.ap(ch * 8192, [[128, 64], [1, 128]]),
                                                          axis=AX.X, op=ALU.add), reads=[kv.Ks.r], writes=[KS.r])
            P.tt("dve", KM.ap(0, [[2, 16]]), KS[:, 0, :], KS[:, 1, :], ALU.add, reads=[KS.r], writes=[KM.r])
            P.tt("dve", KM.ap(1, [[2, 16]]), KS[:, 2, :], KS[:, 3, :], ALU.add, reads=[KS.r], writes=[KM.r])
            P.ts("dve", kmT[:, ch, :], KM[:], 1.0 / 256, None, ALU.mult, reads=[KM.r], writes=[kmT.r])
        run_streams([stream(hg, m) for m in range(NM)])
    P.barrier()
    P.release(mark)


ATT_FUNCS = {}
DBG = {}


def build_stage(k):
    nc = bass.Bass("TRN2", target_bir_lowering=False)
    stack = contextlib.ExitStack()
    cx = Ctx(nc, stack)
    cx.din("cinfo", [128, 4], F32)
    li = k if k < 4 else None
    pl = k - 1 if k >= 1 else None
    if pl is not None:
        Lp = LAYERS[pl]
        cx.din("qT", [Lp["nq"] * 64, TOK], BF16)
        cx.din("kTg", [4, Lp["nk"] * 64, TOK], BF16)
        cx.din("vg", [4, TOK, Lp["nv"]], BF16)
        if Lp["name"] == "dsa":
            cx.din("wi", [TOK, 8], F32)
        cx.din("h_res", [TOK, D], F32)
        cx.din("w_out", [Lp["no"], D], F32)
        cx.din("w_ff_in", [D, 4096], F32)
        cx.din("w_ff_out", [4096, D], F32)
        for g in ("g_mix_post", "g_ffn_pre", "g_ffn_post", "g_ple"):
            cx.din(g, [1, D], F32)
        cx.din("w_ple_gate", [D, D], F32)
        cx.din("w_ple", [256, D], F32)
        cx.din("p", [TOK, 256], F32)
        if Lp["name"] == "dsa":
            cx.dint("mscr", [NT, 128, 64, 128], BF16)
        cx.dint("oT", [Lp["no"], TOK], BF16)
        cx.dint("h_mid", [TOK, D], F32)
        cx.dram["h_in"] = cx.dram["h_mid"]
        cx.dout("h_out", [TOK, D], F32)
    else:
        cx.din("h_in", [TOK, D], F32)
    if li is not None:
        L = LAYERS[li]
        cx.din("w_in", [D, (L["nq"] + L["nk"]) * 64 + L["nv"]], F32)
        cx.din("g_mix_pre", [1, D], F32)
        cx.dout("qT_o", [L["nq"] * 64, TOK], BF16)
        cx.dout("kT_o", [L["nk"] * 64, TOK], BF16)
        cx.dout("tv_o", [TOK, L["nv"]], BF16)
        if L["name"] == "dsa":
            cx.dout("wi_o", [TOK, 8], F32)
        if L["rope"]:
            cx.din("pos", [128, NT], I32)
            cx.din("invf", [128, 8], F32)
    cx.consts()
    if pl is not None:
        if not DBG.get("noatt"):
            ATT_FUNCS[LAYERS[pl]["name"]](cx)
        if not DBG.get("nopost"):
            phase_post(cx, pl)
    if not DBG.get("nopre"):
        phase_pre(cx, li, pl)
    cx.P.barrier()
    cx.P.emit()
    stack.close()
    return nc, cx


def _rows(a, c):
    F_ = a.shape[-1]
    return np.ascontiguousarray(a.reshape(16, 4, 128, F_)[:, c].reshape(TOK, F_))


def _w_in_perm(inp, li):
    if li == 0:
        return inp["w_in_sb"][0]
    if li == 3:
        return inp["w_in_moba"][0]
    if li == 1:
        W = inp["w_in_dil"][0].reshape(D, 3, 3, 512)
        return np.ascontiguousarray(np.concatenate(
            [W[:, g, 0] for g in range(3)] + [W[:, g, 1] for g in range(3)] + [W[:, g, 2] for g in range(3)], axis=1))
    W = inp["w_in_dsa"][0]
    q, kk, v, qi, ki, wi = W[:, 0:1024], W[:, 1024:2048], W[:, 2048:3072], W[:, 3072:3584], W[:, 3584:3648], W[:, 3648:3656]
    return np.ascontiguousarray(np.concatenate([q, qi, kk, ki, np.zeros((D, 64), np.float32), v, wi], axis=1))


_W_OUT = ("w_out_sb", "w_out_dil", "w_out_dsa", "w_out_moba")
_PROGS = {}


def _get_prog(k):
    if k not in _PROGS:
        _PROGS[k] = build_stage(k)[0]
    return _PROGS[k]


def run_stage(k, inp, state):
    nc = _get_prog(k)
    li = k if k < 4 else None
    pl = k - 1 if k >= 1 else None
    invf = np.tile((500000.0 ** (-np.arange(0, 16, 2, dtype=np.float32) / 16)).astype(np.float32)[None, :], (128, 1))
    in_maps = []
    for core in range(8):
        b, c = core // 4, core % 4
        st = state[core]
        d = {"cinfo": np.tile(np.array([[128.0 * c, float(c // 2), 0.0, 0.0]], np.float32), (128, 1))}
        if pl is not None:
            grp = [state[b * 4 + cc] for cc in range(4)]
            d["qT"] = st["qT_o"]
            d["kTg"] = np.ascontiguousarray(np.stack([g["kT_o"] for g in grp], axis=0))
            d["vg"] = np.ascontiguousarray(np.stack([g["tv_o"] for g in grp], axis=0))
            if LAYERS[pl]["name"] == "dsa":
                d["wi"] = st["wi_o"]
            d["h_res"] = st["h"]
            d["w_out"] = inp[_W_OUT[pl]][0]
            d["w_ff_in"] = inp["w_ff_in"][pl]
            d["w_ff_out"] = inp["w_ff_out"][pl]
            for g in ("g_mix_post", "g_ffn_pre", "g_ffn_post", "g_ple"):
                d[g] = inp[g][pl][None, :]
            d["w_ple_gate"] = inp["w_ple_gate"][pl]
            d["w_ple"] = inp["w_ple"][pl]
            d["p"] = _rows(inp["p"][pl, b], c)
        else:
            d["h_in"] = st["h"]
        if li is not None:
            d["w_in"] = _w_in_perm(inp, li)
            d["g_mix_pre"] = inp["g_mix_pre"][li][None, :]
            if LAYERS[li]["rope"]:
                pos = _rows(inp["positions"][b][:, None].astype(np.int32), c)[:, 0]
                d["pos"] = np.ascontiguousarray(pos.reshape(NT, 128).T)
                d["invf"] = invf
        in_maps.append({kk: np.ascontiguousarray(v) for kk, v in d.items()})
    res = run_bass_kernel_spmd(nc, in_maps, core_ids=list(range(8)))
    new = []
    for core in range(8):
        r = res.results[core]
        st = dict(state[core])
        if pl is not None:
            st["h"] = np.asarray(r["h_out"])
        for nm in ("qT_o", "kT_o", "tv_o", "wi_o"):
            if nm in r:
                st[nm] = np.asarray(r[nm])
        new.append(st)
    return new


def kernel(**inp):
    inp = {k: np.asarray(v) for k, v in inp.items()}
    state = []
    for core in range(8):
        b, c = core // 4, core % 4
        state.append({"h": _rows(inp["x"][b].astype(np.float32), c)})
    for k in range(5):
        state = run_stage(k, inp, state)
    out = np.empty((2, 8192, D), np.float32)
    for core in range(8):
        b, c = core // 4, core % 4
        out[b].reshape(16, 4, 128, D)[:, c] = state[core]["h"].reshape(16, 128, D)
    return out


ATT_FUNCS["sb"] = phase_att_sb
ATT_FUNCS["dil"] = phase_att_dil
ATT_FUNCS["dsa"] = phase_att_dsa
ATT_FUNCS["moba"] = phase_att_moba


_GAINS = ("g_mix_pre", "g_mix_post", "g_ffn_pre", "g_ffn_post", "g_ple")
_LW = ("w_ff_in", "w_ff_out", "w_ple_gate", "w_ple")


def build_fused():
    nc = bass.Bass("TRN2", target_bir_lowering=False)
    stack = contextlib.ExitStack()
    cx = Ctx(nc, stack)
    P = cx.P
    cx.din("cinfo", [128, 4], F32)
    cx.din("pos", [128, NT], I32)
    cx.din("invf", [128, 8], F32)
    cx.din("x", [TOK, D], F32)
    for i, L in enumerate(LAYERS):
        cx.din(f"w_in_{i}", [D, (L["nq"] + L["nk"]) * 64 + L["nv"]], F32)
        cx.din(f"w_out_{i}", [L["no"], D], F32)
        cx.din(f"w_ff_in_{i}", [D, 4096], F32)
        cx.din(f"w_ff_out_{i}", [4096, D], F32)
        cx.din(f"w_ple_gate_{i}", [D, D], F32)
        cx.din(f"w_ple_{i}", [256, D], F32)
        cx.din(f"p_{i}", [TOK, 256], F32)
        for g in _GAINS:
            cx.din(f"{g}_{i}", [1, D], F32)
        cx.dint(f"qT_{i}", [L["nq"] * 64, TOK], BF16)
        cx.dint(f"kT_{i}", [L["nk"] * 64, TOK], BF16)
        cx.dint(f"tv_{i}", [TOK, L["nv"]], BF16)
        cx.dint(f"kTg_{i}", [4 * L["nk"] * 64, TOK], BF16)
        cx.dint(f"vg_{i}", [4 * TOK, L["nv"]], BF16)
        cx.dint(f"oT_{i}", [L["no"], TOK], BF16)
        cx.dint(f"hmid_{i}", [TOK, D], F32)
        if i < 3:
            cx.dint(f"h_{i}", [TOK, D], F32)
    cx.dint("wi_2", [TOK, 8], F32)
    cx.dint("mscr", [NT, 128, 64, 128], BF16)
    cx.dout("h_out", [TOK, D], F32)
    cx.consts()

    def alias(**kw):
        for k, v in kw.items():
            cx.dram[k] = cx.dram[v]

    def pre_alias(i):
        alias(w_in=f"w_in_{i}", g_mix_pre=f"g_mix_pre_{i}", qT_o=f"qT_{i}", kT_o=f"kT_{i}", tv_o=f"tv_{i}", wi_o="wi_2")

    alias(h_in="x")
    pre_alias(0)
    phase_pre(cx, 0, None)
    groups = [[0, 1, 2, 3], [4, 5, 6, 7]]
    for i, L in enumerate(LAYERS):
        for src, dst in ((f"kT_{i}", f"kTg_{i}"), (f"tv_{i}", f"vg_{i}")):
            (s_ap, s_r), (d_ap, d_r) = cx.dram[src], cx.dram[dst]
            P.op("pool", lambda e, s_ap=s_ap, d_ap=d_ap: e.collective_compute(
                "AllGather", ALU.bypass, replica_groups=groups, ins=[s_ap], outs=[d_ap]),
                reads=[s_r], writes=[d_r], dma=True)
        kg_ap, kg_r = cx.dram[f"kTg_{i}"]
        vg_ap, vg_r = cx.dram[f"vg_{i}"]
        cx.dram["kTg"] = (kg_ap.rearrange("(r f) t -> r f t", r=4), kg_r)
        cx.dram["vg"] = (vg_ap.rearrange("(r t) v -> r t v", r=4), vg_r)
        alias(qT=f"qT_{i}", wi="wi_2", oT=f"oT_{i}")
        ATT_FUNCS[L["name"]](cx)
        alias(h_res=("x" if i == 0 else f"h_{i - 1}"), h_mid=f"hmid_{i}", w_out=f"w_out_{i}", w_ff_in=f"w_ff_in_{i}",
              w_ff_out=f"w_ff_out_{i}", g_mix_post=f"g_mix_post_{i}", g_ffn_pre=f"g_ffn_pre_{i}",
              g_ffn_post=f"g_ffn_post_{i}")
        phase_post(cx, i)
        alias(h_in=f"hmid_{i}", h_out=(f"h_{i}" if i < 3 else "h_out"), p=f"p_{i}", w_ple_gate=f"w_ple_gate_{i}",
              w_ple=f"w_ple_{i}", g_ple=f"g_ple_{i}")
        if i < 3:
            pre_alias(i + 1)
        phase_pre(cx, i + 1 if i < 3 else None, i)
    P.barrier()
    P.emit()
    stack.close()
    return nc, cx


def kernel_fused(**inp):
    inp = {k: np.asarray(v) for k, v in inp.items()}
    if "fused" not in _PROGS:
        _PROGS["fused"] = build_fused()[0]
    nc = _PROGS["fused"]
    invf = np.tile((500000.0 ** (-np.arange(0, 16, 2, dtype=np.float32) / 16)).astype(np.float32)[None, :], (128, 1))
    shared = {"invf": invf}
    for i in range(4):
        shared[f"w_in_{i}"] = _w_in_perm(inp, i)
        shared[f"w_out_{i}"] = inp[_W_OUT[i]][0]
        for w in _LW:
            shared[f"{w}_{i}"] = inp[w][i]
        for g in _GAINS:
            shared[f"{g}_{i}"] = inp[g][i][None, :]
    shared = {k: np.ascontiguousarray(v) for k, v in shared.items()}
    in_maps = []
    for core in range(8):
        b, c = core // 4, core % 4
        d = dict(shared)
        d["cinfo"] = np.tile(np.array([[128.0 * c, float(c // 2), 0.0, 0.0]], np.float32), (128, 1))
        pos = _rows(inp["positions"][b][:, None].astype(np.int32), c)[:, 0]
        d["pos"] = np.ascontiguousarray(pos.reshape(NT, 128).T)
        d["x"] = _rows(inp["x"][b].astype(np.float32), c)
        for i in range(4):
            d[f"p_{i}"] = _rows(inp["p"][i, b], c)
        in_maps.append(d)
    res = run_bass_kernel_spmd(nc, in_maps, core_ids=list(range(8)))
    out = np.empty((2, 8192, D), np.float32)
    for core in range(8):
        b, c = core // 4, core % 4
        out[b].reshape(16, 4, 128, D)[:, c] = np.asarray(res.results[core]["h_out"]).reshape(16, 128, D)
    return out
```

```python
import contextlib
import math
import numpy as np
import ml_dtypes
import concourse.bass as bass
import concourse.mybir as mybir
from concourse.bass_utils import run_bass_kernel_spmd

F32 = mybir.dt.float32
BF16 = mybir.dt.bfloat16
I32 = mybir.dt.int32
AF = mybir.ActivationFunctionType
ALU = mybir.AluOpType
AX = mybir.AxisListType

SEM_LIMIT = 30000
DMA_POOL = 12
NEG = -1.0e30
EPS = 1e-6
D = 1024
TOK = 2048
NT = 16

LAYERS = [
    dict(name="sb", nq=16, nk=16, nv=1024, rope=False, qs=16, no=1024),
    dict(name="dil", nq=24, nk=24, nv=1536, rope=True, qs=24, no=512),
    dict(name="dsa", nq=24, nk=18, nv=1032, rope=True, qs=16, no=1024),
    dict(name="moba", nq=16, nk=16, nv=1024, rope=True, qs=16, no=1024),
]
DIL_CFG = ((128, 1), (512, 4), (2048, 16))


class Res:
    __slots__ = ("name", "lastw", "readers")

    def __init__(self, name):
        self.name = name
        self.lastw = None
        self.readers = []


class Op:
    __slots__ = ("eng", "fn", "deps", "signal", "sem", "tick", "dma", "idx", "slot_prev")

    def __init__(self, eng, fn, dma):
        self.eng = eng
        self.fn = fn
        self.deps = []
        self.signal = False
        self.sem = None
        self.tick = 0
        self.dma = dma
        self.slot_prev = None


class Prog:
    ENGS = ("pe", "act", "dve", "pool", "sp")

    def __init__(self, nc, stack):
        self.nc = nc
        self.stack = stack
        self.ops = []
        self.by_eng = {e: [] for e in self.ENGS}
        self.nres = 0
        self.sb_off = 0
        self.ntens = 0
        self.dma_since = []

    ARENA = 204800

    def init_arena(self):
        a = self.stack.enter_context(self.nc.sbuf_tensor("arena", [128, self.ARENA // 2], BF16))
        self.h16 = a
        self.h32 = a.bitcast(F32)
        self.hi32 = a.bitcast(I32)

    def alloc_bytes(self, n):
        n = (n + 31) // 32 * 32
        off = self.sb_off
        self.sb_off += n
        assert self.sb_off <= self.ARENA, f"SBUF overflow {self.sb_off}"
        return off

    def mark(self):
        return self.sb_off

    def release(self, mark):
        self.sb_off = mark

    def res(self, name=None):
        self.nres += 1
        return Res(name or f"r{self.nres}")

    def op(self, eng, fn, reads=(), writes=(), dma=False, acc=False):
        o = Op(eng, fn, dma)
        o.idx = len(self.ops)
        deps = set()
        for r in reads:
            if r.lastw is not None:
                deps.add(r.lastw)
        for w in writes:
            if w.lastw is not None:
                lw = self.ops[w.lastw]
                if not (acc and lw.eng == "pe" and eng == "pe" and not lw.dma):
                    deps.add(w.lastw)
            for rd in w.readers:
                deps.add(rd)
        deps.discard(o.idx)
        o.deps = sorted(deps)
        best = {}
        keep = []
        for d in deps:
            po = self.ops[d]
            if po.dma or po.fn is None:
                keep.append(d)
            elif po.eng not in best or best[po.eng] < d:
                best[po.eng] = d
        o.deps = sorted(keep + list(best.values()))
        for r in reads:
            if not dma:
                r.readers = [x for x in r.readers if self.ops[x].dma or self.ops[x].eng != eng]
            r.readers.append(o.idx)
        for w in writes:
            w.lastw = o.idx
            w.readers = []
        self.ops.append(o)
        self.by_eng[eng].append(o)
        if dma:
            self.dma_since.append(o.idx)
        return o

    def wait_only(self, eng, deps):
        o = Op(eng, None, False)
        o.idx = len(self.ops)
        o.deps = sorted(set(deps))
        self.ops.append(o)
        self.by_eng[eng].append(o)
        return o

    def barrier(self):
        deps = list(self.dma_since)
        for e in self.ENGS:
            for o in reversed(self.by_eng[e]):
                if o.fn is not None and not o.dma:
                    deps.append(o.idx)
                    break
        for e in self.ENGS:
            self.wait_only(e, deps)
        self.dma_since = []

    def dma(self, eng, out, in_, reads=(), writes=()):
        return self.op(eng, lambda e: e.dma_start(out=out, in_=in_), reads, writes, dma=True)

    def mm(self, out, lhsT, rhs, start, stop, reads=(), writes=(), acc=False):
        return self.op("pe", lambda e: e.matmul(out=out, lhsT=lhsT, rhs=rhs, start=start, stop=stop),
                       reads, writes, acc=acc)

    def tr(self, out, in_, ident, reads=(), writes=(), acc=False):
        return self.op("pe", lambda e: e.transpose(out=out, in_=in_, identity=ident), reads, writes, acc=acc)

    def act(self, out, in_, func, reads=(), writes=(), scale=1.0, bias=0.0, accum_out=None):
        def fn(e):
            kw = {}
            if accum_out is not None:
                kw["accum_out"] = accum_out
            return e.activation(out=out, in_=in_, func=func, bias=bias, scale=scale, **kw)
        return self.op("act", fn, reads, writes)

    def tt(self, eng, out, in0, in1, op, reads=(), writes=()):
        return self.op(eng, lambda e: e.tensor_tensor(out=out, in0=in0, in1=in1, op=op), reads, writes)

    def ts(self, eng, out, in0, s1, s2, op0, op1=None, reads=(), writes=(), accum_out=None):
        def fn(e):
            kw = {}
            if accum_out is not None:
                kw["accum_out"] = accum_out
            if op1 is None:
                return e.tensor_scalar(out=out, in0=in0, scalar1=s1, scalar2=None, op0=op0, **kw)
            return e.tensor_scalar(out=out, in0=in0, scalar1=s1, scalar2=s2, op0=op0, op1=op1, **kw)
        return self.op(eng, fn, reads, writes)

    def stt(self, eng, out, in0, scalar, in1, op0, op1, reads=(), writes=()):
        return self.op(eng, lambda e: e.scalar_tensor_tensor(out=out, in0=in0, scalar=scalar, in1=in1,
                                                             op0=op0, op1=op1), reads, writes)

    def copy(self, eng, out, in_, reads=(), writes=()):
        if eng == "act":
            return self.act(out, in_, AF.Copy, reads, writes)
        return self.op(eng, lambda e: e.tensor_copy(out=out, in_=in_), reads, writes)

    def memset(self, eng, ap, val, writes=()):
        return self.op(eng, lambda e: e.memset(ap, val), (), writes)

    def emit(self):
        nc = self.nc
        ops = self.ops
        for o in ops:
            for d in o.deps:
                ops[d].signal = True
        sems = []

        def new_sem(nm):
            s = self.stack.enter_context(nc.semaphore(nm))
            sems.append(s)
            return s

        for e in self.ENGS:
            cur = None
            cnt = 0
            slots = []
            nslot = 0
            for o in self.by_eng[e]:
                if o.fn is None:
                    continue
                if o.dma:
                    o.signal = True
                    si = nslot % DMA_POOL
                    if si >= len(slots):
                        slots.append([new_sem(f"d_{e}_{si}"), 0, None])
                    s = slots[si]
                    nslot += 1
                    o.slot_prev = s[2]
                    s[1] += 16
                    if s[1] > SEM_LIMIT:
                        s[0] = new_sem(f"d_{e}_x{o.idx}")
                        s[1] = 16
                    o.sem, o.tick = s[0], s[1]
                    s[2] = o
                elif o.signal:
                    if cur is None or cnt >= SEM_LIMIT:
                        cur = new_sem(f"c_{e}_{o.idx}")
                        cnt = 0
                    cnt += 1
                    o.sem, o.tick = cur, cnt
        self.n_sems = len(sems)

        def emit_engine(ename, eobj):
            seen = {}
            for o in self.by_eng[ename]:
                need = {}
                plist = [ops[d] for d in o.deps]
                if o.dma and o.slot_prev is not None:
                    plist.append(o.slot_prev)
                for p in plist:
                    if p.sem is None:
                        continue
                    k = id(p.sem)
                    if seen.get(k, 0) >= p.tick:
                        continue
                    if k not in need or need[k][1] < p.tick:
                        need[k] = (p.sem, p.tick)
                for k, (s, v) in need.items():
                    eobj.wait_ge(s, v)
                    seen[k] = v
                if o.fn is None:
                    continue
                ins = o.fn(eobj)
                if o.signal:
                    ins.then_inc(o.sem, 16 if o.dma else 1)

        with nc.Block() as block:
            @block.sync
            def _(e):
                emit_engine("sp", e)

            @block.tensor
            def _(e):
                emit_engine("pe", e)

            @block.scalar
            def _(e):
                emit_engine("act", e)

            @block.vector
            def _(e):
                emit_engine("dve", e)

            @block.gpsimd
            def _(e):
                emit_engine("pool", e)


class Buf:
    def __init__(self, P, shape, dtype, name=None):
        self.shape = list(shape)
        self.dtype = dtype
        self.row = int(np.prod(shape[1:]))
        esz = 2 if dtype == BF16 else 4
        off = P.alloc_bytes(self.row * esz)
        self.h = P.h16 if dtype == BF16 else (P.h32 if dtype == F32 else P.hi32)
        self.base = off // esz
        self.ps = P.ARENA // esz
        self.r = P.res(name)
        pat = [[self.ps, shape[0]]]
        st = self.row
        for d in shape[1:]:
            st //= d
            pat.append([st, d])
        self.t = bass.AP(self.h, self.base, pat)

    def __getitem__(self, k):
        return self.t[k]

    def ap(self, off, free, p0=0, npart=None):
        if npart is None:
            npart = self.shape[0] - p0
        return bass.AP(self.h, self.base + p0 * self.ps + off, [[self.ps, npart]] + [list(x) for x in free])


class Ctx:
    def __init__(self, nc, stack):
        self.nc = nc
        self.P = Prog(nc, stack)
        P = self.P
        P.init_arena()
        self.ps = []
        for i in range(7):
            t = stack.enter_context(nc.psum_tensor(f"psf{i}", [128, 512], F32))
            self.ps.append((t, P.res(f"psf{i}")))
        self.pb = []
        t = stack.enter_context(nc.psum_tensor("psb0", [128, 1024], BF16))
        r = P.res("psb0")
        self.pb = [(t, r), (t, r)]
        self.dram = {}
        self.outs = []

    def din(self, name, shape, dtype):
        t = self.nc.dram_tensor(name, list(shape), dtype, kind="ExternalInput").ap()
        self.dram[name] = (t, self.P.res(name))
        return t

    def dout(self, name, shape, dtype):
        t = self.nc.dram_tensor(name, list(shape), dtype, kind="ExternalOutput").ap()
        self.dram[name] = (t, self.P.res(name))
        self.outs.append(name)
        return t

    def dint(self, name, shape, dtype):
        t = self.nc.dram_tensor(name, list(shape), dtype, kind="Internal").ap()
        self.dram[name] = (t, self.P.res(name))
        return t

    def consts(self):
        P = self.P
        self.ident = Buf(P, [128, 128], BF16, "ident")
        self.ones16 = Buf(P, [128, 128], BF16, "ones16")
        self.ones32 = Buf(P, [128, 64], F32, "ones32")
        self.cinfo = Buf(P, [128, 4], F32, "cinfo")
        self.d0 = Buf(P, [128, 128], F32, "d0")
        self.d0t = Buf(P, [128, 128], F32, "d0t")
        tmpi = Buf(P, [128, 128], I32, "tmpi")
        P.memset("pool", self.ident[:], 0.0, writes=[self.ident.r])
        P.op("pool", lambda e: e.affine_select(out=self.ident[:], in_=self.ident[:], pattern=[[-1, 128]],
                                               compare_op=ALU.not_equal, fill=1.0, base=0, channel_multiplier=1),
             reads=[self.ident.r], writes=[self.ident.r])
        P.memset("pool", self.ones16[:], 1.0, writes=[self.ones16.r])
        P.memset("pool", self.ones32[:], 1.0, writes=[self.ones32.r])
        ci, cr = self.dram["cinfo"]
        P.dma("sp", self.cinfo[:], ci[:, :], reads=[cr], writes=[self.cinfo.r])
        P.op("pool", lambda e: e.iota(tmpi[:], pattern=[[1, 128]], base=0, channel_multiplier=-1), writes=[tmpi.r])
        P.copy("dve", self.d0[:], tmpi[:], reads=[tmpi.r], writes=[self.d0.r])
        P.op("pool", lambda e: e.iota(tmpi[:], pattern=[[-1, 128]], base=0, channel_multiplier=1),
             reads=[self.d0.r], writes=[tmpi.r])
        P.copy("dve", self.d0t[:], tmpi[:], reads=[tmpi.r], writes=[self.d0t.r])


def load_weight(cx, dst, src, K, N, stage):
    P = cx.P
    src_ap, src_r = src
    i = 0
    SW = stage[0].shape[1]
    for kc in range((K + 127) // 128):
        rows = min(128, K - kc * 128)
        for n0 in range(0, N, SW):
            n = min(SW, N - n0)
            st = stage[i % 2]
            i += 1
            P.dma("sp", st[0:rows, 0:n], src_ap[kc * 128:kc * 128 + rows, n0:n0 + n], reads=[src_r], writes=[st.r])
            P.copy(("dve", "act", "pool")[i % 3], dst[0:rows, kc, n0:n0 + n], st[0:rows, 0:n], reads=[st.r], writes=[dst.r])


def load_gain(cx, name, row):
    P = cx.P
    g = Buf(P, [128, D], F32, name)
    ap, r = cx.dram[name]
    P.dma("sp", g[:], ap[row:row + 1, :].partition_broadcast(128), reads=[r], writes=[g.r])
    return g


def rms_rstd(cx, src_ap, src_res, junk, sm, col):
    P = cx.P
    P.act(junk[:], src_ap, AF.Square, reads=src_res, writes=[junk.r, sm.r], accum_out=sm[:, col:col + 1])
    P.act(sm[:, col + 1:col + 2], sm[:, col:col + 1], AF.Sqrt, reads=[sm.r], writes=[sm.r], scale=1.0 / D, bias=EPS)
    P.op("dve", lambda e: e.reciprocal(out=sm[:, col + 2:col + 3], in_=sm[:, col + 1:col + 2]),
         reads=[sm.r], writes=[sm.r])
    return sm[:, col + 2:col + 3]


def transpose_rows(cx, dstT, src, nchunks, k, src_reads):
    P = cx.P
    for c0 in range(0, nchunks, 8):
        n = min(8, nchunks - c0)
        pbt, pbr = cx.pb[(k + c0 // 8) % 2]
        for c in range(n):
            P.tr(pbt[:, c * 128:(c + 1) * 128], src[:, (c0 + c) * 128:(c0 + c + 1) * 128], cx.ident[:],
                 reads=list(src_reads) + [cx.ident.r], writes=[pbr], acc=(c > 0))
        eng = "act" if (c0 // 8) % 2 == 0 else "dve"
        P.copy(eng, dstT.ap(c0 * 128, [[1, n * 128]]), pbt[:, 0:n * 128], reads=[pbr], writes=[dstT.r])


def phase_pre(cx, li, ple_li):
    P = cx.P
    mark = P.mark()
    do_ple = ple_li is not None
    do_in = li is not None
    stage = [Buf(P, [128, 1024], F32, "stage") for _ in range(2)]
    hb = [Buf(P, [128, D], F32, "hs") for _ in range(2)]
    sm = Buf(P, [128, 16], F32, "sm")
    ub = Buf(P, [128, D], BF16, "u")
    junk = ub
    uT = Buf(P, [128, 8, 128], BF16, "uT")
    h_ap, h_r = cx.dram["h_in"]
    if do_ple:
        wg = Buf(P, [128, 8, D], BF16, "wg")
        wp = Buf(P, [128, 2, D], BF16, "wp")
        load_weight(cx, wg, cx.dram["w_ple_gate"], D, D, stage)
        load_weight(cx, wp, cx.dram["w_ple"], 256, D, stage)
        gple = load_gain(cx, "g_ple", 0)
        pt = [Buf(P, [128, 256], F32, "p") for _ in range(2)]
        pbf = Buf(P, [128, 256], BF16, "pbf")
        pT = Buf(P, [128, 2, 128], BF16, "pT")
        gate = Buf(P, [128, D], F32, "gate")
        tmp = Buf(P, [128, D], F32, "tmp")
        p_ap, p_r = cx.dram["p"]
        ho_ap, ho_r = cx.dram["h_out"]
    if do_in:
        L = LAYERS[li]
        nT = L["nq"] + L["nk"]
        TC = nT * 64
        NV = L["nv"]
        NW = TC + NV
        win = Buf(P, [128, 8, NW], BF16, "win")
        load_weight(cx, win, cx.dram["w_in"], D, NW, stage)
        gpre = load_gain(cx, "g_mix_pre", 0)
        tq = Buf(P, [128, TC], F32, "tq")
        tqb = Buf(P, [128, TC], BF16, "tqb")
        tTs = [Buf(P, [128, nT // 2, 128], BF16, "tTs") for _ in range(2)]
        tvs = [Buf(P, [128, NV], BF16, "tvs") for _ in range(2)]
        qT_ap, qT_r = cx.dram["qT_o"]
        kT_ap, kT_r = cx.dram["kT_o"]
        tv_ap, tv_r = cx.dram["tv_o"]
        if L["name"] == "dsa":
            wis = Buf(P, [128, 8], F32, "wis")
            wi_ap, wi_r = cx.dram["wi_o"]
        if L["rope"]:
            cs = Buf(P, [128, 2, NT, 8], F32, "cs")
            posi = Buf(P, [128, NT], I32, "posi")
            posf = Buf(P, [128, NT], F32, "posf")
            invf = Buf(P, [128, 8], F32, "invf")
            ang = Buf(P, [128, 2, NT, 8], F32, "ang")
            kk = Buf(P, [128, 2, NT, 8], F32, "kk")
            kki = Buf(P, [128, 2, NT, 8], I32, "kki")
            rt = Buf(P, [128, 4, nT * 8], F32, "ropetmp")
            pa, pr = cx.dram["pos"]
            ia, ir = cx.dram["invf"]
            P.dma("sp", posi[:], pa[:, :], reads=[pr], writes=[posi.r])
            P.dma("sp", invf[:], ia[:, :], reads=[ir], writes=[invf.r])
            P.copy("dve", posf[:], posi[:], reads=[posi.r], writes=[posf.r])
            NA = NT * 8
            P.tt("dve", ang.ap(NA, [[8, NT], [1, 8]]), posf.ap(0, [[1, NT], [0, 8]]), invf.ap(0, [[0, NT], [1, 8]]),
                 ALU.mult, reads=[posf.r, invf.r], writes=[ang.r])
            P.ts("dve", ang.ap(0, [[1, NA]]), ang.ap(NA, [[1, NA]]), math.pi / 2, None, ALU.add,
                 reads=[ang.r], writes=[ang.r])
            P.ts("dve", kk.ap(0, [[1, 2 * NA]]), ang.ap(0, [[1, 2 * NA]]), 1.0 / (2 * math.pi), None, ALU.mult,
                 reads=[ang.r], writes=[kk.r])
            P.copy("dve", kki.ap(0, [[1, 2 * NA]]), kk.ap(0, [[1, 2 * NA]]), reads=[kk.r], writes=[kki.r])
            P.copy("dve", kk.ap(0, [[1, 2 * NA]]), kki.ap(0, [[1, 2 * NA]]), reads=[kki.r], writes=[kk.r])
            P.stt("dve", ang.ap(0, [[1, 2 * NA]]), kk.ap(0, [[1, 2 * NA]]), -2 * math.pi, ang.ap(0, [[1, 2 * NA]]),
                  ALU.mult, ALU.add, reads=[kk.r, ang.r], writes=[ang.r])
            P.ts("dve", kk.ap(0, [[1, 2 * NA]]), ang.ap(0, [[1, 2 * NA]]), math.pi, -2 * math.pi, ALU.is_gt, ALU.mult,
                 reads=[ang.r], writes=[kk.r])
            P.tt("dve", ang.ap(0, [[1, 2 * NA]]), ang.ap(0, [[1, 2 * NA]]), kk.ap(0, [[1, 2 * NA]]), ALU.add,
                 reads=[ang.r, kk.r], writes=[ang.r])
            P.ts("dve", kk.ap(0, [[1, 2 * NA]]), ang.ap(0, [[1, 2 * NA]]), -math.pi, 2 * math.pi, ALU.is_lt, ALU.mult,
                 reads=[ang.r], writes=[kk.r])
            P.tt("dve", ang.ap(0, [[1, 2 * NA]]), ang.ap(0, [[1, 2 * NA]]), kk.ap(0, [[1, 2 * NA]]), ALU.add,
                 reads=[ang.r, kk.r], writes=[ang.r])
            P.act(cs.ap(0, [[1, 2 * NA]]), ang.ap(0, [[1, 2 * NA]]), AF.Sin, reads=[ang.r], writes=[cs.r])

    for m in range(NT):
        hs = hb[m % 2]
        P.dma("sp", hs[:], h_ap[m * 128:(m + 1) * 128, :], reads=[h_r], writes=[hs.r])
        if do_ple:
            pp = pt[m % 2]
            P.dma("sp", pp[:], p_ap[m * 128:(m + 1) * 128, :], reads=[p_r], writes=[pp.r])
            rstd = rms_rstd(cx, hs[:], [hs.r], junk, sm, 0)
            P.stt("dve", ub[:], hs[:], rstd, gple[:], ALU.mult, ALU.mult, reads=[hs.r, sm.r, gple.r], writes=[ub.r])
            transpose_rows(cx, uT, ub, 8, m, [ub.r])
            (g0, g0r), (g1, g1r) = cx.ps[0], cx.ps[1]
            for n, (gt, gr) in enumerate(((g0, g0r), (g1, g1r))):
                for c in range(8):
                    P.mm(gt[:, :], uT[:, c, :], wg[:, c, n * 512:(n + 1) * 512], c == 0, c == 7,
                         reads=[uT.r, wg.r], writes=[gr], acc=(c > 0))
                P.act(gate[:, n * 512:(n + 1) * 512], gt[:, :], AF.Sigmoid, reads=[gr], writes=[gate.r])
            P.copy("pool", pbf[:], pp[:], reads=[pp.r], writes=[pbf.r])
            transpose_rows(cx, pT, pbf, 2, m + 1, [pbf.r])
            for n in range(2):
                wt, wr = cx.ps[2 + n]
                for c in range(2):
                    P.mm(wt[:, :], pT[:, c, :], wp[:, c, n * 512:(n + 1) * 512], c == 0, c == 1,
                         reads=[pT.r, wp.r], writes=[wr], acc=(c > 0))
                P.tt("dve", tmp[:, n * 512:(n + 1) * 512], wt[:, :], gate[:, n * 512:(n + 1) * 512], ALU.mult,
                     reads=[wr, gate.r], writes=[tmp.r])
            P.tt("pool", hs[:], hs[:], tmp[:], ALU.add, reads=[hs.r, tmp.r], writes=[hs.r])
            P.dma("pool", ho_ap[m * 128:(m + 1) * 128, :], hs[:], reads=[hs.r], writes=[ho_r])
        if not do_in:
            continue
        rstd = rms_rstd(cx, hs[:], [hs.r], junk, sm, 4)
        P.stt("dve", ub[:], hs[:], rstd, gpre[:], ALU.mult, ALU.mult, reads=[hs.r, sm.r, gpre.r], writes=[ub.r])
        transpose_rows(cx, uT, ub, 8, m, [ub.r])
        tv_s = tvs[m % 2]
        nchunk = 0
        chunks = [(a, min(512, TC - a)) for a in range(0, TC, 512)] + \
                 [(TC + a, min(512, NV - a)) for a in range(0, NV, 512)]
        for (c0, n) in chunks:
            pst, psr = cx.ps[nchunk % 4]
            nchunk += 1
            for c in range(8):
                P.mm(pst[:, 0:n], uT[:, c, :], win[:, c, c0:c0 + n], c == 0, c == 7,
                     reads=[uT.r, win.r], writes=[psr], acc=(c > 0))
            if c0 < TC:
                qcols = L["qs"] * 64
                a = c0
                while a < c0 + n:
                    if a < qcols:
                        b = min(c0 + n, qcols)
                        sc = 0.125
                    else:
                        b = c0 + n
                        sc = 1.0
                    P.act(tq[:, a:b], pst[:, a - c0:b - c0], AF.Copy, reads=[psr], writes=[tq.r], scale=sc)
                    a = b
            else:
                v0 = c0 - TC
                if L["name"] == "dsa" and v0 + n > 1024:
                    nn = 1024 - v0
                    if nn > 0:
                        P.copy("dve", tv_s[:, v0:v0 + nn], pst[:, 0:nn], reads=[psr], writes=[tv_s.r])
                    P.copy("dve", wis[:, :], pst[:, nn:nn + 8], reads=[psr], writes=[wis.r])
                    P.copy("dve", tv_s[:, 1024:1032], wis[:, :], reads=[wis.r], writes=[tv_s.r])
                    P.dma("pool", wi_ap[m * 128:(m + 1) * 128, :], wis[:, :], reads=[wis.r], writes=[wi_r])
                else:
                    P.copy("dve", tv_s[:, v0:v0 + n], pst[:, 0:n], reads=[psr], writes=[tv_s.r])
        if L["rope"]:
            H8 = nT * 8
            x1 = tq.ap(0, [[64, nT], [1, 8]])
            x2 = tq.ap(8, [[64, nT], [1, 8]])
            cosb = cs.ap(m * 8, [[0, nT], [1, 8]])
            sinb = cs.ap(NT * 8 + m * 8, [[0, nT], [1, 8]])
            ta = rt.ap(0, [[8, nT], [1, 8]])
            tb = rt.ap(H8, [[8, nT], [1, 8]])
            tc = rt.ap(2 * H8, [[8, nT], [1, 8]])
            td = rt.ap(3 * H8, [[8, nT], [1, 8]])
            P.tt("dve", ta, x1, cosb, ALU.mult, reads=[tq.r, cs.r], writes=[rt.r])
            P.tt("pool", tb, x2, sinb, ALU.mult, reads=[tq.r, cs.r], writes=[rt.r])
            P.tt("dve", tc, x2, cosb, ALU.mult, reads=[tq.r, cs.r], writes=[rt.r])
            P.tt("pool", td, x1, sinb, ALU.mult, reads=[tq.r, cs.r], writes=[rt.r])
            P.tt("dve", x1, ta, tb, ALU.subtract, reads=[rt.r], writes=[tq.r])
            P.tt("dve", x2, tc, td, ALU.add, reads=[rt.r], writes=[tq.r])
        P.copy("pool", tqb[:], tq[:], reads=[tq.r], writes=[tqb.r])
        tT = tTs[m % 2]
        transpose_rows(cx, tT, tqb, nT // 2, m, [tqb.r])
        nqc = L["nq"] // 2
        nkc = L["nk"] // 2
        P.dma("pool", qT_ap[:, m * 128:(m + 1) * 128].rearrange("(c p) t -> p c t", p=128), tT[:, 0:nqc, :],
              reads=[tT.r], writes=[qT_r])
        P.dma("pool", kT_ap[:, m * 128:(m + 1) * 128].rearrange("(c p) t -> p c t", p=128), tT[:, nqc:nqc + nkc, :],
              reads=[tT.r], writes=[kT_r])
        P.dma("pool", tv_ap[m * 128:(m + 1) * 128, :], tv_s[:], reads=[tv_s.r], writes=[tv_r])
    P.barrier()
    P.release(mark)


def phase_post(cx, li):
    P = cx.P
    L = LAYERS[li]
    mark = P.mark()
    NOC = L["no"] // 128
    stage = [Buf(P, [128, 1024], F32, "stage") for _ in range(2)]
    wo = Buf(P, [128, NOC, D], BF16, "wo")
    w1 = Buf(P, [128, 8, 4096], BF16, "w1")
    w2 = Buf(P, [128, 32, D], BF16, "w2")
    load_weight(cx, wo, cx.dram["w_out"], L["no"], D, stage)
    load_weight(cx, w1, cx.dram["w_ff_in"], D, 4096, stage)
    load_weight(cx, w2, cx.dram["w_ff_out"], 4096, D, stage)
    gpost = load_gain(cx, "g_mix_post", 0)
    gfpre = load_gain(cx, "g_ffn_pre", 0)
    gfpost = load_gain(cx, "g_ffn_post", 0)
    hb = [Buf(P, [128, D], F32, "hs") for _ in range(2)]
    ob = [Buf(P, [128, NOC, 128], BF16, "oT") for _ in range(2)]
    sm = Buf(P, [128, 16], F32, "sm")
    tmp = Buf(P, [128, D], F32, "tmp")
    ub = Buf(P, [128, D], BF16, "u")
    junk = ub
    uT = Buf(P, [128, 8, 128], BF16, "uT")
    rl = [Buf(P, [128, 512], F32, "relu") for _ in range(2)]
    aT = Buf(P, [128, 32, 128], BF16, "aT")
    h_ap, h_r = cx.dram["h_res"]
    ho_ap, ho_r = cx.dram["h_mid"]
    oT_ap, oT_r = cx.dram["oT"]
    for m in range(NT):
        hs = hb[m % 2]
        ot = ob[m % 2]
        P.dma("sp", hs[:], h_ap[m * 128:(m + 1) * 128, :], reads=[h_r], writes=[hs.r])
        P.dma("sp", ot[:], oT_ap[:, m * 128:(m + 1) * 128].rearrange("(c p) t -> p c t", p=128),
              reads=[oT_r], writes=[ot.r])
        y = (cx.ps[0], cx.ps[1])
        for n in range(2):
            yt, yr = y[n]
            for c in range(NOC):
                P.mm(yt[:, :], ot[:, c, :], wo[:, c, n * 512:(n + 1) * 512], c == 0, c == NOC - 1,
                     reads=[ot.r, wo.r], writes=[yr], acc=(c > 0))
        P.act(junk[:, 0:512], y[0][0][:, :], AF.Square, reads=[y[0][1]], writes=[junk.r, sm.r], accum_out=sm[:, 0:1])
        P.act(junk[:, 512:1024], y[1][0][:, :], AF.Square, reads=[y[1][1]], writes=[junk.r, sm.r], accum_out=sm[:, 1:2])
        P.tt("dve", sm[:, 2:3], sm[:, 0:1], sm[:, 1:2], ALU.add, reads=[sm.r], writes=[sm.r])
        P.act(sm[:, 3:4], sm[:, 2:3], AF.Sqrt, reads=[sm.r], writes=[sm.r], scale=1.0 / D, bias=EPS)
        P.op("dve", lambda e: e.reciprocal(out=sm[:, 4:5], in_=sm[:, 3:4]), reads=[sm.r], writes=[sm.r])
        for n in range(2):
            P.stt("dve", tmp[:, n * 512:(n + 1) * 512], y[n][0][:, :], sm[:, 4:5], gpost[:, n * 512:(n + 1) * 512],
                  ALU.mult, ALU.mult, reads=[y[n][1], sm.r, gpost.r], writes=[tmp.r])
        P.tt("pool", hs[:], hs[:], tmp[:], ALU.add, reads=[hs.r, tmp.r], writes=[hs.r])
        rstd = rms_rstd(cx, hs[:], [hs.r], junk, sm, 5)
        P.stt("dve", ub[:], hs[:], rstd, gfpre[:], ALU.mult, ALU.mult, reads=[hs.r, sm.r, gfpre.r], writes=[ub.r])
        transpose_rows(cx, uT, ub, 8, m, [ub.r])
        for f4 in range(8):
            pt_, pr_ = cx.ps[2 + f4 % 2]
            for f in range(4):
                ff = f4 * 4 + f
                for c in range(8):
                    P.mm(pt_[:, f * 128:(f + 1) * 128], w1[:, c, ff * 128:(ff + 1) * 128], uT[:, c, :], c == 0, c == 7,
                         reads=[w1.r, uT.r], writes=[pr_], acc=(c > 0 or f > 0))
            r_ = rl[f4 % 2]
            P.act(r_[:], pt_[:, :], AF.Relu, reads=[pr_], writes=[r_.r])
            P.tt("pool", aT.ap(f4 * 512, [[1, 512]]), r_[:], r_[:], ALU.mult, reads=[r_.r], writes=[aT.r])
        fo = (cx.ps[4], cx.ps[5])
        for n in range(2):
            ft, fr = fo[n]
            for ff in range(32):
                P.mm(ft[:, :], aT[:, ff, :], w2[:, ff, n * 512:(n + 1) * 512], ff == 0, ff == 31,
                     reads=[aT.r, w2.r], writes=[fr], acc=(ff > 0))
        P.act(junk[:, 0:512], fo[0][0][:, :], AF.Square, reads=[fo[0][1]], writes=[junk.r, sm.r], accum_out=sm[:, 8:9])
        P.act(junk[:, 512:1024], fo[1][0][:, :], AF.Square, reads=[fo[1][1]], writes=[junk.r, sm.r], accum_out=sm[:, 9:10])
        P.tt("dve", sm[:, 10:11], sm[:, 8:9], sm[:, 9:10], ALU.add, reads=[sm.r], writes=[sm.r])
        P.act(sm[:, 11:12], sm[:, 10:11], AF.Sqrt, reads=[sm.r], writes=[sm.r], scale=1.0 / D, bias=EPS)
        P.op("dve", lambda e: e.reciprocal(out=sm[:, 12:13], in_=sm[:, 11:12]), reads=[sm.r], writes=[sm.r])
        for n in range(2):
            P.stt("dve", tmp[:, n * 512:(n + 1) * 512], fo[n][0][:, :], sm[:, 12:13], gfpost[:, n * 512:(n + 1) * 512],
                  ALU.mult, ALU.mult, reads=[fo[n][1], sm.r, gfpost.r], writes=[tmp.r])
        P.tt("pool", hs[:], hs[:], tmp[:], ALU.add, reads=[hs.r, tmp.r], writes=[hs.r])
        P.dma("pool", ho_ap[m * 128:(m + 1) * 128, :], hs[:], reads=[hs.r], writes=[ho_r])
    P.barrier()
    P.release(mark)


def rms_rstd2(cx, src_aps, src_res, junk, sm, col):
    P = cx.P
    for i, ap in enumerate(src_aps):
        w = 512 if len(src_aps) == 2 else D
        P.act(junk[:, i * 512:i * 512 + w], ap, AF.Square, reads=src_res, writes=[junk.r, sm.r],
              accum_out=sm[:, col + i:col + i + 1])
    if len(src_aps) == 2:
        P.tt("dve", sm[:, col + 2:col + 3], sm[:, col:col + 1], sm[:, col + 1:col + 2], ALU.add, reads=[sm.r], writes=[sm.r])
        tot = sm[:, col + 2:col + 3]
    else:
        tot = sm[:, col:col + 1]
    P.act(sm[:, col + 3:col + 4], tot, AF.Sqrt, reads=[sm.r], writes=[sm.r], scale=1.0 / D, bias=EPS)
    P.op("dve", lambda e: e.reciprocal(out=sm[:, col + 4:col + 5], in_=sm[:, col + 3:col + 4]), reads=[sm.r], writes=[sm.r])
    return sm[:, col + 4:col + 5]


def phase_post2(cx, li):
    P = cx.P
    L = LAYERS[li]
    NOC = L["no"] // 128
    h_ap, h_r = cx.dram["h_res"]
    ha_ap, ha_r = cx.dram["h_a"]
    ho_ap, ho_r = cx.dram["h_mid"]
    oT_ap, oT_r = cx.dram["oT"]
    mark = P.mark()
    stage = [Buf(P, [128, 1024], F32, "stage") for _ in range(4)]
    wo = Buf(P, [128, NOC, D], BF16, "wo")
    load_weight(cx, wo, cx.dram["w_out"], L["no"], D, stage)
    gpost = load_gain(cx, "g_mix_post", 0)
    slots = [dict(hs=Buf(P, [128, D], F32, "hs"), ot=Buf(P, [128, NOC, 128], BF16, "oT"), tmp=Buf(P, [128, D], F32, "tmp"),
                  junk=Buf(P, [128, D], BF16, "junk"), sm=Buf(P, [128, 16], F32, "sm"),
                  y=(cx.ps[2 * s], cx.ps[2 * s + 1])) for s in range(2)]

    def streamA(m):
        def gen(s):
            sl = slots[s]
            hs, ot, tmp, junk, sm, y = (sl[k] for k in ("hs", "ot", "tmp", "junk", "sm", "y"))
            P.dma("sp", hs[:], h_ap[m * 128:(m + 1) * 128, :], reads=[h_r], writes=[hs.r])
            P.dma("sp", ot[:], oT_ap[:, m * 128:(m + 1) * 128].rearrange("(c p) t -> p c t", p=128),
                  reads=[oT_r], writes=[ot.r])
            yield
            for n in range(2):
                yt, yr = y[n]
                for c in range(NOC):
                    P.mm(yt[:, :], ot[:, c, :], wo[:, c, n * 512:(n + 1) * 512], c == 0, c == NOC - 1,
                         reads=[ot.r, wo.r], writes=[yr], acc=(c > 0))
            yield
            rstd = rms_rstd2(cx, [y[0][0][:, :], y[1][0][:, :]], [y[0][1], y[1][1]], junk, sm, 0)
            yield
            for n in range(2):
                P.stt("dve", tmp[:, n * 512:(n + 1) * 512], y[n][0][:, :], rstd, gpost[:, n * 512:(n + 1) * 512],
                      ALU.mult, ALU.mult, reads=[y[n][1], sm.r, gpost.r], writes=[tmp.r])
            P.tt("dve", hs[:], hs[:], tmp[:], ALU.add, reads=[hs.r, tmp.r], writes=[hs.r])
            P.dma("pool", ha_ap[m * 128:(m + 1) * 128, :], hs[:], reads=[hs.r], writes=[ha_r])
            yield
        return gen

    run_streams([streamA(m) for m in range(DBG.get("nt", NT))])
    P.barrier()
    P.release(mark)
    mark = P.mark()
    stage = [Buf(P, [128, 1024], F32, "stage") for _ in range(4)]
    w1 = Buf(P, [128, 8, 4096], BF16, "w1")
    w2 = Buf(P, [128, 32, D], BF16, "w2")
    load_weight(cx, w1, cx.dram["w_ff_in"], D, 4096, stage)
    load_weight(cx, w2, cx.dram["w_ff_out"], 4096, D, stage)
    gfpre = load_gain(cx, "g_ffn_pre", 0)
    gfpost = load_gain(cx, "g_ffn_post", 0)
    slots = [dict(hs=Buf(P, [128, D], F32, "hs"), tmp=Buf(P, [128, D], F32, "tmp"), ub=Buf(P, [128, D], BF16, "u"),
                  uT=Buf(P, [128, 8, 128], BF16, "uT"), rl=Buf(P, [128, 512], F32, "relu"),
                  aT=Buf(P, [128, 32, 128], BF16, "aT"), sm=Buf(P, [128, 16], F32, "sm"),
                  F=(cx.ps[3 * s], cx.ps[3 * s + 1]), H=cx.ps[3 * s + 2]) for s in range(2)]

    def streamB(m):
        def gen(s):
            sl = slots[s]
            hs, tmp, ub, uT, rl, aT, sm, F, H = (sl[k] for k in ("hs", "tmp", "ub", "uT", "rl", "aT", "sm", "F", "H"))
            P.dma("sp", hs[:], ha_ap[m * 128:(m + 1) * 128, :], reads=[ha_r], writes=[hs.r])
            yield
            rstd = rms_rstd2(cx, [hs[:]], [hs.r], ub, sm, 0)
            P.stt("dve", ub[:], hs[:], rstd, gfpre[:], ALU.mult, ALU.mult, reads=[hs.r, sm.r, gfpre.r], writes=[ub.r])
            yield
            transpose_rows(cx, uT, ub, 8, 0, [ub.r])
            yield
            Ht, Hr = H
            for f4 in range(8):
                for f in range(4):
                    ff = f4 * 4 + f
                    for c in range(8):
                        P.mm(Ht[:, f * 128:(f + 1) * 128], w1[:, c, ff * 128:(ff + 1) * 128], uT[:, c, :],
                             f == 0 and c == 0, f == 3 and c == 7, reads=[w1.r, uT.r], writes=[Hr], acc=(c > 0 or f > 0))
                yield
                P.act(rl[:], Ht[:, :], AF.Relu, reads=[Hr], writes=[rl.r])
                P.tt("pool", aT.ap(f4 * 512, [[1, 512]]), rl[:], rl[:], ALU.mult, reads=[rl.r], writes=[aT.r])
            yield
            for n in range(2):
                ft, fr = F[n]
                for ff in range(32):
                    P.mm(ft[:, :], aT[:, ff, :], w2[:, ff, n * 512:(n + 1) * 512], ff == 0, ff == 31,
                         reads=[aT.r, w2.r], writes=[fr], acc=(ff > 0))
                yield
            rstd = rms_rstd2(cx, [F[0][0][:, :], F[1][0][:, :]], [F[0][1], F[1][1]], ub, sm, 8)
            yield
            for n in range(2):
                P.stt("dve", tmp[:, n * 512:(n + 1) * 512], F[n][0][:, :], rstd, gfpost[:, n * 512:(n + 1) * 512],
                      ALU.mult, ALU.mult, reads=[F[n][1], sm.r, gfpost.r], writes=[tmp.r])
            P.tt("dve", hs[:], hs[:], tmp[:], ALU.add, reads=[hs.r, tmp.r], writes=[hs.r])
            P.dma("pool", ho_ap[m * 128:(m + 1) * 128, :], hs[:], reads=[hs.r], writes=[ho_r])
            yield
        return gen

    run_streams([streamB(m) for m in range(DBG.get("nt", NT))])
    P.barrier()
    P.release(mark)


def phase_pre2(cx, li, ple_li):
    P = cx.P
    mark = P.mark()
    do_ple = ple_li is not None
    do_in = li is not None
    stage = [Buf(P, [128, 1024], F32, "stage") for _ in range(2)]
    h_ap, h_r = cx.dram["h_in"]
    TC = NV = nT = 0
    if do_ple:
        wg = Buf(P, [128, 8, D], BF16, "wg")
        wp = Buf(P, [128, 2, D], BF16, "wp")
        load_weight(cx, wg, cx.dram["w_ple_gate"], D, D, stage)
        load_weight(cx, wp, cx.dram["w_ple"], 256, D, stage)
        gple = load_gain(cx, "g_ple", 0)
        p_ap, p_r = cx.dram["p"]
        ho_ap, ho_r = cx.dram["h_out"]
    if do_in:
        L = LAYERS[li]
        nT = L["nq"] + L["nk"]
        TC = nT * 64
        NV = L["nv"]
        NW = TC + NV
        win = Buf(P, [128, 8, NW], BF16, "win")
        load_weight(cx, win, cx.dram["w_in"], D, NW, stage)
        gpre = load_gain(cx, "g_mix_pre", 0)
        qT_ap, qT_r = cx.dram["qT_o"]
        kT_ap, kT_r = cx.dram["kT_o"]
        tv_ap, tv_r = cx.dram["tv_o"]
        if L["name"] == "dsa":
            wi_ap, wi_r = cx.dram["wi_o"]
        if L["rope"]:
            cs = Buf(P, [128, 2, NT, 8], F32, "cs")
            rt = Buf(P, [128, 4, nT * 8], F32, "ropetmp")
            m2 = P.mark()
            posi = Buf(P, [128, NT], I32, "posi")
            posf = Buf(P, [128, NT], F32, "posf")
            invf = Buf(P, [128, 8], F32, "invf")
            ang = Buf(P, [128, 2, NT, 8], F32, "ang")
            kk = Buf(P, [128, 2, NT, 8], F32, "kk")
            kki = Buf(P, [128, 2, NT, 8], I32, "kki")
            pa, pr = cx.dram["pos"]
            ia, ir = cx.dram["invf"]
            P.dma("sp", posi[:], pa[:, :], reads=[pr], writes=[posi.r])
            P.dma("sp", invf[:], ia[:, :], reads=[ir], writes=[invf.r])
            P.copy("dve", posf[:], posi[:], reads=[posi.r], writes=[posf.r])
            NA = NT * 8
            fl = [[1, 2 * NA]]
            P.tt("dve", ang.ap(NA, [[8, NT], [1, 8]]), posf.ap(0, [[1, NT], [0, 8]]), invf.ap(0, [[0, NT], [1, 8]]),
                 ALU.mult, reads=[posf.r, invf.r], writes=[ang.r])
            P.ts("dve", ang.ap(0, [[1, NA]]), ang.ap(NA, [[1, NA]]), math.pi / 2, None, ALU.add, reads=[ang.r], writes=[ang.r])
            P.ts("dve", kk.ap(0, fl), ang.ap(0, fl), 1.0 / (2 * math.pi), None, ALU.mult, reads=[ang.r], writes=[kk.r])
            P.copy("dve", kki.ap(0, fl), kk.ap(0, fl), reads=[kk.r], writes=[kki.r])
            P.copy("dve", kk.ap(0, fl), kki.ap(0, fl), reads=[kki.r], writes=[kk.r])
            P.stt("dve", ang.ap(0, fl), kk.ap(0, fl), -2 * math.pi, ang.ap(0, fl), ALU.mult, ALU.add,
                  reads=[kk.r, ang.r], writes=[ang.r])
            P.ts("dve", kk.ap(0, fl), ang.ap(0, fl), math.pi, -2 * math.pi, ALU.is_gt, ALU.mult, reads=[ang.r], writes=[kk.r])
            P.tt("dve", ang.ap(0, fl), ang.ap(0, fl), kk.ap(0, fl), ALU.add, reads=[ang.r, kk.r], writes=[ang.r])
            P.ts("dve", kk.ap(0, fl), ang.ap(0, fl), -math.pi, 2 * math.pi, ALU.is_lt, ALU.mult, reads=[ang.r], writes=[kk.r])
            P.tt("dve", ang.ap(0, fl), ang.ap(0, fl), kk.ap(0, fl), ALU.add, reads=[ang.r, kk.r], writes=[ang.r])
            P.act(cs.ap(0, fl), ang.ap(0, fl), AF.Sin, reads=[ang.r], writes=[cs.r])
            P.barrier()
            P.release(m2)
    SW = max(2048, TC)
    slots = []
    for s in range(2):
        sl = dict(hs=Buf(P, [128, D], F32, "hs"), ub=Buf(P, [128, D], BF16, "u"), uT=Buf(P, [128, 8, 128], BF16, "uT"),
                  sm=Buf(P, [128, 16], F32, "sm"), SCR=Buf(P, [128, SW], F32, "scr"), ps=[cx.ps[3 * s + i] for i in range(3)])
        if do_ple:
            sl.update(pp=Buf(P, [128, 256], F32, "p"), pbf=Buf(P, [128, 256], BF16, "pbf"), pT=Buf(P, [128, 2, 128], BF16, "pT"))
        if do_in:
            sl.update(tqb=Buf(P, [128, TC], BF16, "tqb"), tT=Buf(P, [128, nT // 2, 128], BF16, "tTs"),
                      tvs=Buf(P, [128, NV], BF16, "tvs"))
            if L["name"] == "dsa":
                sl["wis"] = Buf(P, [128, 8], F32, "wis")
        slots.append(sl)

    def stream(m):
        def gen(s):
            sl = slots[s]
            hs, ub, uT, sm, SCR, ps = (sl[k] for k in ("hs", "ub", "uT", "sm", "SCR", "ps"))
            P.dma("sp", hs[:], h_ap[m * 128:(m + 1) * 128, :], reads=[h_r], writes=[hs.r])
            if do_ple:
                pp, pbf, pT = sl["pp"], sl["pbf"], sl["pT"]
                P.dma("sp", pp[:], p_ap[m * 128:(m + 1) * 128, :], reads=[p_r], writes=[pp.r])
                yield
                rstd = rms_rstd2(cx, [hs[:]], [hs.r], ub, sm, 0)
                P.stt("dve", ub[:], hs[:], rstd, gple[:], ALU.mult, ALU.mult, reads=[hs.r, sm.r, gple.r], writes=[ub.r])
                P.copy("pool", pbf[:], pp[:], reads=[pp.r], writes=[pbf.r])
                yield
                transpose_rows(cx, uT, ub, 8, 0, [ub.r])
                transpose_rows(cx, pT, pbf, 2, 0, [pbf.r])
                yield
                for n in range(2):
                    gt, gr = ps[n]
                    for c in range(8):
                        P.mm(gt[:, :], uT[:, c, :], wg[:, c, n * 512:(n + 1) * 512], c == 0, c == 7,
                             reads=[uT.r, wg.r], writes=[gr], acc=(c > 0))
                    P.act(SCR[:, n * 512:(n + 1) * 512], gt[:, :], AF.Sigmoid, reads=[gr], writes=[SCR.r])
                yield
                for n in range(2):
                    wt, wr = ps[2] if n == 0 else ps[0]
                    for c in range(2):
                        P.mm(wt[:, :], pT[:, c, :], wp[:, c, n * 512:(n + 1) * 512], c == 0, c == 1,
                             reads=[pT.r, wp.r], writes=[wr], acc=(c > 0))
                    P.tt("dve", SCR[:, 1024 + n * 512:1024 + (n + 1) * 512], wt[:, :], SCR[:, n * 512:(n + 1) * 512], ALU.mult,
                         reads=[wr, SCR.r], writes=[SCR.r])
                P.tt("dve", hs[:], hs[:], SCR[:, 1024:2048], ALU.add, reads=[hs.r, SCR.r], writes=[hs.r])
                P.dma("pool", ho_ap[m * 128:(m + 1) * 128, :], hs[:], reads=[hs.r], writes=[ho_r])
            yield
            if not do_in:
                return
            tqb, tT, tv_s = sl["tqb"], sl["tT"], sl["tvs"]
            rstd = rms_rstd2(cx, [hs[:]], [hs.r], ub, sm, 8)
            P.stt("dve", ub[:], hs[:], rstd, gpre[:], ALU.mult, ALU.mult, reads=[hs.r, sm.r, gpre.r], writes=[ub.r])
            yield
            transpose_rows(cx, uT, ub, 8, 0, [ub.r])
            yield
            chunks = [(a, min(512, TC - a)) for a in range(0, TC, 512)] + \
                     [(TC + a, min(512, NV - a)) for a in range(0, NV, 512)]
            qcols = L["qs"] * 64
            for ci, (c0, n) in enumerate(chunks):
                pst, psr = ps[ci % 3]
                for c in range(8):
                    P.mm(pst[:, 0:n], uT[:, c, :], win[:, c, c0:c0 + n], c == 0, c == 7,
                         reads=[uT.r, win.r], writes=[psr], acc=(c > 0))
                if c0 < TC:
                    a = c0
                    while a < c0 + n:
                        b, sc = (min(c0 + n, qcols), 0.125) if a < qcols else (c0 + n, 1.0)
                        P.act(SCR[:, a:b], pst[:, a - c0:b - c0], AF.Copy, reads=[psr], writes=[SCR.r], scale=sc)
                        a = b
                else:
                    v0 = c0 - TC
                    if L["name"] == "dsa" and v0 + n > 1024:
                        wis = sl["wis"]
                        nn = 1024 - v0
                        if nn > 0:
                            P.copy("dve", tv_s[:, v0:v0 + nn], pst[:, 0:nn], reads=[psr], writes=[tv_s.r])
                        P.copy("dve", wis[:, :], pst[:, nn:nn + 8], reads=[psr], writes=[wis.r])
                        P.copy("dve", tv_s[:, 1024:1032], wis[:, :], reads=[wis.r], writes=[tv_s.r])
                        P.dma("pool", wi_ap[m * 128:(m + 1) * 128, :], wis[:, :], reads=[wis.r], writes=[wi_r])
                    else:
                        P.copy("dve", tv_s[:, v0:v0 + n], pst[:, 0:n], reads=[psr], writes=[tv_s.r])
                if ci % 2 == 1:
                    yield
            yield
            if L["rope"]:
                H8 = nT * 8
                x1 = SCR.ap(0, [[64, nT], [1, 8]])
                x2 = SCR.ap(8, [[64, nT], [1, 8]])
                cosb = cs.ap(m * 8, [[0, nT], [1, 8]])
                sinb = cs.ap(NT * 8 + m * 8, [[0, nT], [1, 8]])
                ta, tb, tc, td = (rt.ap(i * H8, [[8, nT], [1, 8]]) for i in range(4))
                P.tt("dve", ta, x1, cosb, ALU.mult, reads=[SCR.r, cs.r], writes=[rt.r])
                P.tt("dve", tb, x2, sinb, ALU.mult, reads=[SCR.r, cs.r], writes=[rt.r])
                P.tt("dve", tc, x2, cosb, ALU.mult, reads=[SCR.r, cs.r], writes=[rt.r])
                P.tt("dve", td, x1, sinb, ALU.mult, reads=[SCR.r, cs.r], writes=[rt.r])
                P.tt("dve", x1, ta, tb, ALU.subtract, reads=[rt.r], writes=[SCR.r])
                P.tt("dve", x2, tc, td, ALU.add, reads=[rt.r], writes=[SCR.r])
            hlf = (TC // 2) // 128 * 128
            P.copy("act", tqb[:, 0:hlf], SCR[:, 0:hlf], reads=[SCR.r], writes=[tqb.r])
            P.copy("dve", tqb[:, hlf:TC], SCR[:, hlf:TC], reads=[SCR.r], writes=[tqb.r])
            yield
            for c0 in range(0, nT // 2, 8):
                nn = min(8, nT // 2 - c0)
                pbt, pbr = cx.pb[0]
                for c in range(nn):
                    P.tr(pbt[:, c * 128:(c + 1) * 128], tqb[:, (c0 + c) * 128:(c0 + c + 1) * 128], cx.ident[:],
                         reads=[tqb.r, cx.ident.r], writes=[pbr], acc=(c > 0))
                P.copy("act" if (c0 // 8) % 2 == 0 else "dve", tT.ap(c0 * 128, [[1, nn * 128]]), pbt[:, 0:nn * 128],
                       reads=[pbr], writes=[tT.r])
                yield
            nqc, nkc = L["nq"] // 2, L["nk"] // 2
            P.dma("pool", qT_ap[:, m * 128:(m + 1) * 128].rearrange("(c p) t -> p c t", p=128), tT[:, 0:nqc, :],
                  reads=[tT.r], writes=[qT_r])
            P.dma("pool", kT_ap[:, m * 128:(m + 1) * 128].rearrange("(c p) t -> p c t", p=128), tT[:, nqc:nqc + nkc, :],
                  reads=[tT.r], writes=[kT_r])
            P.dma("pool", tv_ap[m * 128:(m + 1) * 128, :], tv_s[:], reads=[tv_s.r], writes=[tv_r])
            yield
        return gen

    run_streams([stream(m) for m in range(DBG.get("nt", NT))])
    P.barrier()
    P.release(mark)


def run_streams(factories, nslots=2):
    pending = list(factories)
    active = []
    free = list(range(nslots))
    while pending or active:
        while pending and free:
            s = free.pop(0)
            active.append((pending.pop(0)(s), s))
        for item in list(active):
            g, s = item
            try:
                next(g)
            except StopIteration:
                active.remove(item)
                free.append(s)


class KV:
    def __init__(self, cx):
        P = cx.P
        self.Ks_sets = [Buf(P, [128, 2, 4, 2048], BF16, "Ks") for _ in range(2)]
        self.Vs = Buf(P, [128, 4, 16, 256], BF16, "Vs")
        self.Qz_sets = [[Buf(P, [128, 2, 2048], BF16, f"Qz{i}") for i in range(2)] for _ in range(2)]
        self.cur = 1
        for qs in self.Qz_sets:
            for q in qs:
                P.memset("pool", q[:], 0.0, writes=[q.r])

    @property
    def Ks(self):
        return self.Ks_sets[self.cur]

    @property
    def Qz(self):
        return self.Qz_sets[self.cur]

    def load(self, cx, qrow0, krow0, vcol0):
        P = cx.P
        self.cur = 1 - self.cur
        q_ap, q_r = cx.dram["qT"]
        k_ap, k_r = cx.dram["kTg"]
        v_ap, v_r = cx.dram["vg"]
        qsrc = q_ap[qrow0:qrow0 + 256, :].rearrange("(c p) t -> p c t", p=128)
        for par in range(2):
            P.dma("sp", self.Qz[par][par * 64:(par + 1) * 64, :, :], qsrc[par * 64:(par + 1) * 64, :, :],
                  reads=[q_r], writes=[self.Qz[par].r])
        for ch in range(2):
            P.dma("sp", self.Ks[:, ch, :, :],
                  k_ap[:, krow0 + ch * 128:krow0 + (ch + 1) * 128, :].rearrange("r p t -> p r t"),
                  reads=[k_r], writes=[self.Ks.r])
        if vcol0 is not None:
            for rk in range(4):
                P.dma("sp", self.Vs[:, rk, :, :], v_ap[rk, :, vcol0:vcol0 + 256].rearrange("(m p) f -> p m f", p=128),
                      reads=[v_r], writes=[self.Vs.r])

    def qk(self, cx, A, Ar, rank, ml, m, stop=True):
        P = cx.P
        Ks = self.Ks
        for h in range(4):
            qz = self.Qz[h % 2]
            P.mm(A[:, h * 128:(h + 1) * 128], Ks[:, h // 2, rank, ml * 128:(ml + 1) * 128],
                 qz[:, h // 2, m * 128:(m + 1) * 128], h == 0, stop and h == 3,
                 reads=[Ks.r, qz.r], writes=[Ar], acc=(h > 0))

    def av(self, cx, O, Or, pt, rank, ml, first, last, nrow):
        P = cx.P
        for h in range(4):
            P.mm(O[0:64, h * 128:(h + 1) * 128], self.Vs[:, rank, ml, h * 64:(h + 1) * 64], pt[:, h * 128:(h + 1) * 128],
                 first and h == 0, last and h == 3, reads=[self.Vs.r, pt.r], writes=[Or], acc=(not first or h > 0))
        if nrow == 65:
            P.mm(O[64:65, :], cx.ones16[:, 0:1], pt[:, :], first, last, reads=[cx.ones16.r, pt.r], writes=[Or], acc=True)


def causal_masks(cx, strict):
    P = cx.P
    M = Buf(P, [128, 4, 128], BF16, "cmask")
    for r in range(4):
        thr = 128 * r + (0.5 if strict else -0.5)
        P.ts("dve", M[:, r, :], cx.d0[:], cx.cinfo[:, 0:1], thr, ALU.add, ALU.is_gt,
             reads=[cx.d0.r, cx.cinfo.r], writes=[M.r])
    return M


def store_oT(cx, oTs, hrow0, m):
    P = cx.P
    o_ap, o_r = cx.dram["oT"]
    P.dma("pool", o_ap[hrow0:hrow0 + 256, m * 128:(m + 1) * 128].rearrange("(h d) t -> d h t", d=64),
          oTs[0:64, :, :], reads=[oTs.r], writes=[o_r])


def phase_att_sb(cx):
    P = cx.P
    mark = P.mark()
    kv = KV(cx)
    M = causal_masks(cx, True)
    NIU = Buf(P, [128, 128], BF16, "niu")
    P.ts("dve", NIU[:], cx.d0t[:], -0.5, -1.0, ALU.is_gt, ALU.mult, reads=[cx.d0t.r], writes=[NIU.r])
    slots = []
    for s in range(2):
        slots.append(dict(
            E=Buf(P, [128, 512], F32, "E"), SP=Buf(P, [128, 512], BF16, "SP"), ARG=Buf(P, [128, 512], F32, "ARG"),
            C=Buf(P, [128, 512], F32, "C"), PT=Buf(P, [128, 512], BF16, "PT"), OT=Buf(P, [64, 4, 128], BF16, "OT"),
            A=cx.ps[s], A2=cx.ps[2 + s], B=cx.ps[6], O=cx.ps[4 + s]))

    def stream(hg, m):
        def gen(s):
            sl = slots[s]
            (A, Ar), (A2, A2r), (B, Br), (O, Or) = sl["A"], sl["A2"], sl["B"], sl["O"]
            E, SP, ARG, C, PT, OT = sl["E"], sl["SP"], sl["ARG"], sl["C"], sl["PT"], sl["OT"]
            kbs = list(range(4 * m + 3, -1, -1))
            lvl = DBG.get("lvl", 99)
            for i, kb in enumerate(kbs):
                rank, ml, r = kb % 4, kb // 4, kb - 4 * m
                last = i == len(kbs) - 1
                if lvl < 1:
                    continue
                kv.qk(cx, A, Ar, rank, ml, m, stop=True)
                yield
                if lvl < 2:
                    continue
                P.act(E[:], A[:, :], AF.Exp, reads=[Ar], writes=[E.r])
                yield
                if lvl < 3:
                    continue
                P.act(SP[:], E[:], AF.Ln, reads=[E.r], writes=[SP.r], bias=1.0)
                if r >= 0:
                    P.tt(DBG.get("maskeng", "pool"), SP.ap(0, [[128, 4], [1, 128]]), SP.ap(0, [[128, 4], [1, 128]]),
                         M.ap(r * 128, [[0, 4], [1, 128]]), ALU.mult, reads=[SP.r, M.r], writes=[SP.r])
                yield
                if lvl < 4:
                    continue
                kv.qk(cx, A2, A2r, rank, ml, m, stop=False)
                P.mm(A2[:, :], NIU[:], SP[:], False, True, reads=[NIU.r, SP.r], writes=[A2r], acc=True)
                if not last:
                    P.mm(B[:, :], cx.ones16[:], SP[:], True, True, reads=[cx.ones16.r, SP.r], writes=[Br])
                if lvl < 5:
                    continue
                if i == 0:
                    P.copy("dve", ARG[:], A2[:, :], reads=[A2r], writes=[ARG.r])
                    if not last:
                        P.copy("dve", C[:], B[:, :], reads=[Br], writes=[C.r])
                else:
                    P.tt("dve", ARG[:], A2[:, :], C[:], ALU.subtract, reads=[A2r, C.r], writes=[ARG.r])
                    if not last:
                        P.tt("dve", C[:], C[:], B[:, :], ALU.add, reads=[C.r, Br], writes=[C.r])
                yield
                if lvl < 6:
                    continue
                P.act(PT[:], ARG[:], AF.Exp, reads=[ARG.r], writes=[PT.r])
                if r >= 0:
                    P.tt(DBG.get("maskeng", "pool"), PT.ap(0, [[128, 4], [1, 128]]), PT.ap(0, [[128, 4], [1, 128]]),
                         M.ap(r * 128, [[0, 4], [1, 128]]), ALU.mult, reads=[PT.r, M.r], writes=[PT.r])
                yield
                if lvl < 7:
                    continue
                kv.av(cx, O, Or, PT, rank, ml, i == 0, last, 64)
                yield
            if lvl >= 8:
                P.copy("act", OT.ap(0, [[1, 512]], 0, 64), O[0:64, :], reads=[Or], writes=[OT.r])
                store_oT(cx, OT, hg * 256, m)
            yield
        return gen

    for hg in range(DBG.get("hg", 4)):
        kv.load(cx, hg * 256, hg * 256, hg * 256)
        run_streams([stream(hg, m) for m in range(DBG.get("m", NT))])
    P.barrier()
    P.release(mark)


def phase_att_sb2(cx):
    P = cx.P
    mark = P.mark()
    kv = KV(cx)
    M = causal_masks(cx, True)
    NIU = Buf(P, [128, 128], BF16, "niu")
    P.ts("dve", NIU[:], cx.d0t[:], -0.5, -1.0, ALU.is_gt, ALU.mult, reads=[cx.d0t.r], writes=[NIU.r])
    NS = DBG.get("sbslots", 3)
    slots = []
    for s in range(NS):
        slots.append(dict(
            E=[Buf(P, [128, 512], BF16, "E") for _ in range(2)], SP=[Buf(P, [128, 512], BF16, "SP") for _ in range(2)],
            TMP=[Buf(P, [128, 512], F32, "TMP") for _ in range(2)], C=Buf(P, [128, 512], F32, "C"),
            X=[Buf(P, [128, 512], BF16, "X") for _ in range(2)], PT=[Buf(P, [128, 512], BF16, "PT") for _ in range(2)],
            OT=Buf(P, [64, 4, 128], BF16, "OT"), AN=cx.ps[s], O=cx.ps[3 + s]))
    B, Br = cx.ps[6]
    bc4 = [[128, 4], [1, 128]]

    def stream(hg, m):
        def gen(s):
            sl = slots[s]
            (AN, ANr), (O, Or) = sl["AN"], sl["O"]
            C, OT = sl["C"], sl["OT"]
            kbs = list(range(4 * m + 3, -1, -1))
            for i, kb in enumerate(kbs):
                E, SP, TMP, X, PT = (sl[k][i % 2] for k in ("E", "SP", "TMP", "X", "PT"))
                rank, ml, r = kb % 4, kb // 4, kb - 4 * m
                last = i == len(kbs) - 1
                kv.qk(cx, AN, ANr, rank, ml, m)
                yield
                P.act(E[:], AN[:, :], AF.Exp, reads=[ANr], writes=[E.r])
                yield
                P.act(SP[:], E[:], AF.Ln, reads=[E.r], writes=[SP.r], bias=1.0)
                if r >= 0:
                    P.tt("pool", SP.ap(0, bc4), SP.ap(0, bc4), M.ap(r * 128, [[0, 4], [1, 128]]), ALU.mult,
                         reads=[SP.r, M.r], writes=[SP.r])
                yield
                P.mm(AN[:, :], NIU[:], SP[:], True, True, reads=[NIU.r, SP.r], writes=[ANr])
                if not last:
                    P.mm(B[:, :], cx.ones16[:], SP[:], True, True, reads=[cx.ones16.r, SP.r], writes=[Br])
                if i == 0:
                    P.copy("dve", TMP[:], AN[:, :], reads=[ANr], writes=[TMP.r])
                    if not last:
                        P.copy("dve", C[:], B[:, :], reads=[Br], writes=[C.r])
                else:
                    P.tt("dve", TMP[:], AN[:, :], C[:], ALU.subtract, reads=[ANr, C.r], writes=[TMP.r])
                    if not last:
                        P.tt("dve", C[:], C[:], B[:, :], ALU.add, reads=[C.r, Br], writes=[C.r])
                yield
                P.act(X[:], TMP[:], AF.Exp, reads=[TMP.r], writes=[X.r])
                yield
                P.tt("dve", PT[:], E[:], X[:], ALU.mult, reads=[E.r, X.r], writes=[PT.r])
                if r >= 0:
                    P.tt("pool", PT.ap(0, bc4), PT.ap(0, bc4), M.ap(r * 128, [[0, 4], [1, 128]]), ALU.mult,
                         reads=[PT.r, M.r], writes=[PT.r])
                yield
                kv.av(cx, O, Or, PT, rank, ml, i == 0, last, 64)
                yield
            P.copy("act", OT.ap(0, [[1, 512]], 0, 64), O[0:64, :], reads=[Or], writes=[OT.r])
            store_oT(cx, OT, hg * 256, m)
            yield
        return gen

    for hg in range(DBG.get("hg", 4)):
        kv.load(cx, hg * 256, hg * 256, hg * 256)
        run_streams([stream(hg, m) for m in range(DBG.get("m", NT))], nslots=NS)
    P.barrier()
    P.release(mark)


def softmax_finish(cx, src, src_off, RD, OT, hrow0, m):
    P = cx.P
    Bc, Bcr = cx.ps[6]
    P.op("dve", lambda e: e.reciprocal(out=RD.ap(0, [[1, 512]], 64, 1), in_=src.ap(src_off, [[1, 512]], 64, 1)),
         reads=[src.r], writes=[RD.r])
    P.mm(Bc[0:64, :], cx.ones32[64:65, 0:64], RD.ap(0, [[1, 512]], 64, 1), True, True,
         reads=[cx.ones32.r, RD.r], writes=[Bcr])
    P.tt("dve", OT.ap(0, [[1, 512]], 0, 64), src.ap(src_off, [[1, 512]], 0, 64), Bc[0:64, :], ALU.mult,
         reads=[src.r, Bcr], writes=[OT.r])
    store_oT(cx, OT, hrow0, m)


def phase_att_dil(cx):
    P = cx.P
    mark = P.mark()
    kv = KV(cx)
    RLO = (-1, -4, -16)
    idx = {}
    for g in range(3):
        for r in range(RLO[g], 4):
            idx[(g, r)] = len(idx)
    MD = Buf(P, [128, len(idx), 128], BF16, "MD")
    modm = [None, Buf(P, [128, 128], F32, "modm1"), Buf(P, [128, 128], F32, "modm2")]
    t1 = Buf(P, [128, 128], F32, "t1")
    t2 = Buf(P, [128, 128], F32, "t2")
    ti = Buf(P, [128, 128], I32, "ti")
    for g in (1, 2):
        dil = DIL_CFG[g][1]
        P.ts("dve", t1[:], cx.d0[:], 128.0, 1.0 / dil, ALU.add, ALU.mult, reads=[cx.d0.r], writes=[t1.r])
        P.copy("dve", ti[:], t1[:], reads=[t1.r], writes=[ti.r])
        P.copy("dve", t1[:], ti[:], reads=[ti.r], writes=[t1.r])
        P.ts("dve", t1[:], t1[:], float(dil), None, ALU.mult, reads=[t1.r], writes=[t1.r])
        P.ts("dve", t2[:], cx.d0[:], 128.0, None, ALU.add, reads=[cx.d0.r], writes=[t2.r])
        P.tt("dve", modm[g][:], t1[:], t2[:], ALU.is_equal, reads=[t1.r, t2.r], writes=[modm[g].r])
    for (g, r), ix in idx.items():
        W = DIL_CFG[g][0]
        P.ts("dve", t1[:], cx.d0[:], cx.cinfo[:, 0:1], 128.0 * r - 0.5, ALU.add, ALU.is_gt,
             reads=[cx.d0.r, cx.cinfo.r], writes=[t1.r])
        P.ts("dve", t2[:], cx.d0[:], cx.cinfo[:, 0:1], 128.0 * r + W + 0.5, ALU.add, ALU.is_lt,
             reads=[cx.d0.r, cx.cinfo.r], writes=[t2.r])
        if g == 0:
            P.tt("dve", MD[:, ix, :], t1[:], t2[:], ALU.mult, reads=[t1.r, t2.r], writes=[MD.r])
        else:
            P.tt("dve", t1[:], t1[:], t2[:], ALU.mult, reads=[t1.r, t2.r], writes=[t1.r])
            P.tt("dve", MD[:, ix, :], t1[:], modm[g][:], ALU.mult, reads=[t1.r, modm[g].r], writes=[MD.r])
    ACC = Buf(P, [65, NT, 512], F32, "ACC")
    RD = Buf(P, [65, 512], F32, "RD")
    slots = [dict(PT=[Buf(P, [128, 512], BF16, "PT") for _ in range(2)], OT=Buf(P, [64, 4, 128], BF16, "OT"),
                  A=cx.ps[s], O=cx.ps[3 + s]) for s in range(3)]

    def stream(hg, g, m):
        def gen(s):
            sl = slots[s]
            (A, Ar), (O, Or) = sl["A"], sl["O"]
            OT = sl["OT"]
            kbs = [4 * m + r for r in range(RLO[g], 4) if 4 * m + r >= 0]
            for i, kb in enumerate(kbs):
                PT = sl["PT"][i % 2]
                rank, ml, r = kb % 4, kb // 4, kb - 4 * m
                kv.qk(cx, A, Ar, rank, ml, m)
                yield
                P.act(PT[:], A[:, :], AF.Exp, reads=[Ar], writes=[PT.r])
                P.tt("dve", PT.ap(0, [[128, 4], [1, 128]]), PT.ap(0, [[128, 4], [1, 128]]),
                     MD.ap(idx[(g, r)] * 128, [[0, 4], [1, 128]]), ALU.mult, reads=[PT.r, MD.r], writes=[PT.r])
                yield
                kv.av(cx, O, Or, PT, rank, ml, i == 0, i == len(kbs) - 1, 65)
                yield
            if g == 0:
                P.copy("act", ACC.ap(m * 512, [[1, 512]]), O[0:65, :], reads=[Or], writes=[ACC.r])
            else:
                P.tt("dve", ACC.ap(m * 512, [[1, 512]]), ACC.ap(m * 512, [[1, 512]]), O[0:65, :], ALU.add,
                     reads=[ACC.r, Or], writes=[ACC.r])
            if g == 2:
                softmax_finish(cx, ACC, m * 512, RD, OT, hg * 256, m)
            yield
        return gen

    for hg in range(DBG.get("hg", 2)):
        for g in range(3):
            row0 = (g * 8 + hg * 4) * 64
            kv.load(cx, row0, row0, row0)
            run_streams([stream(hg, g, m) for m in range(DBG.get("m", NT))], nslots=3)
    P.barrier()
    P.release(mark)


def phase_att_dsa(cx):
    P = cx.P
    ms_ap, ms_r = cx.dram["mscr"]
    NM = DBG.get("m", NT)
    mark = P.mark()
    q_ap, q_r = cx.dram["qT"]
    k_ap, k_r = cx.dram["kTg"]
    w_ap, w_r = cx.dram["wi"]
    Qiz = [Buf(P, [128, 4, 2048], BF16, f"Qiz{i}") for i in range(2)]
    Ki2 = Buf(P, [128, 4, 2048], BF16, "Ki2")
    WI = Buf(P, [128, NT, 8], F32, "WI")
    Rb = [Buf(P, [128, 512], F32, "Rb") for _ in range(2)]
    MTs = [Buf(P, [128, 8, 128], BF16, "MTs") for _ in range(2)]
    NEGM = Buf(P, [128, 4, 128], F32, "NEGM")
    s1 = [dict(SC=Buf(P, [128, 4, 2048], F32, "SC"), MK=Buf(P, [128, 4, 2048], BF16, "MK"), sm=Buf(P, [128, 16], F32, "bsm"))
          for _ in range(2)]
    qsrc = q_ap[1024:1536, :].rearrange("(c p) t -> p c t", p=128)
    for par in range(2):
        P.memset("pool", Qiz[par][:], 0.0, writes=[Qiz[par].r])
        P.dma("sp", Qiz[par][par * 64:(par + 1) * 64, :, :], qsrc[par * 64:(par + 1) * 64, :, :],
              reads=[q_r], writes=[Qiz[par].r])
        P.dma("sp", Ki2[par * 64:(par + 1) * 64, :, :], k_ap[:, 1024:1088, :].rearrange("r p t -> p r t"),
              reads=[k_r], writes=[Ki2.r])
    P.dma("sp", WI[:], w_ap[:, :].rearrange("(m p) h -> p m h", p=128), reads=[w_r], writes=[WI.r])
    for r in range(4):
        P.ts("dve", NEGM[:, r, :], cx.d0t[:], cx.cinfo[:, 0:1], 128.0 * r - 0.5, ALU.add, ALU.is_lt,
             reads=[cx.d0t.r, cx.cinfo.r], writes=[NEGM.r])
        P.ts("dve", NEGM[:, r, :], NEGM[:, r, :], NEG, None, ALU.mult, reads=[NEGM.r], writes=[NEGM.r])
    cnts = {"mm": 0, "tr": 0}

    def mask_stream(m):
        def gen(s):
            SC, MK, sm = s1[s]["SC"], s1[s]["MK"], s1[s]["sm"]
            L = (m + 1) * 128
            for rank in range(4):
                for c0 in range(0, L, 512):
                    n = min(512, L - c0)
                    for ih in range(8):
                        St, Sr = cx.ps[cnts["mm"] % 4]
                        rb = Rb[cnts["mm"] % 2]
                        cnts["mm"] += 1
                        P.mm(St[:, 0:n], Qiz[ih % 2][:, ih // 2, m * 128:(m + 1) * 128], Ki2[:, rank, c0:c0 + n], True, True,
                             reads=[Qiz[ih % 2].r, Ki2.r], writes=[Sr])
                        P.act(rb[:, 0:n], St[:, 0:n], AF.Relu, reads=[Sr], writes=[rb.r])
                        if ih == 0:
                            P.ts("dve", SC[:, rank, c0:c0 + n], rb[:, 0:n], WI[:, m, 0:1], None, ALU.mult,
                                 reads=[rb.r, WI.r], writes=[SC.r])
                        else:
                            P.stt("dve", SC[:, rank, c0:c0 + n], rb[:, 0:n], WI[:, m, ih:ih + 1], SC[:, rank, c0:c0 + n],
                                  ALU.mult, ALU.add, reads=[rb.r, WI.r, SC.r], writes=[SC.r])
                    yield
            scv = SC.ap(0, [[2048, 4], [1, L]])
            P.op("dve", lambda e: e.tensor_reduce(out=sm[:, 0:1], in_=scv, axis=AX.XY, op=ALU.max), reads=[SC.r], writes=[sm.r])
            P.op("dve", lambda e: e.tensor_reduce(out=sm[:, 1:2], in_=scv, axis=AX.XY, op=ALU.min), reads=[SC.r], writes=[sm.r])
            yield
            P.ts("dve", sm[:, 1:2], sm[:, 1:2], -1.0, None, ALU.mult, reads=[sm.r], writes=[sm.r])
            P.tt("dve", sm[:, 2:3], sm[:, 0:1], sm[:, 1:2], ALU.max, reads=[sm.r], writes=[sm.r])
            P.ts("dve", sm[:, 3:4], sm[:, 2:3], 1.0, None, ALU.add, reads=[sm.r], writes=[sm.r])
            P.ts("dve", sm[:, 4:5], sm[:, 3:4], -1.0, None, ALU.mult, reads=[sm.r], writes=[sm.r])
            for rank in range(4):
                P.tt("dve", SC[:, rank, m * 128:(m + 1) * 128], SC[:, rank, m * 128:(m + 1) * 128], NEGM[:, rank, :], ALU.add,
                     reads=[SC.r, NEGM.r], writes=[SC.r])
            yield
            mkv = MK.ap(0, [[2048, 4], [1, L]])
            hi, lo, mid, cnt, ge, d1, d2 = (sm[:, i:i + 1] for i in (3, 4, 5, 6, 7, 8, 9))
            for it in range(DBG.get("bis", 15)):
                P.ts("dve", mid, lo, hi, 0.5, ALU.add, ALU.mult, reads=[sm.r], writes=[sm.r])
                yield
                P.ts("dve", mkv, scv, mid, 0.0, ALU.is_ge, ALU.add, reads=[SC.r, sm.r], writes=[MK.r, sm.r], accum_out=cnt)
                yield
                P.ts("dve", ge, cnt, 255.5, None, ALU.is_gt, reads=[sm.r], writes=[sm.r])
                P.tt("dve", d1, mid, lo, ALU.subtract, reads=[sm.r], writes=[sm.r])
                P.tt("dve", d2, hi, mid, ALU.subtract, reads=[sm.r], writes=[sm.r])
                yield
                P.stt("dve", lo, d1, ge, lo, ALU.mult, ALU.add, reads=[sm.r], writes=[sm.r])
                P.stt("dve", hi, d2, ge, mid, ALU.mult, ALU.add, reads=[sm.r], writes=[sm.r])
                yield
            P.ts("dve", mkv, scv, lo, None, ALU.is_ge, reads=[SC.r, sm.r], writes=[MK.r])
            yield
            for rank in range(4):
                for b0 in range(0, m + 1, 8):
                    nb = min(8, m + 1 - b0)
                    pbt, pbr = cx.pb[0]
                    mts = MTs[cnts["tr"] % 2]
                    cnts["tr"] += 1
                    for j in range(nb):
                        P.tr(pbt[:, j * 128:(j + 1) * 128], MK[:, rank, (b0 + j) * 128:(b0 + j + 1) * 128], cx.ident[:],
                             reads=[MK.r, cx.ident.r], writes=[pbr], acc=(j > 0))
                    P.copy("act", mts.ap(0, [[1, nb * 128]]), pbt[:, 0:nb * 128], reads=[pbr], writes=[mts.r])
                    P.dma("pool", ms_ap[m, :, rank * 16 + b0:rank * 16 + b0 + nb, :], mts[:, 0:nb, :],
                          reads=[mts.r], writes=[ms_r])
                    yield
        return gen

    run_streams([mask_stream(m) for m in range(NM)])
    P.barrier()
    P.release(mark)
    mark = P.mark()
    kv = KV(cx)
    RD = Buf(P, [65, 512], F32, "RD")
    slots = [dict(PT=[Buf(P, [128, 512], BF16, "PT") for _ in range(2)], OT=Buf(P, [64, 4, 128], BF16, "OT"),
                  ON=Buf(P, [65, 512], F32, "ON"), MT=Buf(P, [128, 4, 2048], BF16, "MT"),
                  A=cx.ps[s], O=cx.ps[3 + s]) for s in range(3)]

    def stream(hg, m):
        def gen(s):
            sl = slots[s]
            (A, Ar), (O, Or) = sl["A"], sl["O"]
            OT, ON, MT = sl["OT"], sl["ON"], sl["MT"]
            L = (m + 1) * 128
            P.dma("sp", MT[:, :, 0:L], ms_ap[m, :, :, :].rearrange("s (r b) t -> s r (b t)", r=4)[:, :, 0:L],
                  reads=[ms_r], writes=[MT.r])
            steps = [(rank, ml) for ml in range(m + 1) for rank in range(4)]
            for i, (rank, ml) in enumerate(steps):
                PT = sl["PT"][i % 2]
                kv.qk(cx, A, Ar, rank, ml, m)
                yield
                P.act(PT[:], A[:, :], AF.Exp, reads=[Ar], writes=[PT.r])
                P.tt("dve", PT.ap(0, [[128, 4], [1, 128]]), PT.ap(0, [[128, 4], [1, 128]]),
                     MT.ap(rank * 2048 + ml * 128, [[0, 4], [1, 128]]), ALU.mult, reads=[PT.r, MT.r], writes=[PT.r])
                yield
                kv.av(cx, O, Or, PT, rank, ml, i == 0, i == len(steps) - 1, 65)
                yield
            P.copy("act", ON[:], O[0:65, :], reads=[Or], writes=[ON.r])
            softmax_finish(cx, ON, 0, RD, OT, hg * 256, m)
            yield
        return gen

    for hg in range(DBG.get("hg", 4)):
        kv.load(cx, hg * 256, hg * 256, hg * 256)
        run_streams([stream(hg, m) for m in range(NM)], nslots=3)
    P.barrier()
    P.release(mark)


def phase_att_moba(cx):
    P = cx.P
    mark = P.mark()
    NM = DBG.get("m", NT)
    kv = KV(cx)
    Mle = causal_masks(cx, False)
    o_ap, o_r = cx.dram["oT"]
    kmT = Buf(P, [128, 2, 32], BF16, "kmT")
    KS = Buf(P, [128, 4, 16], F32, "KS")
    KM = Buf(P, [128, 32], F32, "KM")
    iotaI = Buf(P, [128, 32], I32, "iotaI")
    iotaN = Buf(P, [128, 32], F32, "iotaN")
    P.op("pool", lambda e: e.iota(iotaI[:], pattern=[[1, 32]], base=0, channel_multiplier=0), writes=[iotaI.r])
    P.copy("dve", iotaN[:], iotaI[:], reads=[iotaI.r], writes=[iotaN.r])
    chalf = cx.cinfo[:, 1:2]
    slots = []
    for s in range(3):
        slots.append(dict(
            PT=[Buf(P, [128, 512], BF16, "PT") for _ in range(2)], VAL=Buf(P, [128, 32], F32, "VAL"),
            EQ=Buf(P, [128, 32], F32, "EQ"), GN=Buf(P, [128, 32], F32, "GN"), GS=Buf(P, [128, 4, 32], F32, "GS"),
            M8=Buf(P, [128, 32], F32, "M8"), SEL=Buf(P, [128, 4, 32], F32, "SEL"), TMP=Buf(P, [128, 4, 65], F32, "TMP"),
            ACC=Buf(P, [128, 4, 65], F32, "ACC"), RC=Buf(P, [128, 4], F32, "RC"), OK=Buf(P, [128, 256], BF16, "OK"),
            OTt=Buf(P, [128, 2, 128], BF16, "OTt"), A=cx.ps[s], ON=cx.ps[3 + s], G=cx.ps[6]))

    def stream(hg, m):
        def gen(s):
            sl = slots[s]
            (A, Ar), (ON, ONr), (G, Gr) = sl["A"], sl["ON"], sl["G"]
            PTs, VAL, EQ, GN, GS, M8, SEL, TMP, ACC, RC, OK, OTt = (sl[k] for k in (
                "PT", "VAL", "EQ", "GN", "GS", "M8", "SEL", "TMP", "ACC", "RC", "OK", "OTt"))
            for h in range(4):
                qz = kv.Qz[h % 2]
                P.mm(G[:, h * 32:(h + 1) * 32], qz[:, h // 2, m * 128:(m + 1) * 128], kmT[:, h // 2, :], h == 0, h == 3,
                     reads=[qz.r, kmT.r], writes=[Gr], acc=(h > 0))
            P.ts("dve", VAL[:], iotaN[:], -2.0 * m, chalf, ALU.add, ALU.is_lt, reads=[iotaN.r, cx.cinfo.r], writes=[VAL.r])
            P.ts("dve", EQ[:], iotaN[:], -2.0 * m, chalf, ALU.add, ALU.is_equal, reads=[iotaN.r, cx.cinfo.r], writes=[EQ.r])
            P.ts("dve", GN[:], VAL[:], -NEG, NEG, ALU.mult, ALU.add, reads=[VAL.r], writes=[GN.r])
            P.tt("dve", GS.ap(0, [[32, 4], [1, 32]]), G[:, 0:128].rearrange("p (h n) -> p h n", h=4),
                 GN.ap(0, [[0, 4], [1, 32]]), ALU.add, reads=[Gr, GN.r], writes=[GS.r])
            for h in range(4):
                P.op("dve", lambda e, h=h: e.max(out=M8[:, h * 8:(h + 1) * 8], in_=GS[:, h, :]), reads=[GS.r], writes=[M8.r])
            for h in range(4):
                P.ts("dve", SEL[:, h, :], GS[:, h, :], M8[:, h * 8 + 2:h * 8 + 3], None, ALU.is_ge,
                     reads=[GS.r, M8.r], writes=[SEL.r])
            P.tt("dve", SEL.ap(0, [[32, 4], [1, 32]]), SEL.ap(0, [[32, 4], [1, 32]]), VAL.ap(0, [[0, 4], [1, 32]]), ALU.mult,
                 reads=[SEL.r, VAL.r], writes=[SEL.r])
            P.tt("dve", SEL.ap(0, [[32, 4], [1, 32]]), SEL.ap(0, [[32, 4], [1, 32]]), EQ.ap(0, [[0, 4], [1, 32]]), ALU.add,
                 reads=[SEL.r, EQ.r], writes=[SEL.r])
            yield
            nblk = 2 * m + 2
            for n in range(nblk):
                for kbi in range(2):
                    kb = 2 * n + kbi
                    rank, ml, r = kb % 4, kb // 4, kb - 4 * m
                    PT = PTs[kbi]
                    kv.qk(cx, A, Ar, rank, ml, m)
                    yield
                    P.act(PT[:], A[:, :], AF.Exp, reads=[Ar], writes=[PT.r])
                    if r >= 0:
                        P.tt("dve", PT.ap(0, [[128, 4], [1, 128]]), PT.ap(0, [[128, 4], [1, 128]]),
                             Mle.ap(r * 128, [[0, 4], [1, 128]]), ALU.mult, reads=[PT.r, Mle.r], writes=[PT.r])
                    yield
                    for h in range(4):
                        P.mm(ON[:, h * 65:h * 65 + 64], PT[:, h * 128:(h + 1) * 128], kv.Vs[:, rank, ml, h * 64:(h + 1) * 64],
                             kbi == 0 and h == 0, False, reads=[PT.r, kv.Vs.r], writes=[ONr], acc=(kbi > 0 or h > 0))
                        P.mm(ON[:, h * 65 + 64:h * 65 + 65], PT[:, h * 128:(h + 1) * 128], cx.ones16[:, 0:1],
                             False, kbi == 1 and h == 3, reads=[PT.r, cx.ones16.r], writes=[ONr], acc=True)
                    yield
                onv = ON[:, 0:260].rearrange("p (h d) -> p h d", h=4)
                selb = SEL.ap(n, [[32, 4], [0, 65]])
                if n == 0:
                    P.tt("dve", ACC.ap(0, [[65, 4], [1, 65]]), onv, selb, ALU.mult, reads=[ONr, SEL.r], writes=[ACC.r])
                else:
                    P.tt("dve", TMP.ap(0, [[65, 4], [1, 65]]), onv, selb, ALU.mult, reads=[ONr, SEL.r], writes=[TMP.r])
                    P.tt("pool", ACC[:], ACC[:], TMP[:], ALU.add, reads=[ACC.r, TMP.r], writes=[ACC.r])
            P.op("dve", lambda e: e.reciprocal(out=RC[:], in_=ACC.ap(64, [[65, 4]])), reads=[ACC.r], writes=[RC.r])
            P.tt("dve", OK.ap(0, [[64, 4], [1, 64]]), ACC.ap(0, [[65, 4], [1, 64]]), RC.ap(0, [[1, 4], [0, 64]]), ALU.mult,
                 reads=[ACC.r, RC.r], writes=[OK.r])
            transpose_rows(cx, OTt, OK, 2, 0, [OK.r])
            P.dma("pool", o_ap[hg * 256:(hg + 1) * 256, m * 128:(m + 1) * 128].rearrange("(c p) t -> p c t", p=128),
                  OTt[:], reads=[OTt.r], writes=[o_r])
            yield
        return gen

    for hg in range(DBG.get("hg", 4)):
        kv.load(cx, hg * 256, hg * 256, hg * 256)
        for ch in range(2):
            ksrc = kv.Ks.ap(ch * 8192, [[128, 64], [1, 128]])
            P.op("dve", lambda e, ksrc=ksrc: e.tensor_reduce(out=KS.ap(0, [[1, 64]]), in_=ksrc, axis=AX.X, op=ALU.add),
                 reads=[kv.Ks.r], writes=[KS.r])
            P.tt("dve", KM.ap(0, [[2, 16]]), KS[:, 0, :], KS[:, 1, :], ALU.add, reads=[KS.r], writes=[KM.r])
            P.tt("dve", KM.ap(1, [[2, 16]]), KS[:, 2, :], KS[:, 3, :], ALU.add, reads=[KS.r], writes=[KM.r])
            P.ts("dve", kmT[:, ch, :], KM[:], 1.0 / 256, None, ALU.mult, reads=[KM.r], writes=[kmT.r])
        run_streams([stream(hg, m) for m in range(NM)], nslots=3)
    P.barrier()
    P.release(mark)


ATT_FUNCS = {}
DBG = {}


def build_stage(k):
    nc = bass.Bass("TRN2", target_bir_lowering=False)
    stack = contextlib.ExitStack()
    cx = Ctx(nc, stack)
    cx.din("cinfo", [128, 4], F32)
    li = k if k < 4 else None
    pl = k - 1 if k >= 1 else None
    if pl is not None:
        Lp = LAYERS[pl]
        cx.din("qT", [Lp["nq"] * 64, TOK], BF16)
        cx.din("kTg", [4, Lp["nk"] * 64, TOK], BF16)
        cx.din("vg", [4, TOK, Lp["nv"]], BF16)
        if Lp["name"] == "dsa":
            cx.din("wi", [TOK, 8], F32)
        cx.din("h_res", [TOK, D], F32)
        cx.din("w_out", [Lp["no"], D], F32)
        cx.din("w_ff_in", [D, 4096], F32)
        cx.din("w_ff_out", [4096, D], F32)
        for g in ("g_mix_post", "g_ffn_pre", "g_ffn_post", "g_ple"):
            cx.din(g, [1, D], F32)
        cx.din("w_ple_gate", [D, D], F32)
        cx.din("w_ple", [256, D], F32)
        cx.din("p", [TOK, 256], F32)
        if Lp["name"] == "dsa":
            cx.dint("mscr", [NT, 128, 64, 128], BF16)
        cx.dint("oT", [Lp["no"], TOK], BF16)
        cx.dint("h_mid", [TOK, D], F32)
        cx.dint("h_a", [TOK, D], F32)
        cx.dram["h_in"] = cx.dram["h_mid"]
        cx.dout("h_out", [TOK, D], F32)
    else:
        cx.din("h_in", [TOK, D], F32)
    if li is not None:
        L = LAYERS[li]
        cx.din("w_in", [D, (L["nq"] + L["nk"]) * 64 + L["nv"]], F32)
        cx.din("g_mix_pre", [1, D], F32)
        cx.dout("qT_o", [L["nq"] * 64, TOK], BF16)
        cx.dout("kT_o", [L["nk"] * 64, TOK], BF16)
        cx.dout("tv_o", [TOK, L["nv"]], BF16)
        if L["name"] == "dsa":
            cx.dout("wi_o", [TOK, 8], F32)
        if L["rope"]:
            cx.din("pos", [128, NT], I32)
            cx.din("invf", [128, 8], F32)
    cx.consts()
    if pl is not None:
        if not DBG.get("noatt"):
            ATT_FUNCS[LAYERS[pl]["name"]](cx)
        if not DBG.get("nopost"):
            (phase_post if DBG.get("oldpost") else phase_post2)(cx, pl)
    if not DBG.get("nopre"):
        (phase_pre if DBG.get("oldpre") else phase_pre2)(cx, li, pl)
    cx.P.barrier()
    cx.P.emit()
    stack.close()
    return nc, cx


def _rows(a, c):
    F_ = a.shape[-1]
    return np.ascontiguousarray(a.reshape(16, 4, 128, F_)[:, c].reshape(TOK, F_))


def _w_in_perm(inp, li):
    if li == 0:
        return inp["w_in_sb"][0]
    if li == 3:
        return inp["w_in_moba"][0]
    if li == 1:
        W = inp["w_in_dil"][0].reshape(D, 3, 3, 512)
        return np.ascontiguousarray(np.concatenate(
            [W[:, g, 0] for g in range(3)] + [W[:, g, 1] for g in range(3)] + [W[:, g, 2] for g in range(3)], axis=1))
    W = inp["w_in_dsa"][0]
    q, kk, v, qi, ki, wi = W[:, 0:1024], W[:, 1024:2048], W[:, 2048:3072], W[:, 3072:3584], W[:, 3584:3648], W[:, 3648:3656]
    return np.ascontiguousarray(np.concatenate([q, qi, kk, ki, np.zeros((D, 64), np.float32), v, wi], axis=1))


_W_OUT = ("w_out_sb", "w_out_dil", "w_out_dsa", "w_out_moba")
_PROGS = {}


def _get_prog(k):
    if k not in _PROGS:
        _PROGS[k] = build_stage(k)[0]
    return _PROGS[k]


def run_stage(k, inp, state):
    nc = _get_prog(k)
    li = k if k < 4 else None
    pl = k - 1 if k >= 1 else None
    invf = np.tile((500000.0 ** (-np.arange(0, 16, 2, dtype=np.float32) / 16)).astype(np.float32)[None, :], (128, 1))
    in_maps = []
    for core in range(8):
        b, c = core // 4, core % 4
        st = state[core]
        d = {"cinfo": np.tile(np.array([[128.0 * c, float(c // 2), 0.0, 0.0]], np.float32), (128, 1))}
        if pl is not None:
            grp = [state[b * 4 + cc] for cc in range(4)]
            d["qT"] = st["qT_o"]
            d["kTg"] = np.ascontiguousarray(np.stack([g["kT_o"] for g in grp], axis=0))
            d["vg"] = np.ascontiguousarray(np.stack([g["tv_o"] for g in grp], axis=0))
            if LAYERS[pl]["name"] == "dsa":
                d["wi"] = st["wi_o"]
            d["h_res"] = st["h"]
            d["w_out"] = inp[_W_OUT[pl]][0]
            d["w_ff_in"] = inp["w_ff_in"][pl]
            d["w_ff_out"] = inp["w_ff_out"][pl]
            for g in ("g_mix_post", "g_ffn_pre", "g_ffn_post", "g_ple"):
                d[g] = inp[g][pl][None, :]
            d["w_ple_gate"] = inp["w_ple_gate"][pl]
            d["w_ple"] = inp["w_ple"][pl]
            d["p"] = _rows(inp["p"][pl, b], c)
        else:
            d["h_in"] = st["h"]
        if li is not None:
            d["w_in"] = _w_in_perm(inp, li)
            d["g_mix_pre"] = inp["g_mix_pre"][li][None, :]
            if LAYERS[li]["rope"]:
                pos = _rows(inp["positions"][b][:, None].astype(np.int32), c)[:, 0]
                d["pos"] = np.ascontiguousarray(pos.reshape(NT, 128).T)
                d["invf"] = invf
        in_maps.append({kk: np.ascontiguousarray(v) for kk, v in d.items()})
    res = run_bass_kernel_spmd(nc, in_maps, core_ids=list(range(8)))
    new = []
    for core in range(8):
        r = res.results[core]
        st = dict(state[core])
        if pl is not None:
            st["h"] = np.asarray(r["h_out"])
        for nm in ("qT_o", "kT_o", "tv_o", "wi_o"):
            if nm in r:
                st[nm] = np.asarray(r[nm])
        new.append(st)
    return new


def kernel(**inp):
    inp = {k: np.asarray(v) for k, v in inp.items()}
    state = []
    for core in range(8):
        b, c = core // 4, core % 4
        state.append({"h": _rows(inp["x"][b].astype(np.float32), c)})
    for k in range(5):
        state = run_stage(k, inp, state)
    out = np.empty((2, 8192, D), np.float32)
    for core in range(8):
        b, c = core // 4, core % 4
        out[b].reshape(16, 4, 128, D)[:, c] = state[core]["h"].reshape(16, 128, D)
    return out


ATT_FUNCS["sb"] = phase_att_sb2
ATT_FUNCS["dil"] = phase_att_dil
ATT_FUNCS["dsa"] = phase_att_dsa
ATT_FUNCS["moba"] = phase_att_moba


_GAINS = ("g_mix_pre", "g_mix_post", "g_ffn_pre", "g_ffn_post", "g_ple")
_LW = ("w_ff_in", "w_ff_out", "w_ple_gate", "w_ple")


def build_fused():
    nc = bass.Bass("TRN2", target_bir_lowering=False)
    stack = contextlib.ExitStack()
    cx = Ctx(nc, stack)
    P = cx.P
    cx.din("cinfo", [128, 4], F32)
    cx.din("pos", [128, NT], I32)
    cx.din("invf", [128, 8], F32)
    cx.din("x", [TOK, D], F32)
    for i, L in enumerate(LAYERS):
        cx.din(f"w_in_{i}", [D, (L["nq"] + L["nk"]) * 64 + L["nv"]], F32)
        cx.din(f"w_out_{i}", [L["no"], D], F32)
        cx.din(f"w_ff_in_{i}", [D, 4096], F32)
        cx.din(f"w_ff_out_{i}", [4096, D], F32)
        cx.din(f"w_ple_gate_{i}", [D, D], F32)
        cx.din(f"w_ple_{i}", [256, D], F32)
        cx.din(f"p_{i}", [TOK, 256], F32)
        for g in _GAINS:
            cx.din(f"{g}_{i}", [1, D], F32)
        cx.dint(f"qT_{i}", [L["nq"] * 64, TOK], BF16)
        cx.dint(f"kT_{i}", [L["nk"] * 64, TOK], BF16)
        cx.dint(f"tv_{i}", [TOK, L["nv"]], BF16)
        cx.dint(f"kTg_{i}", [4 * L["nk"] * 64, TOK], BF16)
        cx.dint(f"vg_{i}", [4 * TOK, L["nv"]], BF16)
        cx.dint(f"oT_{i}", [L["no"], TOK], BF16)
        cx.dint(f"hmid_{i}", [TOK, D], F32)
        if i < 3:
            cx.dint(f"h_{i}", [TOK, D], F32)
    cx.dint("wi_2", [TOK, 8], F32)
    cx.dint("mscr", [NT, 128, 64, 128], BF16)
    cx.dout("h_out", [TOK, D], F32)
    cx.consts()

    def alias(**kw):
        for k, v in kw.items():
            cx.dram[k] = cx.dram[v]

    def pre_alias(i):
        alias(w_in=f"w_in_{i}", g_mix_pre=f"g_mix_pre_{i}", qT_o=f"qT_{i}", kT_o=f"kT_{i}", tv_o=f"tv_{i}", wi_o="wi_2")

    alias(h_in="x")
    pre_alias(0)
    phase_pre(cx, 0, None)
    groups = [[0, 1, 2, 3], [4, 5, 6, 7]]
    for i, L in enumerate(LAYERS):
        for src, dst in ((f"kT_{i}", f"kTg_{i}"), (f"tv_{i}", f"vg_{i}")):
            (s_ap, s_r), (d_ap, d_r) = cx.dram[src], cx.dram[dst]
            P.op("pool", lambda e, s_ap=s_ap, d_ap=d_ap: e.collective_compute(
                "AllGather", ALU.bypass, replica_groups=groups, ins=[s_ap], outs=[d_ap]),
                reads=[s_r], writes=[d_r], dma=True)
        kg_ap, kg_r = cx.dram[f"kTg_{i}"]
        vg_ap, vg_r = cx.dram[f"vg_{i}"]
        cx.dram["kTg"] = (kg_ap.rearrange("(r f) t -> r f t", r=4), kg_r)
        cx.dram["vg"] = (vg_ap.rearrange("(r t) v -> r t v", r=4), vg_r)
        alias(qT=f"qT_{i}", wi="wi_2", oT=f"oT_{i}")
        ATT_FUNCS[L["name"]](cx)
        alias(h_res=("x" if i == 0 else f"h_{i - 1}"), h_mid=f"hmid_{i}", w_out=f"w_out_{i}", w_ff_in=f"w_ff_in_{i}",
              w_ff_out=f"w_ff_out_{i}", g_mix_post=f"g_mix_post_{i}", g_ffn_pre=f"g_ffn_pre_{i}",
              g_ffn_post=f"g_ffn_post_{i}")
        phase_post(cx, i)
        alias(h_in=f"hmid_{i}", h_out=(f"h_{i}" if i < 3 else "h_out"), p=f"p_{i}", w_ple_gate=f"w_ple_gate_{i}",
              w_ple=f"w_ple_{i}", g_ple=f"g_ple_{i}")
        if i < 3:
            pre_alias(i + 1)
        phase_pre(cx, i + 1 if i < 3 else None, i)
    P.barrier()
    P.emit()
    stack.close()
    return nc, cx


def kernel_fused(**inp):
    inp = {k: np.asarray(v) for k, v in inp.items()}
    if "fused" not in _PROGS:
        _PROGS["fused"] = build_fused()[0]
    nc = _PROGS["fused"]
    invf = np.tile((500000.0 ** (-np.arange(0, 16, 2, dtype=np.float32) / 16)).astype(np.float32)[None, :], (128, 1))
    shared = {"invf": invf}
    for i in range(4):
        shared[f"w_in_{i}"] = _w_in_perm(inp, i)
        shared[f"w_out_{i}"] = inp[_W_OUT[i]][0]
        for w in _LW:
            shared[f"{w}_{i}"] = inp[w][i]
        for g in _GAINS:
            shared[f"{g}_{i}"] = inp[g][i][None, :]
    shared = {k: np.ascontiguousarray(v) for k, v in shared.items()}
    in_maps = []
    for core in range(8):
        b, c = core // 4, core % 4
        d = dict(shared)
        d["cinfo"] = np.tile(np.array([[128.0 * c, float(c // 2), 0.0, 0.0]], np.float32), (128, 1))
        pos = _rows(inp["positions"][b][:, None].astype(np.int32), c)[:, 0]
        d["pos"] = np.ascontiguousarray(pos.reshape(NT, 128).T)
        d["x"] = _rows(inp["x"][b].astype(np.float32), c)
        for i in range(4):
            d[f"p_{i}"] = _rows(inp["p"][i, b], c)
        in_maps.append(d)
    res = run_bass_kernel_spmd(nc, in_maps, core_ids=list(range(8)))
    out = np.empty((2, 8192, D), np.float32)
    for core in range(8):
        b, c = core // 4, core % 4
        out[b].reshape(16, 4, 128, D)[:, c] = np.asarray(res.results[core]["h_out"]).reshape(16, 128, D)
    return out
```

```python
import contextlib
import math
import numpy as np
import ml_dtypes
import concourse.bass as bass
import concourse.mybir as mybir
from concourse.bass_utils import run_bass_kernel_spmd

F32 = mybir.dt.float32
BF16 = mybir.dt.bfloat16
I32 = mybir.dt.int32
AF = mybir.ActivationFunctionType
ALU = mybir.AluOpType
AX = mybir.AxisListType

SEM_LIMIT = 30000
DMA_POOL = 12
NEG = -1.0e30
EPS = 1e-6
D = 1024
TOK = 2048
NT = 16

LAYERS = [
    dict(name="sb", nq=16, nk=16, nv=1024, rope=False, qs=16, no=1024),
    dict(name="dil", nq=24, nk=24, nv=1536, rope=True, qs=24, no=512),
    dict(name="dsa", nq=24, nk=18, nv=1032, rope=True, qs=16, no=1024),
    dict(name="moba", nq=16, nk=16, nv=1024, rope=True, qs=16, no=1024),
]
DIL_CFG = ((128, 1), (512, 4), (2048, 16))


class Res:
    __slots__ = ("name", "lastw", "readers")

    def __init__(self, name):
        self.name = name
        self.lastw = None
        self.readers = []


class Op:
    __slots__ = ("eng", "fn", "deps", "signal", "sem", "tick", "dma", "idx", "slot_prev")

    def __init__(self, eng, fn, dma):
        self.eng = eng
        self.fn = fn
        self.deps = []
        self.signal = False
        self.sem = None
        self.tick = 0
        self.dma = dma
        self.slot_prev = None


class Prog:
    ENGS = ("pe", "act", "dve", "pool", "sp")

    def __init__(self, nc, stack):
        self.nc = nc
        self.stack = stack
        self.ops = []
        self.by_eng = {e: [] for e in self.ENGS}
        self.nres = 0
        self.sb_off = 0
        self.ntens = 0
        self.dma_since = []

    ARENA = 204800

    def init_arena(self):
        a = self.stack.enter_context(self.nc.sbuf_tensor("arena", [128, self.ARENA // 2], BF16))
        self.h16 = a
        self.h32 = a.bitcast(F32)
        self.hi32 = a.bitcast(I32)

    def alloc_bytes(self, n):
        n = (n + 31) // 32 * 32
        off = self.sb_off
        self.sb_off += n
        assert self.sb_off <= self.ARENA, f"SBUF overflow {self.sb_off}"
        return off

    def mark(self):
        return self.sb_off

    def release(self, mark):
        self.sb_off = mark

    def res(self, name=None):
        self.nres += 1
        return Res(name or f"r{self.nres}")

    def op(self, eng, fn, reads=(), writes=(), dma=False, acc=False):
        o = Op(eng, fn, dma)
        o.idx = len(self.ops)
        deps = set()
        for r in reads:
            if r.lastw is not None:
                deps.add(r.lastw)
        for w in writes:
            if w.lastw is not None:
                lw = self.ops[w.lastw]
                if not (acc and lw.eng == "pe" and eng == "pe" and not lw.dma):
                    deps.add(w.lastw)
            for rd in w.readers:
                deps.add(rd)
        deps.discard(o.idx)
        o.deps = sorted(deps)
        best = {}
        keep = []
        for d in deps:
            po = self.ops[d]
            if po.dma or po.fn is None:
                keep.append(d)
            elif po.eng not in best or best[po.eng] < d:
                best[po.eng] = d
        o.deps = sorted(keep + list(best.values()))
        for r in reads:
            if not dma:
                r.readers = [x for x in r.readers if self.ops[x].dma or self.ops[x].eng != eng]
            r.readers.append(o.idx)
        for w in writes:
            w.lastw = o.idx
            w.readers = []
        self.ops.append(o)
        self.by_eng[eng].append(o)
        if dma:
            self.dma_since.append(o.idx)
        return o

    def wait_only(self, eng, deps):
        o = Op(eng, None, False)
        o.idx = len(self.ops)
        o.deps = sorted(set(deps))
        self.ops.append(o)
        self.by_eng[eng].append(o)
        return o

    def barrier(self):
        deps = list(self.dma_since)
        for e in self.ENGS:
            for o in reversed(self.by_eng[e]):
                if o.fn is not None and not o.dma:
                    deps.append(o.idx)
                    break
        for e in self.ENGS:
            self.wait_only(e, deps)
        self.dma_since = []

    def dma(self, eng, out, in_, reads=(), writes=()):
        return self.op(eng, lambda e: e.dma_start(out=out, in_=in_), reads, writes, dma=True)

    def mm(self, out, lhsT, rhs, start, stop, reads=(), writes=(), acc=False):
        return self.op("pe", lambda e: e.matmul(out=out, lhsT=lhsT, rhs=rhs, start=start, stop=stop),
                       reads, writes, acc=acc)

    def tr(self, out, in_, ident, reads=(), writes=(), acc=False):
        return self.op("pe", lambda e: e.transpose(out=out, in_=in_, identity=ident), reads, writes, acc=acc)

    def act(self, out, in_, func, reads=(), writes=(), scale=1.0, bias=0.0, accum_out=None):
        def fn(e):
            kw = {}
            if accum_out is not None:
                kw["accum_out"] = accum_out
            return e.activation(out=out, in_=in_, func=func, bias=bias, scale=scale, **kw)
        return self.op("act", fn, reads, writes)

    def tt(self, eng, out, in0, in1, op, reads=(), writes=()):
        return self.op(eng, lambda e: e.tensor_tensor(out=out, in0=in0, in1=in1, op=op), reads, writes)

    def ts(self, eng, out, in0, s1, s2, op0, op1=None, reads=(), writes=(), accum_out=None):
        def fn(e):
            kw = {}
            if accum_out is not None:
                kw["accum_out"] = accum_out
            if op1 is None:
                return e.tensor_scalar(out=out, in0=in0, scalar1=s1, scalar2=None, op0=op0, **kw)
            return e.tensor_scalar(out=out, in0=in0, scalar1=s1, scalar2=s2, op0=op0, op1=op1, **kw)
        return self.op(eng, fn, reads, writes)

    def stt(self, eng, out, in0, scalar, in1, op0, op1, reads=(), writes=()):
        return self.op(eng, lambda e: e.scalar_tensor_tensor(out=out, in0=in0, scalar=scalar, in1=in1,
                                                             op0=op0, op1=op1), reads, writes)

    def copy(self, eng, out, in_, reads=(), writes=()):
        if eng == "act":
            return self.act(out, in_, AF.Copy, reads, writes)
        return self.op(eng, lambda e: e.tensor_copy(out=out, in_=in_), reads, writes)

    def memset(self, eng, ap, val, writes=()):
        return self.op(eng, lambda e: e.memset(ap, val), (), writes)

    def emit(self):
        nc = self.nc
        ops = self.ops
        for o in ops:
            for d in o.deps:
                ops[d].signal = True
        sems = []

        def new_sem(nm):
            s = self.stack.enter_context(nc.semaphore(nm))
            sems.append(s)
            return s

        for e in self.ENGS:
            cur = None
            cnt = 0
            slots = []
            nslot = 0
            for o in self.by_eng[e]:
                if o.fn is None:
                    continue
                if o.dma:
                    o.signal = True
                    si = nslot % DMA_POOL
                    if si >= len(slots):
                        slots.append([new_sem(f"d_{e}_{si}"), 0, None])
                    s = slots[si]
                    nslot += 1
                    o.slot_prev = s[2]
                    s[1] += 16
                    if s[1] > SEM_LIMIT:
                        s[0] = new_sem(f"d_{e}_x{o.idx}")
                        s[1] = 16
                    o.sem, o.tick = s[0], s[1]
                    s[2] = o
                elif o.signal:
                    if cur is None or cnt >= SEM_LIMIT:
                        cur = new_sem(f"c_{e}_{o.idx}")
                        cnt = 0
                    cnt += 1
                    o.sem, o.tick = cur, cnt
        self.n_sems = len(sems)

        def emit_engine(ename, eobj):
            seen = {}
            for o in self.by_eng[ename]:
                need = {}
                plist = [ops[d] for d in o.deps]
                if o.dma and o.slot_prev is not None:
                    plist.append(o.slot_prev)
                for p in plist:
                    if p.sem is None:
                        continue
                    k = id(p.sem)
                    if seen.get(k, 0) >= p.tick:
                        continue
                    if k not in need or need[k][1] < p.tick:
                        need[k] = (p.sem, p.tick)
                for k, (s, v) in need.items():
                    eobj.wait_ge(s, v)
                    seen[k] = v
                if o.fn is None:
                    continue
                ins = o.fn(eobj)
                if o.signal:
                    ins.then_inc(o.sem, 16 if o.dma else 1)

        with nc.Block() as block:
            @block.sync
            def _(e):
                emit_engine("sp", e)

            @block.tensor
            def _(e):
                emit_engine("pe", e)

            @block.scalar
            def _(e):
                emit_engine("act", e)

            @block.vector
            def _(e):
                emit_engine("dve", e)

            @block.gpsimd
            def _(e):
                emit_engine("pool", e)


class Buf:
    def __init__(self, P, shape, dtype, name=None):
        self.shape = list(shape)
        self.dtype = dtype
        self.row = int(np.prod(shape[1:]))
        esz = 2 if dtype == BF16 else 4
        off = P.alloc_bytes(self.row * esz)
        self.h = P.h16 if dtype == BF16 else (P.h32 if dtype == F32 else P.hi32)
        self.base = off // esz
        self.ps = P.ARENA // esz
        self.r = P.res(name)
        pat = [[self.ps, shape[0]]]
        st = self.row
        for d in shape[1:]:
            st //= d
            pat.append([st, d])
        self.t = bass.AP(self.h, self.base, pat)

    def __getitem__(self, k):
        return self.t[k]

    def ap(self, off, free, p0=0, npart=None):
        if npart is None:
            npart = self.shape[0] - p0
        return bass.AP(self.h, self.base + p0 * self.ps + off, [[self.ps, npart]] + [list(x) for x in free])


class Ctx:
    def __init__(self, nc, stack):
        self.nc = nc
        self.P = Prog(nc, stack)
        P = self.P
        P.init_arena()
        self.ps = []
        for i in range(7):
            t = stack.enter_context(nc.psum_tensor(f"psf{i}", [128, 512], F32))
            self.ps.append((t, P.res(f"psf{i}")))
        self.pb = []
        t = stack.enter_context(nc.psum_tensor("psb0", [128, 1024], BF16))
        r = P.res("psb0")
        self.pb = [(t, r), (t, r)]
        self.dram = {}
        self.outs = []

    def din(self, name, shape, dtype):
        t = self.nc.dram_tensor(name, list(shape), dtype, kind="ExternalInput").ap()
        self.dram[name] = (t, self.P.res(name))
        return t

    def dout(self, name, shape, dtype):
        t = self.nc.dram_tensor(name, list(shape), dtype, kind="ExternalOutput").ap()
        self.dram[name] = (t, self.P.res(name))
        self.outs.append(name)
        return t

    def dint(self, name, shape, dtype):
        t = self.nc.dram_tensor(name, list(shape), dtype, kind="Internal").ap()
        self.dram[name] = (t, self.P.res(name))
        return t

    def consts(self):
        P = self.P
        self.ident = Buf(P, [128, 128], BF16, "ident")
        self.ones16 = Buf(P, [128, 128], BF16, "ones16")
        self.ones32 = Buf(P, [128, 64], F32, "ones32")
        self.cinfo = Buf(P, [128, 4], F32, "cinfo")
        self.d0 = Buf(P, [128, 128], F32, "d0")
        self.d0t = Buf(P, [128, 128], F32, "d0t")
        tmpi = Buf(P, [128, 128], I32, "tmpi")
        P.memset("pool", self.ident[:], 0.0, writes=[self.ident.r])
        P.op("pool", lambda e: e.affine_select(out=self.ident[:], in_=self.ident[:], pattern=[[-1, 128]],
                                               compare_op=ALU.not_equal, fill=1.0, base=0, channel_multiplier=1),
             reads=[self.ident.r], writes=[self.ident.r])
        P.memset("pool", self.ones16[:], 1.0, writes=[self.ones16.r])
        P.memset("pool", self.ones32[:], 1.0, writes=[self.ones32.r])
        ci, cr = self.dram["cinfo"]
        P.dma("sp", self.cinfo[:], ci[:, :], reads=[cr], writes=[self.cinfo.r])
        P.op("pool", lambda e: e.iota(tmpi[:], pattern=[[1, 128]], base=0, channel_multiplier=-1), writes=[tmpi.r])
        P.copy("dve", self.d0[:], tmpi[:], reads=[tmpi.r], writes=[self.d0.r])
        P.op("pool", lambda e: e.iota(tmpi[:], pattern=[[-1, 128]], base=0, channel_multiplier=1),
             reads=[self.d0.r], writes=[tmpi.r])
        P.copy("dve", self.d0t[:], tmpi[:], reads=[tmpi.r], writes=[self.d0t.r])


def load_weight(cx, dst, src, K, N, stage):
    P = cx.P
    src_ap, src_r = src
    i = 0
    SW = stage[0].shape[1]
    for kc in range((K + 127) // 128):
        rows = min(128, K - kc * 128)
        for n0 in range(0, N, SW):
            n = min(SW, N - n0)
            st = stage[i % 2]
            i += 1
            P.dma("sp", st[0:rows, 0:n], src_ap[kc * 128:kc * 128 + rows, n0:n0 + n], reads=[src_r], writes=[st.r])
            P.copy(("dve", "act", "pool")[i % 3], dst[0:rows, kc, n0:n0 + n], st[0:rows, 0:n], reads=[st.r], writes=[dst.r])


def load_gain(cx, name, row):
    P = cx.P
    g = Buf(P, [128, D], F32, name)
    ap, r = cx.dram[name]
    P.dma("sp", g[:], ap[row:row + 1, :].partition_broadcast(128), reads=[r], writes=[g.r])
    return g


def rms_rstd(cx, src_ap, src_res, junk, sm, col):
    P = cx.P
    P.act(junk[:], src_ap, AF.Square, reads=src_res, writes=[junk.r, sm.r], accum_out=sm[:, col:col + 1])
    P.act(sm[:, col + 1:col + 2], sm[:, col:col + 1], AF.Sqrt, reads=[sm.r], writes=[sm.r], scale=1.0 / D, bias=EPS)
    P.op("dve", lambda e: e.reciprocal(out=sm[:, col + 2:col + 3], in_=sm[:, col + 1:col + 2]),
         reads=[sm.r], writes=[sm.r])
    return sm[:, col + 2:col + 3]


def transpose_rows(cx, dstT, src, nchunks, k, src_reads):
    P = cx.P
    for c0 in range(0, nchunks, 8):
        n = min(8, nchunks - c0)
        pbt, pbr = cx.pb[(k + c0 // 8) % 2]
        for c in range(n):
            P.tr(pbt[:, c * 128:(c + 1) * 128], src[:, (c0 + c) * 128:(c0 + c + 1) * 128], cx.ident[:],
                 reads=list(src_reads) + [cx.ident.r], writes=[pbr], acc=(c > 0))
        eng = "act" if (c0 // 8) % 2 == 0 else "dve"
        P.copy(eng, dstT.ap(c0 * 128, [[1, n * 128]]), pbt[:, 0:n * 128], reads=[pbr], writes=[dstT.r])


def phase_pre(cx, li, ple_li):
    P = cx.P
    mark = P.mark()
    do_ple = ple_li is not None
    do_in = li is not None
    stage = [Buf(P, [128, 1024], F32, "stage") for _ in range(2)]
    hb = [Buf(P, [128, D], F32, "hs") for _ in range(2)]
    sm = Buf(P, [128, 16], F32, "sm")
    ub = Buf(P, [128, D], BF16, "u")
    junk = ub
    uT = Buf(P, [128, 8, 128], BF16, "uT")
    h_ap, h_r = cx.dram["h_in"]
    if do_ple:
        wg = Buf(P, [128, 8, D], BF16, "wg")
        wp = Buf(P, [128, 2, D], BF16, "wp")
        load_weight(cx, wg, cx.dram["w_ple_gate"], D, D, stage)
        load_weight(cx, wp, cx.dram["w_ple"], 256, D, stage)
        gple = load_gain(cx, "g_ple", 0)
        pt = [Buf(P, [128, 256], F32, "p") for _ in range(2)]
        pbf = Buf(P, [128, 256], BF16, "pbf")
        pT = Buf(P, [128, 2, 128], BF16, "pT")
        gate = Buf(P, [128, D], F32, "gate")
        tmp = Buf(P, [128, D], F32, "tmp")
        p_ap, p_r = cx.dram["p"]
        ho_ap, ho_r = cx.dram["h_out"]
    if do_in:
        L = LAYERS[li]
        nT = L["nq"] + L["nk"]
        TC = nT * 64
        NV = L["nv"]
        NW = TC + NV
        win = Buf(P, [128, 8, NW], BF16, "win")
        load_weight(cx, win, cx.dram["w_in"], D, NW, stage)
        gpre = load_gain(cx, "g_mix_pre", 0)
        tq = Buf(P, [128, TC], F32, "tq")
        tqb = Buf(P, [128, TC], BF16, "tqb")
        tTs = [Buf(P, [128, nT // 2, 128], BF16, "tTs") for _ in range(2)]
        tvs = [Buf(P, [128, NV], BF16, "tvs") for _ in range(2)]
        qT_ap, qT_r = cx.dram["qT_o"]
        kT_ap, kT_r = cx.dram["kT_o"]
        tv_ap, tv_r = cx.dram["tv_o"]
        if L["name"] == "dsa":
            wis = Buf(P, [128, 8], F32, "wis")
            wi_ap, wi_r = cx.dram["wi_o"]
        if L["rope"]:
            cs = Buf(P, [128, 2, NT, 8], F32, "cs")
            posi = Buf(P, [128, NT], I32, "posi")
            posf = Buf(P, [128, NT], F32, "posf")
            invf = Buf(P, [128, 8], F32, "invf")
            ang = Buf(P, [128, 2, NT, 8], F32, "ang")
            kk = Buf(P, [128, 2, NT, 8], F32, "kk")
            kki = Buf(P, [128, 2, NT, 8], I32, "kki")
            rt = Buf(P, [128, 4, nT * 8], F32, "ropetmp")
            pa, pr = cx.dram["pos"]
            ia, ir = cx.dram["invf"]
            P.dma("sp", posi[:], pa[:, :], reads=[pr], writes=[posi.r])
            P.dma("sp", invf[:], ia[:, :], reads=[ir], writes=[invf.r])
            P.copy("dve", posf[:], posi[:], reads=[posi.r], writes=[posf.r])
            NA = NT * 8
            P.tt("dve", ang.ap(NA, [[8, NT], [1, 8]]), posf.ap(0, [[1, NT], [0, 8]]), invf.ap(0, [[0, NT], [1, 8]]),
                 ALU.mult, reads=[posf.r, invf.r], writes=[ang.r])
            P.ts("dve", ang.ap(0, [[1, NA]]), ang.ap(NA, [[1, NA]]), math.pi / 2, None, ALU.add,
                 reads=[ang.r], writes=[ang.r])
            P.ts("dve", kk.ap(0, [[1, 2 * NA]]), ang.ap(0, [[1, 2 * NA]]), 1.0 / (2 * math.pi), None, ALU.mult,
                 reads=[ang.r], writes=[kk.r])
            P.copy("dve", kki.ap(0, [[1, 2 * NA]]), kk.ap(0, [[1, 2 * NA]]), reads=[kk.r], writes=[kki.r])
            P.copy("dve", kk.ap(0, [[1, 2 * NA]]), kki.ap(0, [[1, 2 * NA]]), reads=[kki.r], writes=[kk.r])
            P.stt("dve", ang.ap(0, [[1, 2 * NA]]), kk.ap(0, [[1, 2 * NA]]), -2 * math.pi, ang.ap(0, [[1, 2 * NA]]),
                  ALU.mult, ALU.add, reads=[kk.r, ang.r], writes=[ang.r])
            P.ts("dve", kk.ap(0, [[1, 2 * NA]]), ang.ap(0, [[1, 2 * NA]]), math.pi, -2 * math.pi, ALU.is_gt, ALU.mult,
                 reads=[ang.r], writes=[kk.r])
            P.tt("dve", ang.ap(0, [[1, 2 * NA]]), ang.ap(0, [[1, 2 * NA]]), kk.ap(0, [[1, 2 * NA]]), ALU.add,
                 reads=[ang.r, kk.r], writes=[ang.r])
            P.ts("dve", kk.ap(0, [[1, 2 * NA]]), ang.ap(0, [[1, 2 * NA]]), -math.pi, 2 * math.pi, ALU.is_lt, ALU.mult,
                 reads=[ang.r], writes=[kk.r])
            P.tt("dve", ang.ap(0, [[1, 2 * NA]]), ang.ap(0, [[1, 2 * NA]]), kk.ap(0, [[1, 2 * NA]]), ALU.add,
                 reads=[ang.r, kk.r], writes=[ang.r])
            P.act(cs.ap(0, [[1, 2 * NA]]), ang.ap(0, [[1, 2 * NA]]), AF.Sin, reads=[ang.r], writes=[cs.r])

    for m in range(NT):
        hs = hb[m % 2]
        P.dma("sp", hs[:], h_ap[m * 128:(m + 1) * 128, :], reads=[h_r], writes=[hs.r])
        if do_ple:
            pp = pt[m % 2]
            P.dma("sp", pp[:], p_ap[m * 128:(m + 1) * 128, :], reads=[p_r], writes=[pp.r])
            rstd = rms_rstd(cx, hs[:], [hs.r], junk, sm, 0)
            P.stt("dve", ub[:], hs[:], rstd, gple[:], ALU.mult, ALU.mult, reads=[hs.r, sm.r, gple.r], writes=[ub.r])
            transpose_rows(cx, uT, ub, 8, m, [ub.r])
            (g0, g0r), (g1, g1r) = cx.ps[0], cx.ps[1]
            for n, (gt, gr) in enumerate(((g0, g0r), (g1, g1r))):
                for c in range(8):
                    P.mm(gt[:, :], uT[:, c, :], wg[:, c, n * 512:(n + 1) * 512], c == 0, c == 7,
                         reads=[uT.r, wg.r], writes=[gr], acc=(c > 0))
                P.act(gate[:, n * 512:(n + 1) * 512], gt[:, :], AF.Sigmoid, reads=[gr], writes=[gate.r])
            P.copy("pool", pbf[:], pp[:], reads=[pp.r], writes=[pbf.r])
            transpose_rows(cx, pT, pbf, 2, m + 1, [pbf.r])
            for n in range(2):
                wt, wr = cx.ps[2 + n]
                for c in range(2):
                    P.mm(wt[:, :], pT[:, c, :], wp[:, c, n * 512:(n + 1) * 512], c == 0, c == 1,
                         reads=[pT.r, wp.r], writes=[wr], acc=(c > 0))
                P.tt("dve", tmp[:, n * 512:(n + 1) * 512], wt[:, :], gate[:, n * 512:(n + 1) * 512], ALU.mult,
                     reads=[wr, gate.r], writes=[tmp.r])
            P.tt("pool", hs[:], hs[:], tmp[:], ALU.add, reads=[hs.r, tmp.r], writes=[hs.r])
            P.dma("pool", ho_ap[m * 128:(m + 1) * 128, :], hs[:], reads=[hs.r], writes=[ho_r])
        if not do_in:
            continue
        rstd = rms_rstd(cx, hs[:], [hs.r], junk, sm, 4)
        P.stt("dve", ub[:], hs[:], rstd, gpre[:], ALU.mult, ALU.mult, reads=[hs.r, sm.r, gpre.r], writes=[ub.r])
        transpose_rows(cx, uT, ub, 8, m, [ub.r])
        tv_s = tvs[m % 2]
        nchunk = 0
        chunks = [(a, min(512, TC - a)) for a in range(0, TC, 512)] + \
                 [(TC + a, min(512, NV - a)) for a in range(0, NV, 512)]
        for (c0, n) in chunks:
            pst, psr = cx.ps[nchunk % 4]
            nchunk += 1
            for c in range(8):
                P.mm(pst[:, 0:n], uT[:, c, :], win[:, c, c0:c0 + n], c == 0, c == 7,
                     reads=[uT.r, win.r], writes=[psr], acc=(c > 0))
            if c0 < TC:
                qcols = L["qs"] * 64
                a = c0
                while a < c0 + n:
                    if a < qcols:
                        b = min(c0 + n, qcols)
                        sc = 0.125
                    else:
                        b = c0 + n
                        sc = 1.0
                    P.act(tq[:, a:b], pst[:, a - c0:b - c0], AF.Copy, reads=[psr], writes=[tq.r], scale=sc)
                    a = b
            else:
                v0 = c0 - TC
                if L["name"] == "dsa" and v0 + n > 1024:
                    nn = 1024 - v0
                    if nn > 0:
                        P.copy("dve", tv_s[:, v0:v0 + nn], pst[:, 0:nn], reads=[psr], writes=[tv_s.r])
                    P.copy("dve", wis[:, :], pst[:, nn:nn + 8], reads=[psr], writes=[wis.r])
                    P.copy("dve", tv_s[:, 1024:1032], wis[:, :], reads=[wis.r], writes=[tv_s.r])
                    P.dma("pool", wi_ap[m * 128:(m + 1) * 128, :], wis[:, :], reads=[wis.r], writes=[wi_r])
                else:
                    P.copy("dve", tv_s[:, v0:v0 + n], pst[:, 0:n], reads=[psr], writes=[tv_s.r])
        if L["rope"]:
            H8 = nT * 8
            x1 = tq.ap(0, [[64, nT], [1, 8]])
            x2 = tq.ap(8, [[64, nT], [1, 8]])
            cosb = cs.ap(m * 8, [[0, nT], [1, 8]])
            sinb = cs.ap(NT * 8 + m * 8, [[0, nT], [1, 8]])
            ta = rt.ap(0, [[8, nT], [1, 8]])
            tb = rt.ap(H8, [[8, nT], [1, 8]])
            tc = rt.ap(2 * H8, [[8, nT], [1, 8]])
            td = rt.ap(3 * H8, [[8, nT], [1, 8]])
            P.tt("dve", ta, x1, cosb, ALU.mult, reads=[tq.r, cs.r], writes=[rt.r])
            P.tt("pool", tb, x2, sinb, ALU.mult, reads=[tq.r, cs.r], writes=[rt.r])
            P.tt("dve", tc, x2, cosb, ALU.mult, reads=[tq.r, cs.r], writes=[rt.r])
            P.tt("pool", td, x1, sinb, ALU.mult, reads=[tq.r, cs.r], writes=[rt.r])
            P.tt("dve", x1, ta, tb, ALU.subtract, reads=[rt.r], writes=[tq.r])
            P.tt("dve", x2, tc, td, ALU.add, reads=[rt.r], writes=[tq.r])
        P.copy("pool", tqb[:], tq[:], reads=[tq.r], writes=[tqb.r])
        tT = tTs[m % 2]
        transpose_rows(cx, tT, tqb, nT // 2, m, [tqb.r])
        nqc = L["nq"] // 2
        nkc = L["nk"] // 2
        P.dma("pool", qT_ap[:, m * 128:(m + 1) * 128].rearrange("(c p) t -> p c t", p=128), tT[:, 0:nqc, :],
              reads=[tT.r], writes=[qT_r])
        P.dma("pool", kT_ap[:, m * 128:(m + 1) * 128].rearrange("(c p) t -> p c t", p=128), tT[:, nqc:nqc + nkc, :],
              reads=[tT.r], writes=[kT_r])
        P.dma("pool", tv_ap[m * 128:(m + 1) * 128, :], tv_s[:], reads=[tv_s.r], writes=[tv_r])
    P.barrier()
    P.release(mark)


def phase_post(cx, li):
    P = cx.P
    L = LAYERS[li]
    mark = P.mark()
    NOC = L["no"] // 128
    stage = [Buf(P, [128, 1024], F32, "stage") for _ in range(2)]
    wo = Buf(P, [128, NOC, D], BF16, "wo")
    w1 = Buf(P, [128, 8, 4096], BF16, "w1")
    w2 = Buf(P, [128, 32, D], BF16, "w2")
    load_weight(cx, wo, cx.dram["w_out"], L["no"], D, stage)
    load_weight(cx, w1, cx.dram["w_ff_in"], D, 4096, stage)
    load_weight(cx, w2, cx.dram["w_ff_out"], 4096, D, stage)
    gpost = load_gain(cx, "g_mix_post", 0)
    gfpre = load_gain(cx, "g_ffn_pre", 0)
    gfpost = load_gain(cx, "g_ffn_post", 0)
    hb = [Buf(P, [128, D], F32, "hs") for _ in range(2)]
    ob = [Buf(P, [128, NOC, 128], BF16, "oT") for _ in range(2)]
    sm = Buf(P, [128, 16], F32, "sm")
    tmp = Buf(P, [128, D], F32, "tmp")
    ub = Buf(P, [128, D], BF16, "u")
    junk = ub
    uT = Buf(P, [128, 8, 128], BF16, "uT")
    rl = [Buf(P, [128, 512], F32, "relu") for _ in range(2)]
    aT = Buf(P, [128, 32, 128], BF16, "aT")
    h_ap, h_r = cx.dram["h_res"]
    ho_ap, ho_r = cx.dram["h_mid"]
    oT_ap, oT_r = cx.dram["oT"]
    for m in range(NT):
        hs = hb[m % 2]
        ot = ob[m % 2]
        P.dma("sp", hs[:], h_ap[m * 128:(m + 1) * 128, :], reads=[h_r], writes=[hs.r])
        P.dma("sp", ot[:], oT_ap[:, m * 128:(m + 1) * 128].rearrange("(c p) t -> p c t", p=128),
              reads=[oT_r], writes=[ot.r])
        y = (cx.ps[0], cx.ps[1])
        for n in range(2):
            yt, yr = y[n]
            for c in range(NOC):
                P.mm(yt[:, :], ot[:, c, :], wo[:, c, n * 512:(n + 1) * 512], c == 0, c == NOC - 1,
                     reads=[ot.r, wo.r], writes=[yr], acc=(c > 0))
        P.act(junk[:, 0:512], y[0][0][:, :], AF.Square, reads=[y[0][1]], writes=[junk.r, sm.r], accum_out=sm[:, 0:1])
        P.act(junk[:, 512:1024], y[1][0][:, :], AF.Square, reads=[y[1][1]], writes=[junk.r, sm.r], accum_out=sm[:, 1:2])
        P.tt("dve", sm[:, 2:3], sm[:, 0:1], sm[:, 1:2], ALU.add, reads=[sm.r], writes=[sm.r])
        P.act(sm[:, 3:4], sm[:, 2:3], AF.Sqrt, reads=[sm.r], writes=[sm.r], scale=1.0 / D, bias=EPS)
        P.op("dve", lambda e: e.reciprocal(out=sm[:, 4:5], in_=sm[:, 3:4]), reads=[sm.r], writes=[sm.r])
        for n in range(2):
            P.stt("dve", tmp[:, n * 512:(n + 1) * 512], y[n][0][:, :], sm[:, 4:5], gpost[:, n * 512:(n + 1) * 512],
                  ALU.mult, ALU.mult, reads=[y[n][1], sm.r, gpost.r], writes=[tmp.r])
        P.tt("pool", hs[:], hs[:], tmp[:], ALU.add, reads=[hs.r, tmp.r], writes=[hs.r])
        rstd = rms_rstd(cx, hs[:], [hs.r], junk, sm, 5)
        P.stt("dve", ub[:], hs[:], rstd, gfpre[:], ALU.mult, ALU.mult, reads=[hs.r, sm.r, gfpre.r], writes=[ub.r])
        transpose_rows(cx, uT, ub, 8, m, [ub.r])
        for f4 in range(8):
            pt_, pr_ = cx.ps[2 + f4 % 2]
            for f in range(4):
                ff = f4 * 4 + f
                for c in range(8):
                    P.mm(pt_[:, f * 128:(f + 1) * 128], w1[:, c, ff * 128:(ff + 1) * 128], uT[:, c, :], c == 0, c == 7,
                         reads=[w1.r, uT.r], writes=[pr_], acc=(c > 0 or f > 0))
            r_ = rl[f4 % 2]
            P.act(r_[:], pt_[:, :], AF.Relu, reads=[pr_], writes=[r_.r])
            P.tt("pool", aT.ap(f4 * 512, [[1, 512]]), r_[:], r_[:], ALU.mult, reads=[r_.r], writes=[aT.r])
        fo = (cx.ps[4], cx.ps[5])
        for n in range(2):
            ft, fr = fo[n]
            for ff in range(32):
                P.mm(ft[:, :], aT[:, ff, :], w2[:, ff, n * 512:(n + 1) * 512], ff == 0, ff == 31,
                     reads=[aT.r, w2.r], writes=[fr], acc=(ff > 0))
        P.act(junk[:, 0:512], fo[0][0][:, :], AF.Square, reads=[fo[0][1]], writes=[junk.r, sm.r], accum_out=sm[:, 8:9])
        P.act(junk[:, 512:1024], fo[1][0][:, :], AF.Square, reads=[fo[1][1]], writes=[junk.r, sm.r], accum_out=sm[:, 9:10])
        P.tt("dve", sm[:, 10:11], sm[:, 8:9], sm[:, 9:10], ALU.add, reads=[sm.r], writes=[sm.r])
        P.act(sm[:, 11:12], sm[:, 10:11], AF.Sqrt, reads=[sm.r], writes=[sm.r], scale=1.0 / D, bias=EPS)
        P.op("dve", lambda e: e.reciprocal(out=sm[:, 12:13], in_=sm[:, 11:12]), reads=[sm.r], writes=[sm.r])
        for n in range(2):
            P.stt("dve", tmp[:, n * 512:(n + 1) * 512], fo[n][0][:, :], sm[:, 12:13], gfpost[:, n * 512:(n + 1) * 512],
                  ALU.mult, ALU.mult, reads=[fo[n][1], sm.r, gfpost.r], writes=[tmp.r])
        P.tt("pool", hs[:], hs[:], tmp[:], ALU.add, reads=[hs.r, tmp.r], writes=[hs.r])
        P.dma("pool", ho_ap[m * 128:(m + 1) * 128, :], hs[:], reads=[hs.r], writes=[ho_r])
    P.barrier()
    P.release(mark)


def rms_rstd2(cx, src_aps, src_res, junk, sm, col):
    P = cx.P
    for i, ap in enumerate(src_aps):
        w = 512 if len(src_aps) == 2 else D
        P.act(junk[:, i * 512:i * 512 + w], ap, AF.Square, reads=src_res, writes=[junk.r, sm.r],
              accum_out=sm[:, col + i:col + i + 1])
    if len(src_aps) == 2:
        P.tt("dve", sm[:, col + 2:col + 3], sm[:, col:col + 1], sm[:, col + 1:col + 2], ALU.add, reads=[sm.r], writes=[sm.r])
        tot = sm[:, col + 2:col + 3]
    else:
        tot = sm[:, col:col + 1]
    P.act(sm[:, col + 3:col + 4], tot, AF.Sqrt, reads=[sm.r], writes=[sm.r], scale=1.0 / D, bias=EPS)
    P.op("dve", lambda e: e.reciprocal(out=sm[:, col + 4:col + 5], in_=sm[:, col + 3:col + 4]), reads=[sm.r], writes=[sm.r])
    return sm[:, col + 4:col + 5]


def phase_post2(cx, li):
    P = cx.P
    L = LAYERS[li]
    NOC = L["no"] // 128
    h_ap, h_r = cx.dram["h_res"]
    ha_ap, ha_r = cx.dram["h_a"]
    ho_ap, ho_r = cx.dram["h_mid"]
    oT_ap, oT_r = cx.dram["oT"]
    mark = P.mark()
    stage = [Buf(P, [128, 1024], F32, "stage") for _ in range(4)]
    wo = Buf(P, [128, NOC, D], BF16, "wo")
    load_weight(cx, wo, cx.dram["w_out"], L["no"], D, stage)
    gpost = load_gain(cx, "g_mix_post", 0)
    slots = [dict(hs=Buf(P, [128, D], F32, "hs"), ot=Buf(P, [128, NOC, 128], BF16, "oT"), tmp=Buf(P, [128, D], F32, "tmp"),
                  junk=Buf(P, [128, D], BF16, "junk"), sm=Buf(P, [128, 16], F32, "sm"),
                  y=(cx.ps[2 * s], cx.ps[2 * s + 1])) for s in range(2)]

    def streamA(m):
        def gen(s):
            sl = slots[s]
            hs, ot, tmp, junk, sm, y = (sl[k] for k in ("hs", "ot", "tmp", "junk", "sm", "y"))
            P.dma("sp", hs[:], h_ap[m * 128:(m + 1) * 128, :], reads=[h_r], writes=[hs.r])
            P.dma("sp", ot[:], oT_ap[:, m * 128:(m + 1) * 128].rearrange("(c p) t -> p c t", p=128),
                  reads=[oT_r], writes=[ot.r])
            yield
            for n in range(2):
                yt, yr = y[n]
                for c in range(NOC):
                    P.mm(yt[:, :], ot[:, c, :], wo[:, c, n * 512:(n + 1) * 512], c == 0, c == NOC - 1,
                         reads=[ot.r, wo.r], writes=[yr], acc=(c > 0))
            yield
            rstd = rms_rstd2(cx, [y[0][0][:, :], y[1][0][:, :]], [y[0][1], y[1][1]], junk, sm, 0)
            yield
            for n in range(2):
                P.stt("dve", tmp[:, n * 512:(n + 1) * 512], y[n][0][:, :], rstd, gpost[:, n * 512:(n + 1) * 512],
                      ALU.mult, ALU.mult, reads=[y[n][1], sm.r, gpost.r], writes=[tmp.r])
            P.tt("dve", hs[:], hs[:], tmp[:], ALU.add, reads=[hs.r, tmp.r], writes=[hs.r])
            P.dma("pool", ha_ap[m * 128:(m + 1) * 128, :], hs[:], reads=[hs.r], writes=[ha_r])
            yield
        return gen

    run_streams([streamA(m) for m in range(DBG.get("nt", NT))])
    P.barrier()
    P.release(mark)
    mark = P.mark()
    stage = [Buf(P, [128, 1024], F32, "stage") for _ in range(4)]
    w1 = Buf(P, [128, 8, 4096], BF16, "w1")
    w2 = Buf(P, [128, 32, D], BF16, "w2")
    load_weight(cx, w1, cx.dram["w_ff_in"], D, 4096, stage)
    load_weight(cx, w2, cx.dram["w_ff_out"], 4096, D, stage)
    gfpre = load_gain(cx, "g_ffn_pre", 0)
    gfpost = load_gain(cx, "g_ffn_post", 0)
    slots = [dict(hs=Buf(P, [128, D], F32, "hs"), tmp=Buf(P, [128, D], F32, "tmp"), ub=Buf(P, [128, D], BF16, "u"),
                  uT=Buf(P, [128, 8, 128], BF16, "uT"), rl=Buf(P, [128, 512], F32, "relu"),
                  aT=Buf(P, [128, 32, 128], BF16, "aT"), sm=Buf(P, [128, 16], F32, "sm"),
                  F=(cx.ps[3 * s], cx.ps[3 * s + 1]), H=cx.ps[3 * s + 2]) for s in range(2)]

    def streamB(m):
        def gen(s):
            sl = slots[s]
            hs, tmp, ub, uT, rl, aT, sm, F, H = (sl[k] for k in ("hs", "tmp", "ub", "uT", "rl", "aT", "sm", "F", "H"))
            P.dma("sp", hs[:], ha_ap[m * 128:(m + 1) * 128, :], reads=[ha_r], writes=[hs.r])
            yield
            rstd = rms_rstd2(cx, [hs[:]], [hs.r], ub, sm, 0)
            P.stt("dve", ub[:], hs[:], rstd, gfpre[:], ALU.mult, ALU.mult, reads=[hs.r, sm.r, gfpre.r], writes=[ub.r])
            yield
            transpose_rows(cx, uT, ub, 8, 0, [ub.r])
            yield
            Ht, Hr = H
            for f4 in range(8):
                for f in range(4):
                    ff = f4 * 4 + f
                    for c in range(8):
                        P.mm(Ht[:, f * 128:(f + 1) * 128], w1[:, c, ff * 128:(ff + 1) * 128], uT[:, c, :],
                             f == 0 and c == 0, f == 3 and c == 7, reads=[w1.r, uT.r], writes=[Hr], acc=(c > 0 or f > 0))
                yield
                P.act(rl[:], Ht[:, :], AF.Relu, reads=[Hr], writes=[rl.r])
                P.tt("pool", aT.ap(f4 * 512, [[1, 512]]), rl[:], rl[:], ALU.mult, reads=[rl.r], writes=[aT.r])
            yield
            for n in range(2):
                ft, fr = F[n]
                for ff in range(32):
                    P.mm(ft[:, :], aT[:, ff, :], w2[:, ff, n * 512:(n + 1) * 512], ff == 0, ff == 31,
                         reads=[aT.r, w2.r], writes=[fr], acc=(ff > 0))
                yield
            rstd = rms_rstd2(cx, [F[0][0][:, :], F[1][0][:, :]], [F[0][1], F[1][1]], ub, sm, 8)
            yield
            for n in range(2):
                P.stt("dve", tmp[:, n * 512:(n + 1) * 512], F[n][0][:, :], rstd, gfpost[:, n * 512:(n + 1) * 512],
                      ALU.mult, ALU.mult, reads=[F[n][1], sm.r, gfpost.r], writes=[tmp.r])
            P.tt("dve", hs[:], hs[:], tmp[:], ALU.add, reads=[hs.r, tmp.r], writes=[hs.r])
            P.dma("pool", ho_ap[m * 128:(m + 1) * 128, :], hs[:], reads=[hs.r], writes=[ho_r])
            yield
        return gen

    run_streams([streamB(m) for m in range(DBG.get("nt", NT))])
    P.barrier()
    P.release(mark)


def phase_pre2(cx, li, ple_li):
    P = cx.P
    mark = P.mark()
    do_ple = ple_li is not None
    do_in = li is not None
    stage = [Buf(P, [128, 1024], F32, "stage") for _ in range(2)]
    h_ap, h_r = cx.dram["h_in"]
    TC = NV = nT = 0
    if do_ple:
        wg = Buf(P, [128, 8, D], BF16, "wg")
        wp = Buf(P, [128, 2, D], BF16, "wp")
        load_weight(cx, wg, cx.dram["w_ple_gate"], D, D, stage)
        load_weight(cx, wp, cx.dram["w_ple"], 256, D, stage)
        gple = load_gain(cx, "g_ple", 0)
        p_ap, p_r = cx.dram["p"]
        ho_ap, ho_r = cx.dram["h_out"]
    if do_in:
        L = LAYERS[li]
        nT = L["nq"] + L["nk"]
        TC = nT * 64
        NV = L["nv"]
        NW = TC + NV
        win = Buf(P, [128, 8, NW], BF16, "win")
        load_weight(cx, win, cx.dram["w_in"], D, NW, stage)
        gpre = load_gain(cx, "g_mix_pre", 0)
        qT_ap, qT_r = cx.dram["qT_o"]
        kT_ap, kT_r = cx.dram["kT_o"]
        tv_ap, tv_r = cx.dram["tv_o"]
        if L["name"] == "dsa":
            wi_ap, wi_r = cx.dram["wi_o"]
        if L["rope"]:
            cs = Buf(P, [128, 2, NT, 8], F32, "cs")
            rt = Buf(P, [128, 4, nT * 8], F32, "ropetmp")
            m2 = P.mark()
            posi = Buf(P, [128, NT], I32, "posi")
            posf = Buf(P, [128, NT], F32, "posf")
            invf = Buf(P, [128, 8], F32, "invf")
            ang = Buf(P, [128, 2, NT, 8], F32, "ang")
            kk = Buf(P, [128, 2, NT, 8], F32, "kk")
            kki = Buf(P, [128, 2, NT, 8], I32, "kki")
            pa, pr = cx.dram["pos"]
            ia, ir = cx.dram["invf"]
            P.dma("sp", posi[:], pa[:, :], reads=[pr], writes=[posi.r])
            P.dma("sp", invf[:], ia[:, :], reads=[ir], writes=[invf.r])
            P.copy("dve", posf[:], posi[:], reads=[posi.r], writes=[posf.r])
            NA = NT * 8
            fl = [[1, 2 * NA]]
            P.tt("dve", ang.ap(NA, [[8, NT], [1, 8]]), posf.ap(0, [[1, NT], [0, 8]]), invf.ap(0, [[0, NT], [1, 8]]),
                 ALU.mult, reads=[posf.r, invf.r], writes=[ang.r])
            P.ts("dve", ang.ap(0, [[1, NA]]), ang.ap(NA, [[1, NA]]), math.pi / 2, None, ALU.add, reads=[ang.r], writes=[ang.r])
            P.ts("dve", kk.ap(0, fl), ang.ap(0, fl), 1.0 / (2 * math.pi), None, ALU.mult, reads=[ang.r], writes=[kk.r])
            P.copy("dve", kki.ap(0, fl), kk.ap(0, fl), reads=[kk.r], writes=[kki.r])
            P.copy("dve", kk.ap(0, fl), kki.ap(0, fl), reads=[kki.r], writes=[kk.r])
            P.stt("dve", ang.ap(0, fl), kk.ap(0, fl), -2 * math.pi, ang.ap(0, fl), ALU.mult, ALU.add,
                  reads=[kk.r, ang.r], writes=[ang.r])
            P.ts("dve", kk.ap(0, fl), ang.ap(0, fl), math.pi, -2 * math.pi, ALU.is_gt, ALU.mult, reads=[ang.r], writes=[kk.r])
            P.tt("dve", ang.ap(0, fl), ang.ap(0, fl), kk.ap(0, fl), ALU.add, reads=[ang.r, kk.r], writes=[ang.r])
            P.ts("dve", kk.ap(0, fl), ang.ap(0, fl), -math.pi, 2 * math.pi, ALU.is_lt, ALU.mult, reads=[ang.r], writes=[kk.r])
            P.tt("dve", ang.ap(0, fl), ang.ap(0, fl), kk.ap(0, fl), ALU.add, reads=[ang.r, kk.r], writes=[ang.r])
            P.act(cs.ap(0, fl), ang.ap(0, fl), AF.Sin, reads=[ang.r], writes=[cs.r])
            P.barrier()
            P.release(m2)
    SW = max(2048, TC)
    slots = []
    for s in range(2):
        sl = dict(hs=Buf(P, [128, D], F32, "hs"), ub=Buf(P, [128, D], BF16, "u"), uT=Buf(P, [128, 8, 128], BF16, "uT"),
                  sm=Buf(P, [128, 16], F32, "sm"), SCR=Buf(P, [128, SW], F32, "scr"), ps=[cx.ps[3 * s + i] for i in range(3)])
        if do_ple:
            sl.update(pp=Buf(P, [128, 256], F32, "p"), pbf=Buf(P, [128, 256], BF16, "pbf"), pT=Buf(P, [128, 2, 128], BF16, "pT"))
        if do_in:
            sl.update(tqb=Buf(P, [128, TC], BF16, "tqb"), tT=Buf(P, [128, nT // 2, 128], BF16, "tTs"),
                      tvs=Buf(P, [128, NV], BF16, "tvs"))
            if L["name"] == "dsa":
                sl["wis"] = Buf(P, [128, 8], F32, "wis")
        slots.append(sl)

    def stream(m):
        def gen(s):
            sl = slots[s]
            hs, ub, uT, sm, SCR, ps = (sl[k] for k in ("hs", "ub", "uT", "sm", "SCR", "ps"))
            P.dma("sp", hs[:], h_ap[m * 128:(m + 1) * 128, :], reads=[h_r], writes=[hs.r])
            if do_ple:
                pp, pbf, pT = sl["pp"], sl["pbf"], sl["pT"]
                P.dma("sp", pp[:], p_ap[m * 128:(m + 1) * 128, :], reads=[p_r], writes=[pp.r])
                yield
                rstd = rms_rstd2(cx, [hs[:]], [hs.r], ub, sm, 0)
                P.stt("dve", ub[:], hs[:], rstd, gple[:], ALU.mult, ALU.mult, reads=[hs.r, sm.r, gple.r], writes=[ub.r])
                P.copy("pool", pbf[:], pp[:], reads=[pp.r], writes=[pbf.r])
                yield
                transpose_rows(cx, uT, ub, 8, 0, [ub.r])
                transpose_rows(cx, pT, pbf, 2, 0, [pbf.r])
                yield
                for n in range(2):
                    gt, gr = ps[n]
                    for c in range(8):
                        P.mm(gt[:, :], uT[:, c, :], wg[:, c, n * 512:(n + 1) * 512], c == 0, c == 7,
                             reads=[uT.r, wg.r], writes=[gr], acc=(c > 0))
                    P.act(SCR[:, n * 512:(n + 1) * 512], gt[:, :], AF.Sigmoid, reads=[gr], writes=[SCR.r])
                yield
                for n in range(2):
                    wt, wr = ps[2] if n == 0 else ps[0]
                    for c in range(2):
                        P.mm(wt[:, :], pT[:, c, :], wp[:, c, n * 512:(n + 1) * 512], c == 0, c == 1,
                             reads=[pT.r, wp.r], writes=[wr], acc=(c > 0))
                    P.tt("dve", SCR[:, 1024 + n * 512:1024 + (n + 1) * 512], wt[:, :], SCR[:, n * 512:(n + 1) * 512], ALU.mult,
                         reads=[wr, SCR.r], writes=[SCR.r])
                P.tt("dve", hs[:], hs[:], SCR[:, 1024:2048], ALU.add, reads=[hs.r, SCR.r], writes=[hs.r])
                P.dma("pool", ho_ap[m * 128:(m + 1) * 128, :], hs[:], reads=[hs.r], writes=[ho_r])
            yield
            if not do_in:
                return
            tqb, tT, tv_s = sl["tqb"], sl["tT"], sl["tvs"]
            rstd = rms_rstd2(cx, [hs[:]], [hs.r], ub, sm, 8)
            P.stt("dve", ub[:], hs[:], rstd, gpre[:], ALU.mult, ALU.mult, reads=[hs.r, sm.r, gpre.r], writes=[ub.r])
            yield
            transpose_rows(cx, uT, ub, 8, 0, [ub.r])
            yield
            chunks = [(a, min(512, TC - a)) for a in range(0, TC, 512)] + \
                     [(TC + a, min(512, NV - a)) for a in range(0, NV, 512)]
            qcols = L["qs"] * 64
            for ci, (c0, n) in enumerate(chunks):
                pst, psr = ps[ci % 3]
                for c in range(8):
                    P.mm(pst[:, 0:n], uT[:, c, :], win[:, c, c0:c0 + n], c == 0, c == 7,
                         reads=[uT.r, win.r], writes=[psr], acc=(c > 0))
                if c0 < TC:
                    a = c0
                    while a < c0 + n:
                        b, sc = (min(c0 + n, qcols), 0.125) if a < qcols else (c0 + n, 1.0)
                        P.act(SCR[:, a:b], pst[:, a - c0:b - c0], AF.Copy, reads=[psr], writes=[SCR.r], scale=sc)
                        a = b
                else:
                    v0 = c0 - TC
                    if L["name"] == "dsa" and v0 + n > 1024:
                        wis = sl["wis"]
                        nn = 1024 - v0
                        if nn > 0:
                            P.copy("dve", tv_s[:, v0:v0 + nn], pst[:, 0:nn], reads=[psr], writes=[tv_s.r])
                        P.copy("dve", wis[:, :], pst[:, nn:nn + 8], reads=[psr], writes=[wis.r])
                        P.copy("dve", tv_s[:, 1024:1032], wis[:, :], reads=[wis.r], writes=[tv_s.r])
                        P.dma("pool", wi_ap[m * 128:(m + 1) * 128, :], wis[:, :], reads=[wis.r], writes=[wi_r])
                    else:
                        P.copy("dve", tv_s[:, v0:v0 + n], pst[:, 0:n], reads=[psr], writes=[tv_s.r])
                if ci % 2 == 1:
                    yield
            yield
            if L["rope"]:
                H8 = nT * 8
                x1 = SCR.ap(0, [[64, nT], [1, 8]])
                x2 = SCR.ap(8, [[64, nT], [1, 8]])
                cosb = cs.ap(m * 8, [[0, nT], [1, 8]])
                sinb = cs.ap(NT * 8 + m * 8, [[0, nT], [1, 8]])
                ta, tb, tc, td = (rt.ap(i * H8, [[8, nT], [1, 8]]) for i in range(4))
                P.tt("dve", ta, x1, cosb, ALU.mult, reads=[SCR.r, cs.r], writes=[rt.r])
                P.tt("dve", tb, x2, sinb, ALU.mult, reads=[SCR.r, cs.r], writes=[rt.r])
                P.tt("dve", tc, x2, cosb, ALU.mult, reads=[SCR.r, cs.r], writes=[rt.r])
                P.tt("dve", td, x1, sinb, ALU.mult, reads=[SCR.r, cs.r], writes=[rt.r])
                P.tt("dve", x1, ta, tb, ALU.subtract, reads=[rt.r], writes=[SCR.r])
                P.tt("dve", x2, tc, td, ALU.add, reads=[rt.r], writes=[SCR.r])
            hlf = (TC // 2) // 128 * 128
            P.copy("act", tqb[:, 0:hlf], SCR[:, 0:hlf], reads=[SCR.r], writes=[tqb.r])
            P.copy("dve", tqb[:, hlf:TC], SCR[:, hlf:TC], reads=[SCR.r], writes=[tqb.r])
            yield
            for c0 in range(0, nT // 2, 8):
                nn = min(8, nT // 2 - c0)
                pbt, pbr = cx.pb[0]
                for c in range(nn):
                    P.tr(pbt[:, c * 128:(c + 1) * 128], tqb[:, (c0 + c) * 128:(c0 + c + 1) * 128], cx.ident[:],
                         reads=[tqb.r, cx.ident.r], writes=[pbr], acc=(c > 0))
                P.copy("act" if (c0 // 8) % 2 == 0 else "dve", tT.ap(c0 * 128, [[1, nn * 128]]), pbt[:, 0:nn * 128],
                       reads=[pbr], writes=[tT.r])
                yield
            nqc, nkc = L["nq"] // 2, L["nk"] // 2
            P.dma("pool", qT_ap[:, m * 128:(m + 1) * 128].rearrange("(c p) t -> p c t", p=128), tT[:, 0:nqc, :],
                  reads=[tT.r], writes=[qT_r])
            P.dma("pool", kT_ap[:, m * 128:(m + 1) * 128].rearrange("(c p) t -> p c t", p=128), tT[:, nqc:nqc + nkc, :],
                  reads=[tT.r], writes=[kT_r])
            P.dma("pool", tv_ap[m * 128:(m + 1) * 128, :], tv_s[:], reads=[tv_s.r], writes=[tv_r])
            yield
        return gen

    run_streams([stream(m) for m in range(DBG.get("nt", NT))])
    P.barrier()
    P.release(mark)


def run_streams(factories, nslots=2):
    pending = list(factories)
    active = []
    free = list(range(nslots))
    while pending or active:
        while pending and free:
            s = free.pop(0)
            active.append((pending.pop(0)(s), s))
        for item in list(active):
            g, s = item
            try:
                next(g)
            except StopIteration:
                active.remove(item)
                free.append(s)


class KV:
    def __init__(self, cx):
        P = cx.P
        self.Ks_sets = [Buf(P, [128, 2, 4, 2048], BF16, "Ks") for _ in range(2)]
        self.Vs = Buf(P, [128, 4, 16, 256], BF16, "Vs")
        self.Qz_sets = [[Buf(P, [128, 2, 2048], BF16, f"Qz{i}") for i in range(2)] for _ in range(2)]
        self.cur = 1
        for qs in self.Qz_sets:
            for q in qs:
                P.memset("pool", q[:], 0.0, writes=[q.r])

    @property
    def Ks(self):
        return self.Ks_sets[self.cur]

    @property
    def Qz(self):
        return self.Qz_sets[self.cur]

    def load(self, cx, qrow0, krow0, vcol0):
        P = cx.P
        self.cur = 1 - self.cur
        q_ap, q_r = cx.dram["qT"]
        k_ap, k_r = cx.dram["kTg"]
        v_ap, v_r = cx.dram["vg"]
        qsrc = q_ap[qrow0:qrow0 + 256, :].rearrange("(c p) t -> p c t", p=128)
        for par in range(2):
            P.dma("sp", self.Qz[par][par * 64:(par + 1) * 64, :, :], qsrc[par * 64:(par + 1) * 64, :, :],
                  reads=[q_r], writes=[self.Qz[par].r])
        for ch in range(2):
            P.dma("sp", self.Ks[:, ch, :, :],
                  k_ap[:, krow0 + ch * 128:krow0 + (ch + 1) * 128, :].rearrange("r p t -> p r t"),
                  reads=[k_r], writes=[self.Ks.r])
        if vcol0 is not None:
            for rk in range(4):
                P.dma("sp", self.Vs[:, rk, :, :], v_ap[rk, :, vcol0:vcol0 + 256].rearrange("(m p) f -> p m f", p=128),
                      reads=[v_r], writes=[self.Vs.r])

    def qk(self, cx, A, Ar, rank, ml, m, stop=True):
        P = cx.P
        Ks = self.Ks
        for h in range(4):
            qz = self.Qz[h % 2]
            P.mm(A[:, h * 128:(h + 1) * 128], Ks[:, h // 2, rank, ml * 128:(ml + 1) * 128],
                 qz[:, h // 2, m * 128:(m + 1) * 128], h == 0, stop and h == 3,
                 reads=[Ks.r, qz.r], writes=[Ar], acc=(h > 0))

    def av(self, cx, O, Or, pt, rank, ml, first, last, nrow):
        P = cx.P
        for h in range(4):
            P.mm(O[0:64, h * 128:(h + 1) * 128], self.Vs[:, rank, ml, h * 64:(h + 1) * 64], pt[:, h * 128:(h + 1) * 128],
                 first and h == 0, last and h == 3, reads=[self.Vs.r, pt.r], writes=[Or], acc=(not first or h > 0))
        if nrow == 65:
            P.mm(O[64:65, :], cx.ones16[:, 0:1], pt[:, :], first, last, reads=[cx.ones16.r, pt.r], writes=[Or], acc=True)


def causal_masks(cx, strict):
    P = cx.P
    M = Buf(P, [128, 4, 128], BF16, "cmask")
    for r in range(4):
        thr = 128 * r + (0.5 if strict else -0.5)
        P.ts("dve", M[:, r, :], cx.d0[:], cx.cinfo[:, 0:1], thr, ALU.add, ALU.is_gt,
             reads=[cx.d0.r, cx.cinfo.r], writes=[M.r])
    return M


def store_oT(cx, oTs, hrow0, m):
    P = cx.P
    o_ap, o_r = cx.dram["oT"]
    P.dma("pool", o_ap[hrow0:hrow0 + 256, m * 128:(m + 1) * 128].rearrange("(h d) t -> d h t", d=64),
          oTs[0:64, :, :], reads=[oTs.r], writes=[o_r])


def phase_att_sb(cx):
    P = cx.P
    mark = P.mark()
    kv = KV(cx)
    M = causal_masks(cx, True)
    NIU = Buf(P, [128, 128], BF16, "niu")
    P.ts("dve", NIU[:], cx.d0t[:], -0.5, -1.0, ALU.is_gt, ALU.mult, reads=[cx.d0t.r], writes=[NIU.r])
    slots = []
    for s in range(2):
        slots.append(dict(
            E=Buf(P, [128, 512], F32, "E"), SP=Buf(P, [128, 512], BF16, "SP"), ARG=Buf(P, [128, 512], F32, "ARG"),
            C=Buf(P, [128, 512], F32, "C"), PT=Buf(P, [128, 512], BF16, "PT"), OT=Buf(P, [64, 4, 128], BF16, "OT"),
            A=cx.ps[s], A2=cx.ps[2 + s], B=cx.ps[6], O=cx.ps[4 + s]))

    def stream(hg, m):
        def gen(s):
            sl = slots[s]
            (A, Ar), (A2, A2r), (B, Br), (O, Or) = sl["A"], sl["A2"], sl["B"], sl["O"]
            E, SP, ARG, C, PT, OT = sl["E"], sl["SP"], sl["ARG"], sl["C"], sl["PT"], sl["OT"]
            kbs = list(range(4 * m + 3, -1, -1))
            lvl = DBG.get("lvl", 99)
            for i, kb in enumerate(kbs):
                rank, ml, r = kb % 4, kb // 4, kb - 4 * m
                last = i == len(kbs) - 1
                if lvl < 1:
                    continue
                kv.qk(cx, A, Ar, rank, ml, m, stop=True)
                yield
                if lvl < 2:
                    continue
                P.act(E[:], A[:, :], AF.Exp, reads=[Ar], writes=[E.r])
                yield
                if lvl < 3:
                    continue
                P.act(SP[:], E[:], AF.Ln, reads=[E.r], writes=[SP.r], bias=1.0)
                if r >= 0:
                    P.tt(DBG.get("maskeng", "pool"), SP.ap(0, [[128, 4], [1, 128]]), SP.ap(0, [[128, 4], [1, 128]]),
                         M.ap(r * 128, [[0, 4], [1, 128]]), ALU.mult, reads=[SP.r, M.r], writes=[SP.r])
                yield
                if lvl < 4:
                    continue
                kv.qk(cx, A2, A2r, rank, ml, m, stop=False)
                P.mm(A2[:, :], NIU[:], SP[:], False, True, reads=[NIU.r, SP.r], writes=[A2r], acc=True)
                if not last:
                    P.mm(B[:, :], cx.ones16[:], SP[:], True, True, reads=[cx.ones16.r, SP.r], writes=[Br])
                if lvl < 5:
                    continue
                if i == 0:
                    P.copy("dve", ARG[:], A2[:, :], reads=[A2r], writes=[ARG.r])
                    if not last:
                        P.copy("dve", C[:], B[:, :], reads=[Br], writes=[C.r])
                else:
                    P.tt("dve", ARG[:], A2[:, :], C[:], ALU.subtract, reads=[A2r, C.r], writes=[ARG.r])
                    if not last:
                        P.tt("dve", C[:], C[:], B[:, :], ALU.add, reads=[C.r, Br], writes=[C.r])
                yield
                if lvl < 6:
                    continue
                P.act(PT[:], ARG[:], AF.Exp, reads=[ARG.r], writes=[PT.r])
                if r >= 0:
                    P.tt(DBG.get("maskeng", "pool"), PT.ap(0, [[128, 4], [1, 128]]), PT.ap(0, [[128, 4], [1, 128]]),
                         M.ap(r * 128, [[0, 4], [1, 128]]), ALU.mult, reads=[PT.r, M.r], writes=[PT.r])
                yield
                if lvl < 7:
                    continue
                kv.av(cx, O, Or, PT, rank, ml, i == 0, last, 64)
                yield
            if lvl >= 8:
                P.copy("act", OT.ap(0, [[1, 512]], 0, 64), O[0:64, :], reads=[Or], writes=[OT.r])
                store_oT(cx, OT, hg * 256, m)
            yield
        return gen

    for hg in range(DBG.get("hg", 4)):
        kv.load(cx, hg * 256, hg * 256, hg * 256)
        run_streams([stream(hg, m) for m in range(DBG.get("m", NT))])
    P.barrier()
    P.release(mark)


def phase_att_sb2(cx):
    P = cx.P
    mark = P.mark()
    kv = KV(cx)
    M = causal_masks(cx, True)
    NIU = Buf(P, [128, 128], BF16, "niu")
    P.ts("dve", NIU[:], cx.d0t[:], -0.5, -1.0, ALU.is_gt, ALU.mult, reads=[cx.d0t.r], writes=[NIU.r])
    NS = DBG.get("sbslots", 3)
    slots = []
    for s in range(NS):
        slots.append(dict(
            E=[Buf(P, [128, 512], BF16, "E") for _ in range(2)], SP=[Buf(P, [128, 512], BF16, "SP") for _ in range(2)],
            TMP=[Buf(P, [128, 512], F32, "TMP") for _ in range(2)], C=Buf(P, [128, 512], F32, "C"),
            X=[Buf(P, [128, 512], BF16, "X") for _ in range(2)], PT=[Buf(P, [128, 512], BF16, "PT") for _ in range(2)],
            OT=Buf(P, [64, 4, 128], BF16, "OT"), AN=cx.ps[s], O=cx.ps[3 + s]))
    B, Br = cx.ps[6]
    bc4 = [[128, 4], [1, 128]]

    def stream(hg, m):
        def gen(s):
            sl = slots[s]
            (AN, ANr), (O, Or) = sl["AN"], sl["O"]
            C, OT = sl["C"], sl["OT"]
            kbs = list(range(4 * m + 3, -1, -1))
            for i, kb in enumerate(kbs):
                E, SP, TMP, X, PT = (sl[k][i % 2] for k in ("E", "SP", "TMP", "X", "PT"))
                rank, ml, r = kb % 4, kb // 4, kb - 4 * m
                last = i == len(kbs) - 1
                kv.qk(cx, AN, ANr, rank, ml, m)
                yield
                P.act(E[:], AN[:, :], AF.Exp, reads=[ANr], writes=[E.r])
                yield
                P.act(SP[:], E[:], AF.Ln, reads=[E.r], writes=[SP.r], bias=1.0)
                if r >= 0:
                    P.tt("dve", SP.ap(0, bc4), SP.ap(0, bc4), M.ap(r * 128, [[0, 4], [1, 128]]), ALU.mult,
                         reads=[SP.r, M.r], writes=[SP.r])
                yield
                P.mm(AN[:, :], NIU[:], SP[:], True, True, reads=[NIU.r, SP.r], writes=[ANr])
                if not last:
                    P.mm(B[:, :], cx.ones16[:], SP[:], True, True, reads=[cx.ones16.r, SP.r], writes=[Br])
                if i == 0:
                    P.copy("dve", TMP[:], AN[:, :], reads=[ANr], writes=[TMP.r])
                    if not last:
                        P.copy("dve", C[:], B[:, :], reads=[Br], writes=[C.r])
                else:
                    P.tt("dve", TMP[:], AN[:, :], C[:], ALU.subtract, reads=[ANr, C.r], writes=[TMP.r])
                    if not last:
                        P.tt("dve", C[:], C[:], B[:, :], ALU.add, reads=[C.r, Br], writes=[C.r])
                yield
                P.act(X[:], TMP[:], AF.Exp, reads=[TMP.r], writes=[X.r])
                yield
                P.tt("dve", PT[:], E[:], X[:], ALU.mult, reads=[E.r, X.r], writes=[PT.r])
                if r >= 0:
                    P.tt("dve", PT.ap(0, bc4), PT.ap(0, bc4), M.ap(r * 128, [[0, 4], [1, 128]]), ALU.mult,
                         reads=[PT.r, M.r], writes=[PT.r])
                yield
                kv.av(cx, O, Or, PT, rank, ml, i == 0, last, 64)
                yield
            P.copy("act", OT.ap(0, [[1, 512]], 0, 64), O[0:64, :], reads=[Or], writes=[OT.r])
            store_oT(cx, OT, hg * 256, m)
            yield
        return gen

    for hg in range(DBG.get("hg", 4)):
        kv.load(cx, hg * 256, hg * 256, hg * 256)
        run_streams([stream(hg, m) for m in range(DBG.get("m", NT))], nslots=NS)
    P.barrier()
    P.release(mark)


def softmax_finish(cx, src, src_off, RD, OT, hrow0, m):
    P = cx.P
    Bc, Bcr = cx.ps[6]
    P.op("dve", lambda e: e.reciprocal(out=RD.ap(0, [[1, 512]], 64, 1), in_=src.ap(src_off, [[1, 512]], 64, 1)),
         reads=[src.r], writes=[RD.r])
    P.mm(Bc[0:64, :], cx.ones32[64:65, 0:64], RD.ap(0, [[1, 512]], 64, 1), True, True,
         reads=[cx.ones32.r, RD.r], writes=[Bcr])
    P.tt("dve", OT.ap(0, [[1, 512]], 0, 64), src.ap(src_off, [[1, 512]], 0, 64), Bc[0:64, :], ALU.mult,
         reads=[src.r, Bcr], writes=[OT.r])
    store_oT(cx, OT, hrow0, m)


def phase_att_dil(cx):
    P = cx.P
    mark = P.mark()
    kv = KV(cx)
    RLO = (-1, -4, -16)
    idx = {}
    for g in range(3):
        for r in range(RLO[g], 4):
            idx[(g, r)] = len(idx)
    MD = Buf(P, [128, len(idx), 128], BF16, "MD")
    modm = [None, Buf(P, [128, 128], F32, "modm1"), Buf(P, [128, 128], F32, "modm2")]
    t1 = Buf(P, [128, 128], F32, "t1")
    t2 = Buf(P, [128, 128], F32, "t2")
    ti = Buf(P, [128, 128], I32, "ti")
    for g in (1, 2):
        dil = DIL_CFG[g][1]
        P.ts("dve", t1[:], cx.d0[:], 128.0, 1.0 / dil, ALU.add, ALU.mult, reads=[cx.d0.r], writes=[t1.r])
        P.copy("dve", ti[:], t1[:], reads=[t1.r], writes=[ti.r])
        P.copy("dve", t1[:], ti[:], reads=[ti.r], writes=[t1.r])
        P.ts("dve", t1[:], t1[:], float(dil), None, ALU.mult, reads=[t1.r], writes=[t1.r])
        P.ts("dve", t2[:], cx.d0[:], 128.0, None, ALU.add, reads=[cx.d0.r], writes=[t2.r])
        P.tt("dve", modm[g][:], t1[:], t2[:], ALU.is_equal, reads=[t1.r, t2.r], writes=[modm[g].r])
    for (g, r), ix in idx.items():
        W = DIL_CFG[g][0]
        P.ts("dve", t1[:], cx.d0[:], cx.cinfo[:, 0:1], 128.0 * r - 0.5, ALU.add, ALU.is_gt,
             reads=[cx.d0.r, cx.cinfo.r], writes=[t1.r])
        P.ts("dve", t2[:], cx.d0[:], cx.cinfo[:, 0:1], 128.0 * r + W + 0.5, ALU.add, ALU.is_lt,
             reads=[cx.d0.r, cx.cinfo.r], writes=[t2.r])
        if g == 0:
            P.tt("dve", MD[:, ix, :], t1[:], t2[:], ALU.mult, reads=[t1.r, t2.r], writes=[MD.r])
        else:
            P.tt("dve", t1[:], t1[:], t2[:], ALU.mult, reads=[t1.r, t2.r], writes=[t1.r])
            P.tt("dve", MD[:, ix, :], t1[:], modm[g][:], ALU.mult, reads=[t1.r, modm[g].r], writes=[MD.r])
    ACC = Buf(P, [65, NT, 512], F32, "ACC")
    RD = Buf(P, [65, 512], F32, "RD")
    slots = [dict(PT=[Buf(P, [128, 512], BF16, "PT") for _ in range(2)], OT=Buf(P, [64, 4, 128], BF16, "OT"),
                  A=cx.ps[s], O=cx.ps[3 + s]) for s in range(3)]

    def stream(hg, g, m):
        def gen(s):
            sl = slots[s]
            (A, Ar), (O, Or) = sl["A"], sl["O"]
            OT = sl["OT"]
            kbs = [4 * m + r for r in range(RLO[g], 4) if 4 * m + r >= 0]
            for i, kb in enumerate(kbs):
                PT = sl["PT"][i % 2]
                rank, ml, r = kb % 4, kb // 4, kb - 4 * m
                kv.qk(cx, A, Ar, rank, ml, m)
                yield
                P.act(PT[:], A[:, :], AF.Exp, reads=[Ar], writes=[PT.r])
                P.tt("dve", PT.ap(0, [[128, 4], [1, 128]]), PT.ap(0, [[128, 4], [1, 128]]),
                     MD.ap(idx[(g, r)] * 128, [[0, 4], [1, 128]]), ALU.mult, reads=[PT.r, MD.r], writes=[PT.r])
                yield
                kv.av(cx, O, Or, PT, rank, ml, i == 0, i == len(kbs) - 1, 65)
                yield
            if g == 0:
                P.copy("act", ACC.ap(m * 512, [[1, 512]]), O[0:65, :], reads=[Or], writes=[ACC.r])
            else:
                P.tt("dve", ACC.ap(m * 512, [[1, 512]]), ACC.ap(m * 512, [[1, 512]]), O[0:65, :], ALU.add,
                     reads=[ACC.r, Or], writes=[ACC.r])
            if g == 2:
                softmax_finish(cx, ACC, m * 512, RD, OT, hg * 256, m)
            yield
        return gen

    for hg in range(DBG.get("hg", 2)):
        for g in range(3):
            row0 = (g * 8 + hg * 4) * 64
            kv.load(cx, row0, row0, row0)
            run_streams([stream(hg, g, m) for m in range(DBG.get("m", NT))], nslots=3)
    P.barrier()
    P.release(mark)


def phase_att_dsa(cx):
    P = cx.P
    ms_ap, ms_r = cx.dram["mscr"]
    NM = DBG.get("m", NT)
    mark = P.mark()
    q_ap, q_r = cx.dram["qT"]
    k_ap, k_r = cx.dram["kTg"]
    w_ap, w_r = cx.dram["wi"]
    Qiz = [Buf(P, [128, 4, 2048], BF16, f"Qiz{i}") for i in range(2)]
    Ki2 = Buf(P, [128, 4, 2048], BF16, "Ki2")
    WI = Buf(P, [128, NT, 8], F32, "WI")
    Rb = [Buf(P, [128, 512], F32, "Rb") for _ in range(2)]
    MTs = [Buf(P, [128, 8, 128], BF16, "MTs") for _ in range(2)]
    NEGM = Buf(P, [128, 4, 128], F32, "NEGM")
    s1 = [dict(SC=Buf(P, [128, 4, 2048], F32, "SC"), MK=Buf(P, [128, 4, 2048], BF16, "MK"), sm=Buf(P, [128, 16], F32, "bsm"))
          for _ in range(2)]
    qsrc = q_ap[1024:1536, :].rearrange("(c p) t -> p c t", p=128)
    for par in range(2):
        P.memset("pool", Qiz[par][:], 0.0, writes=[Qiz[par].r])
        P.dma("sp", Qiz[par][par * 64:(par + 1) * 64, :, :], qsrc[par * 64:(par + 1) * 64, :, :],
              reads=[q_r], writes=[Qiz[par].r])
        P.dma("sp", Ki2[par * 64:(par + 1) * 64, :, :], k_ap[:, 1024:1088, :].rearrange("r p t -> p r t"),
              reads=[k_r], writes=[Ki2.r])
    P.dma("sp", WI[:], w_ap[:, :].rearrange("(m p) h -> p m h", p=128), reads=[w_r], writes=[WI.r])
    for r in range(4):
        P.ts("dve", NEGM[:, r, :], cx.d0t[:], cx.cinfo[:, 0:1], 128.0 * r - 0.5, ALU.add, ALU.is_lt,
             reads=[cx.d0t.r, cx.cinfo.r], writes=[NEGM.r])
        P.ts("dve", NEGM[:, r, :], NEGM[:, r, :], NEG, None, ALU.mult, reads=[NEGM.r], writes=[NEGM.r])
    cnts = {"mm": 0, "tr": 0}

    def mask_stream(m):
        def gen(s):
            SC, MK, sm = s1[s]["SC"], s1[s]["MK"], s1[s]["sm"]
            L = (m + 1) * 128
            for rank in range(4):
                for c0 in range(0, L, 512):
                    n = min(512, L - c0)
                    for ih in range(8):
                        St, Sr = cx.ps[cnts["mm"] % 4]
                        rb = Rb[cnts["mm"] % 2]
                        cnts["mm"] += 1
                        P.mm(St[:, 0:n], Qiz[ih % 2][:, ih // 2, m * 128:(m + 1) * 128], Ki2[:, rank, c0:c0 + n], True, True,
                             reads=[Qiz[ih % 2].r, Ki2.r], writes=[Sr])
                        P.act(rb[:, 0:n], St[:, 0:n], AF.Relu, reads=[Sr], writes=[rb.r])
                        if ih == 0:
                            P.ts("dve", SC[:, rank, c0:c0 + n], rb[:, 0:n], WI[:, m, 0:1], None, ALU.mult,
                                 reads=[rb.r, WI.r], writes=[SC.r])
                        else:
                            P.stt("dve", SC[:, rank, c0:c0 + n], rb[:, 0:n], WI[:, m, ih:ih + 1], SC[:, rank, c0:c0 + n],
                                  ALU.mult, ALU.add, reads=[rb.r, WI.r, SC.r], writes=[SC.r])
                    yield
            scv = SC.ap(0, [[2048, 4], [1, L]])
            P.op("dve", lambda e: e.tensor_reduce(out=sm[:, 0:1], in_=scv, axis=AX.XY, op=ALU.max), reads=[SC.r], writes=[sm.r])
            P.op("dve", lambda e: e.tensor_reduce(out=sm[:, 1:2], in_=scv, axis=AX.XY, op=ALU.min), reads=[SC.r], writes=[sm.r])
            yield
            P.ts("dve", sm[:, 1:2], sm[:, 1:2], -1.0, None, ALU.mult, reads=[sm.r], writes=[sm.r])
            P.tt("dve", sm[:, 2:3], sm[:, 0:1], sm[:, 1:2], ALU.max, reads=[sm.r], writes=[sm.r])
            P.ts("dve", sm[:, 3:4], sm[:, 2:3], 1.0, None, ALU.add, reads=[sm.r], writes=[sm.r])
            P.ts("dve", sm[:, 4:5], sm[:, 3:4], -1.0, None, ALU.mult, reads=[sm.r], writes=[sm.r])
            for rank in range(4):
                P.tt("dve", SC[:, rank, m * 128:(m + 1) * 128], SC[:, rank, m * 128:(m + 1) * 128], NEGM[:, rank, :], ALU.add,
                     reads=[SC.r, NEGM.r], writes=[SC.r])
            yield
            mkv = MK.ap(0, [[2048, 4], [1, L]])
            hi, lo, mid, cnt, ge, d1, d2, nmid = (sm[:, i:i + 1] for i in (3, 4, 5, 6, 7, 8, 9, 10))
            for it in range(DBG.get("bis", 15)):
                P.ts("dve", mid, lo, hi, 0.5, ALU.add, ALU.mult, reads=[sm.r], writes=[sm.r])
                yield
                if s == 1 and not DBG.get("noactcount"):
                    P.ts("dve", nmid, mid, -1.0, None, ALU.mult, reads=[sm.r], writes=[sm.r])
                    P.act(mkv, scv, AF.Sign, reads=[SC.r, sm.r], writes=[MK.r, sm.r], bias=nmid, accum_out=cnt)
                    yield
                    P.ts("dve", ge, cnt, 511.0 - 4 * L, None, ALU.is_gt, reads=[sm.r], writes=[sm.r])
                else:
                    P.ts("dve", mkv, scv, mid, 0.0, ALU.is_ge, ALU.add, reads=[SC.r, sm.r], writes=[MK.r, sm.r], accum_out=cnt)
                    yield
                    P.ts("dve", ge, cnt, 255.5, None, ALU.is_gt, reads=[sm.r], writes=[sm.r])
                P.tt("dve", d1, mid, lo, ALU.subtract, reads=[sm.r], writes=[sm.r])
                P.tt("dve", d2, hi, mid, ALU.subtract, reads=[sm.r], writes=[sm.r])
                yield
                P.stt("dve", lo, d1, ge, lo, ALU.mult, ALU.add, reads=[sm.r], writes=[sm.r])
                P.stt("dve", hi, d2, ge, mid, ALU.mult, ALU.add, reads=[sm.r], writes=[sm.r])
                yield
            P.ts("dve", mkv, scv, lo, None, ALU.is_ge, reads=[SC.r, sm.r], writes=[MK.r])
            yield
            for rank in range(4):
                for b0 in range(0, m + 1, 8):
                    nb = min(8, m + 1 - b0)
                    pbt, pbr = cx.pb[0]
                    mts = MTs[cnts["tr"] % 2]
                    cnts["tr"] += 1
                    for j in range(nb):
                        P.tr(pbt[:, j * 128:(j + 1) * 128], MK[:, rank, (b0 + j) * 128:(b0 + j + 1) * 128], cx.ident[:],
                             reads=[MK.r, cx.ident.r], writes=[pbr], acc=(j > 0))
                    P.copy("act", mts.ap(0, [[1, nb * 128]]), pbt[:, 0:nb * 128], reads=[pbr], writes=[mts.r])
                    P.dma("pool", ms_ap[m, :, rank * 16 + b0:rank * 16 + b0 + nb, :], mts[:, 0:nb, :],
                          reads=[mts.r], writes=[ms_r])
                    yield
        return gen

    run_streams([mask_stream(m) for m in range(NM)])
    P.barrier()
    P.release(mark)
    mark = P.mark()
    kv = KV(cx)
    RD = Buf(P, [65, 512], F32, "RD")
    slots = [dict(PT=[Buf(P, [128, 512], BF16, "PT") for _ in range(2)], OT=Buf(P, [64, 4, 128], BF16, "OT"),
                  ON=Buf(P, [65, 512], F32, "ON"), MT=Buf(P, [128, 4, 2048], BF16, "MT"),
                  A=cx.ps[s], O=cx.ps[3 + s]) for s in range(3)]

    def stream(hg, m):
        def gen(s):
            sl = slots[s]
            (A, Ar), (O, Or) = sl["A"], sl["O"]
            OT, ON, MT = sl["OT"], sl["ON"], sl["MT"]
            L = (m + 1) * 128
            P.dma("sp", MT[:, :, 0:L], ms_ap[m, :, :, :].rearrange("s (r b) t -> s r (b t)", r=4)[:, :, 0:L],
                  reads=[ms_r], writes=[MT.r])
            steps = [(rank, ml) for ml in range(m + 1) for rank in range(4)]
            for i, (rank, ml) in enumerate(steps):
                PT = sl["PT"][i % 2]
                kv.qk(cx, A, Ar, rank, ml, m)
                yield
                P.act(PT[:], A[:, :], AF.Exp, reads=[Ar], writes=[PT.r])
                P.tt("dve", PT.ap(0, [[128, 4], [1, 128]]), PT.ap(0, [[128, 4], [1, 128]]),
                     MT.ap(rank * 2048 + ml * 128, [[0, 4], [1, 128]]), ALU.mult, reads=[PT.r, MT.r], writes=[PT.r])
                yield
                kv.av(cx, O, Or, PT, rank, ml, i == 0, i == len(steps) - 1, 65)
                yield
            P.copy("act", ON[:], O[0:65, :], reads=[Or], writes=[ON.r])
            softmax_finish(cx, ON, 0, RD, OT, hg * 256, m)
            yield
        return gen

    for hg in range(DBG.get("hg", 4)):
        kv.load(cx, hg * 256, hg * 256, hg * 256)
        run_streams([stream(hg, m) for m in range(NM)], nslots=3)
    P.barrier()
    P.release(mark)


def phase_att_moba(cx):
    P = cx.P
    mark = P.mark()
    NM = DBG.get("m", NT)
    kv = KV(cx)
    Mle = causal_masks(cx, False)
    o_ap, o_r = cx.dram["oT"]
    kmT = Buf(P, [128, 2, 32], BF16, "kmT")
    KS = Buf(P, [128, 4, 16], F32, "KS")
    KM = Buf(P, [128, 32], F32, "KM")
    iotaI = Buf(P, [128, 32], I32, "iotaI")
    iotaN = Buf(P, [128, 32], F32, "iotaN")
    P.op("pool", lambda e: e.iota(iotaI[:], pattern=[[1, 32]], base=0, channel_multiplier=0), writes=[iotaI.r])
    P.copy("dve", iotaN[:], iotaI[:], reads=[iotaI.r], writes=[iotaN.r])
    chalf = cx.cinfo[:, 1:2]
    slots = []
    for s in range(3):
        slots.append(dict(
            PT=[Buf(P, [128, 512], BF16, "PT") for _ in range(2)], VAL=Buf(P, [128, 32], F32, "VAL"),
            EQ=Buf(P, [128, 32], F32, "EQ"), GN=Buf(P, [128, 32], F32, "GN"), GS=Buf(P, [128, 4, 32], F32, "GS"),
            M8=Buf(P, [128, 32], F32, "M8"), SEL=Buf(P, [128, 4, 32], F32, "SEL"), TMP=Buf(P, [128, 4, 65], F32, "TMP"),
            ACC=Buf(P, [128, 4, 65], F32, "ACC"), RC=Buf(P, [128, 4], F32, "RC"), OK=Buf(P, [128, 256], BF16, "OK"),
            OTt=Buf(P, [128, 2, 128], BF16, "OTt"), A=cx.ps[s], ON=cx.ps[3 + s], G=cx.ps[6]))

    def stream(hg, m):
        def gen(s):
            sl = slots[s]
            (A, Ar), (ON, ONr), (G, Gr) = sl["A"], sl["ON"], sl["G"]
            PTs, VAL, EQ, GN, GS, M8, SEL, TMP, ACC, RC, OK, OTt = (sl[k] for k in (
                "PT", "VAL", "EQ", "GN", "GS", "M8", "SEL", "TMP", "ACC", "RC", "OK", "OTt"))
            for h in range(4):
                qz = kv.Qz[h % 2]
                P.mm(G[:, h * 32:(h + 1) * 32], qz[:, h // 2, m * 128:(m + 1) * 128], kmT[:, h // 2, :], h == 0, h == 3,
                     reads=[qz.r, kmT.r], writes=[Gr], acc=(h > 0))
            P.ts("dve", VAL[:], iotaN[:], -2.0 * m, chalf, ALU.add, ALU.is_lt, reads=[iotaN.r, cx.cinfo.r], writes=[VAL.r])
            P.ts("dve", EQ[:], iotaN[:], -2.0 * m, chalf, ALU.add, ALU.is_equal, reads=[iotaN.r, cx.cinfo.r], writes=[EQ.r])
            P.ts("dve", GN[:], VAL[:], -NEG, NEG, ALU.mult, ALU.add, reads=[VAL.r], writes=[GN.r])
            P.tt("dve", GS.ap(0, [[32, 4], [1, 32]]), G[:, 0:128].rearrange("p (h n) -> p h n", h=4),
                 GN.ap(0, [[0, 4], [1, 32]]), ALU.add, reads=[Gr, GN.r], writes=[GS.r])
            for h in range(4):
                P.op("dve", lambda e, h=h: e.max(out=M8[:, h * 8:(h + 1) * 8], in_=GS[:, h, :]), reads=[GS.r], writes=[M8.r])
            for h in range(4):
                P.ts("dve", SEL[:, h, :], GS[:, h, :], M8[:, h * 8 + 2:h * 8 + 3], None, ALU.is_ge,
                     reads=[GS.r, M8.r], writes=[SEL.r])
            P.tt("dve", SEL.ap(0, [[32, 4], [1, 32]]), SEL.ap(0, [[32, 4], [1, 32]]), VAL.ap(0, [[0, 4], [1, 32]]), ALU.mult,
                 reads=[SEL.r, VAL.r], writes=[SEL.r])
            P.tt("dve", SEL.ap(0, [[32, 4], [1, 32]]), SEL.ap(0, [[32, 4], [1, 32]]), EQ.ap(0, [[0, 4], [1, 32]]), ALU.add,
                 reads=[SEL.r, EQ.r], writes=[SEL.r])
            yield
            nblk = 2 * m + 2
            for n in range(nblk):
                for kbi in range(2):
                    kb = 2 * n + kbi
                    rank, ml, r = kb % 4, kb // 4, kb - 4 * m
                    PT = PTs[kbi]
                    kv.qk(cx, A, Ar, rank, ml, m)
                    yield
                    P.act(PT[:], A[:, :], AF.Exp, reads=[Ar], writes=[PT.r])
                    if r >= 0:
                        P.tt("dve", PT.ap(0, [[128, 4], [1, 128]]), PT.ap(0, [[128, 4], [1, 128]]),
                             Mle.ap(r * 128, [[0, 4], [1, 128]]), ALU.mult, reads=[PT.r, Mle.r], writes=[PT.r])
                    yield
                    for h in range(4):
                        P.mm(ON[:, h * 65:h * 65 + 64], PT[:, h * 128:(h + 1) * 128], kv.Vs[:, rank, ml, h * 64:(h + 1) * 64],
                             kbi == 0 and h == 0, False, reads=[PT.r, kv.Vs.r], writes=[ONr], acc=(kbi > 0 or h > 0))
                        P.mm(ON[:, h * 65 + 64:h * 65 + 65], PT[:, h * 128:(h + 1) * 128], cx.ones16[:, 0:1],
                             False, kbi == 1 and h == 3, reads=[PT.r, cx.ones16.r], writes=[ONr], acc=True)
                    yield
                onv = ON[:, 0:260].rearrange("p (h d) -> p h d", h=4)
                selb = SEL.ap(n, [[32, 4], [0, 65]])
                if n == 0:
                    P.tt("dve", ACC.ap(0, [[65, 4], [1, 65]]), onv, selb, ALU.mult, reads=[ONr, SEL.r], writes=[ACC.r])
                else:
                    P.tt("dve", TMP.ap(0, [[65, 4], [1, 65]]), onv, selb, ALU.mult, reads=[ONr, SEL.r], writes=[TMP.r])
                    P.tt("pool", ACC[:], ACC[:], TMP[:], ALU.add, reads=[ACC.r, TMP.r], writes=[ACC.r])
            P.op("dve", lambda e: e.reciprocal(out=RC[:], in_=ACC.ap(64, [[65, 4]])), reads=[ACC.r], writes=[RC.r])
            P.tt("dve", OK.ap(0, [[64, 4], [1, 64]]), ACC.ap(0, [[65, 4], [1, 64]]), RC.ap(0, [[1, 4], [0, 64]]), ALU.mult,
                 reads=[ACC.r, RC.r], writes=[OK.r])
            transpose_rows(cx, OTt, OK, 2, 0, [OK.r])
            P.dma("pool", o_ap[hg * 256:(hg + 1) * 256, m * 128:(m + 1) * 128].rearrange("(c p) t -> p c t", p=128),
                  OTt[:], reads=[OTt.r], writes=[o_r])
            yield
        return gen

    for hg in range(DBG.get("hg", 4)):
        kv.load(cx, hg * 256, hg * 256, hg * 256)
        for ch in range(2):
            ksrc = kv.Ks.ap(ch * 8192, [[128, 64], [1, 128]])
            P.op("dve", lambda e, ksrc=ksrc: e.tensor_reduce(out=KS.ap(0, [[1, 64]]), in_=ksrc, axis=AX.X, op=ALU.add),
                 reads=[kv.Ks.r], writes=[KS.r])
            P.tt("dve", KM.ap(0, [[2, 16]]), KS[:, 0, :], KS[:, 1, :], ALU.add, reads=[KS.r], writes=[KM.r])
            P.tt("dve", KM.ap(1, [[2, 16]]), KS[:, 2, :], KS[:, 3, :], ALU.add, reads=[KS.r], writes=[KM.r])
            P.ts("dve", kmT[:, ch, :], KM[:], 1.0 / 256, None, ALU.mult, reads=[KM.r], writes=[kmT.r])
        run_streams([stream(hg, m) for m in range(NM)], nslots=3)
    P.barrier()
    P.release(mark)


ATT_FUNCS = {}
DBG = {}


def build_stage(k):
    nc = bass.Bass("TRN2", target_bir_lowering=False)
    stack = contextlib.ExitStack()
    cx = Ctx(nc, stack)
    cx.din("cinfo", [128, 4], F32)
    li = k if k < 4 else None
    pl = k - 1 if k >= 1 else None
    if pl is not None:
        Lp = LAYERS[pl]
        cx.din("qT", [Lp["nq"] * 64, TOK], BF16)
        cx.din("kTg", [4, Lp["nk"] * 64, TOK], BF16)
        cx.din("vg", [4, TOK, Lp["nv"]], BF16)
        if Lp["name"] == "dsa":
            cx.din("wi", [TOK, 8], F32)
        cx.din("h_res", [TOK, D], F32)
        cx.din("w_out", [Lp["no"], D], F32)
        cx.din("w_ff_in", [D, 4096], F32)
        cx.din("w_ff_out", [4096, D], F32)
        for g in ("g_mix_post", "g_ffn_pre", "g_ffn_post", "g_ple"):
            cx.din(g, [1, D], F32)
        cx.din("w_ple_gate", [D, D], F32)
        cx.din("w_ple", [256, D], F32)
        cx.din("p", [TOK, 256], F32)
        if Lp["name"] == "dsa":
            cx.dint("mscr", [NT, 128, 64, 128], BF16)
        cx.dint("oT", [Lp["no"], TOK], BF16)
        cx.dint("h_mid", [TOK, D], F32)
        cx.dint("h_a", [TOK, D], F32)
        cx.dram["h_in"] = cx.dram["h_mid"]
        cx.dout("h_out", [TOK, D], F32)
    else:
        cx.din("h_in", [TOK, D], F32)
    if li is not None:
        L = LAYERS[li]
        cx.din("w_in", [D, (L["nq"] + L["nk"]) * 64 + L["nv"]], F32)
        cx.din("g_mix_pre", [1, D], F32)
        cx.dout("qT_o", [L["nq"] * 64, TOK], BF16)
        cx.dout("kT_o", [L["nk"] * 64, TOK], BF16)
        cx.dout("tv_o", [TOK, L["nv"]], BF16)
        if L["name"] == "dsa":
            cx.dout("wi_o", [TOK, 8], F32)
        if L["rope"]:
            cx.din("pos", [128, NT], I32)
            cx.din("invf", [128, 8], F32)
    cx.consts()
    if pl is not None:
        if not DBG.get("noatt"):
            ATT_FUNCS[LAYERS[pl]["name"]](cx)
        if not DBG.get("nopost"):
            (phase_post if DBG.get("oldpost") else phase_post2)(cx, pl)
    if not DBG.get("nopre"):
        (phase_pre if DBG.get("oldpre") else phase_pre2)(cx, li, pl)
    cx.P.barrier()
    cx.P.emit()
    stack.close()
    return nc, cx


def _rows(a, c):
    F_ = a.shape[-1]
    return np.ascontiguousarray(a.reshape(16, 4, 128, F_)[:, c].reshape(TOK, F_))


def _w_in_perm(inp, li):
    if li == 0:
        return inp["w_in_sb"][0]
    if li == 3:
        return inp["w_in_moba"][0]
    if li == 1:
        W = inp["w_in_dil"][0].reshape(D, 3, 3, 512)
        return np.ascontiguousarray(np.concatenate(
            [W[:, g, 0] for g in range(3)] + [W[:, g, 1] for g in range(3)] + [W[:, g, 2] for g in range(3)], axis=1))
    W = inp["w_in_dsa"][0]
    q, kk, v, qi, ki, wi = W[:, 0:1024], W[:, 1024:2048], W[:, 2048:3072], W[:, 3072:3584], W[:, 3584:3648], W[:, 3648:3656]
    return np.ascontiguousarray(np.concatenate([q, qi, kk, ki, np.zeros((D, 64), np.float32), v, wi], axis=1))


_W_OUT = ("w_out_sb", "w_out_dil", "w_out_dsa", "w_out_moba")
_PROGS = {}


def _get_prog(k):
    if k not in _PROGS:
        _PROGS[k] = build_stage(k)[0]
    return _PROGS[k]


def run_stage(k, inp, state):
    nc = _get_prog(k)
    li = k if k < 4 else None
    pl = k - 1 if k >= 1 else None
    invf = np.tile((500000.0 ** (-np.arange(0, 16, 2, dtype=np.float32) / 16)).astype(np.float32)[None, :], (128, 1))
    in_maps = []
    for core in range(8):
        b, c = core // 4, core % 4
        st = state[core]
        d = {"cinfo": np.tile(np.array([[128.0 * c, float(c // 2), 0.0, 0.0]], np.float32), (128, 1))}
        if pl is not None:
            grp = [state[b * 4 + cc] for cc in range(4)]
            d["qT"] = st["qT_o"]
            d["kTg"] = np.ascontiguousarray(np.stack([g["kT_o"] for g in grp], axis=0))
            d["vg"] = np.ascontiguousarray(np.stack([g["tv_o"] for g in grp], axis=0))
            if LAYERS[pl]["name"] == "dsa":
                d["wi"] = st["wi_o"]
            d["h_res"] = st["h"]
            d["w_out"] = inp[_W_OUT[pl]][0]
            d["w_ff_in"] = inp["w_ff_in"][pl]
            d["w_ff_out"] = inp["w_ff_out"][pl]
            for g in ("g_mix_post", "g_ffn_pre", "g_ffn_post", "g_ple"):
                d[g] = inp[g][pl][None, :]
            d["w_ple_gate"] = inp["w_ple_gate"][pl]
            d["w_ple"] = inp["w_ple"][pl]
            d["p"] = _rows(inp["p"][pl, b], c)
        else:
            d["h_in"] = st["h"]
        if li is not None:
            d["w_in"] = _w_in_perm(inp, li)
            d["g_mix_pre"] = inp["g_mix_pre"][li][None, :]
            if LAYERS[li]["rope"]:
                pos = _rows(inp["positions"][b][:, None].astype(np.int32), c)[:, 0]
                d["pos"] = np.ascontiguousarray(pos.reshape(NT, 128).T)
                d["invf"] = invf
        in_maps.append({kk: np.ascontiguousarray(v) for kk, v in d.items()})
    res = run_bass_kernel_spmd(nc, in_maps, core_ids=list(range(8)))
    new = []
    for core in range(8):
        r = res.results[core]
        st = dict(state[core])
        if pl is not None:
            st["h"] = np.asarray(r["h_out"])
        for nm in ("qT_o", "kT_o", "tv_o", "wi_o"):
            if nm in r:
                st[nm] = np.asarray(r[nm])
        new.append(st)
    return new


def kernel(**inp):
    inp = {k: np.asarray(v) for k, v in inp.items()}
    state = []
    for core in range(8):
        b, c = core // 4, core % 4
        state.append({"h": _rows(inp["x"][b].astype(np.float32), c)})
    for k in range(5):
        state = run_stage(k, inp, state)
    out = np.empty((2, 8192, D), np.float32)
    for core in range(8):
        b, c = core // 4, core % 4
        out[b].reshape(16, 4, 128, D)[:, c] = state[core]["h"].reshape(16, 128, D)
    return out


ATT_FUNCS["sb"] = phase_att_sb2
ATT_FUNCS["dil"] = phase_att_dil
ATT_FUNCS["dsa"] = phase_att_dsa
ATT_FUNCS["moba"] = phase_att_moba


_GAINS = ("g_mix_pre", "g_mix_post", "g_ffn_pre", "g_ffn_post", "g_ple")
_LW = ("w_ff_in", "w_ff_out", "w_ple_gate", "w_ple")


def build_fused():
    nc = bass.Bass("TRN2", target_bir_lowering=False)
    stack = contextlib.ExitStack()
    cx = Ctx(nc, stack)
    P = cx.P
    cx.din("cinfo", [128, 4], F32)
    cx.din("pos", [128, NT], I32)
    cx.din("invf", [128, 8], F32)
    cx.din("x", [TOK, D], F32)
    for i, L in enumerate(LAYERS):
        cx.din(f"w_in_{i}", [D, (L["nq"] + L["nk"]) * 64 + L["nv"]], F32)
        cx.din(f"w_out_{i}", [L["no"], D], F32)
        cx.din(f"w_ff_in_{i}", [D, 4096], F32)
        cx.din(f"w_ff_out_{i}", [4096, D], F32)
        cx.din(f"w_ple_gate_{i}", [D, D], F32)
        cx.din(f"w_ple_{i}", [256, D], F32)
        cx.din(f"p_{i}", [TOK, 256], F32)
        for g in _GAINS:
            cx.din(f"{g}_{i}", [1, D], F32)
        cx.dint(f"qT_{i}", [L["nq"] * 64, TOK], BF16)
        cx.dint(f"kT_{i}", [L["nk"] * 64, TOK], BF16)
        cx.dint(f"tv_{i}", [TOK, L["nv"]], BF16)
        cx.dint(f"kTg_{i}", [4 * L["nk"] * 64, TOK], BF16)
        cx.dint(f"vg_{i}", [4 * TOK, L["nv"]], BF16)
        cx.dint(f"oT_{i}", [L["no"], TOK], BF16)
        cx.dint(f"hmid_{i}", [TOK, D], F32)
        if i < 3:
            cx.dint(f"h_{i}", [TOK, D], F32)
    cx.dint("wi_2", [TOK, 8], F32)
    cx.dint("mscr", [NT, 128, 64, 128], BF16)
    cx.dout("h_out", [TOK, D], F32)
    cx.consts()

    def alias(**kw):
        for k, v in kw.items():
            cx.dram[k] = cx.dram[v]

    def pre_alias(i):
        alias(w_in=f"w_in_{i}", g_mix_pre=f"g_mix_pre_{i}", qT_o=f"qT_{i}", kT_o=f"kT_{i}", tv_o=f"tv_{i}", wi_o="wi_2")

    alias(h_in="x")
    pre_alias(0)
    phase_pre(cx, 0, None)
    groups = [[0, 1, 2, 3], [4, 5, 6, 7]]
    for i, L in enumerate(LAYERS):
        for src, dst in ((f"kT_{i}", f"kTg_{i}"), (f"tv_{i}", f"vg_{i}")):
            (s_ap, s_r), (d_ap, d_r) = cx.dram[src], cx.dram[dst]
            P.op("pool", lambda e, s_ap=s_ap, d_ap=d_ap: e.collective_compute(
                "AllGather", ALU.bypass, replica_groups=groups, ins=[s_ap], outs=[d_ap]),
                reads=[s_r], writes=[d_r], dma=True)
        kg_ap, kg_r = cx.dram[f"kTg_{i}"]
        vg_ap, vg_r = cx.dram[f"vg_{i}"]
        cx.dram["kTg"] = (kg_ap.rearrange("(r f) t -> r f t", r=4), kg_r)
        cx.dram["vg"] = (vg_ap.rearrange("(r t) v -> r t v", r=4), vg_r)
        alias(qT=f"qT_{i}", wi="wi_2", oT=f"oT_{i}")
        ATT_FUNCS[L["name"]](cx)
        alias(h_res=("x" if i == 0 else f"h_{i - 1}"), h_mid=f"hmid_{i}", w_out=f"w_out_{i}", w_ff_in=f"w_ff_in_{i}",
              w_ff_out=f"w_ff_out_{i}", g_mix_post=f"g_mix_post_{i}", g_ffn_pre=f"g_ffn_pre_{i}",
              g_ffn_post=f"g_ffn_post_{i}")
        phase_post(cx, i)
        alias(h_in=f"hmid_{i}", h_out=(f"h_{i}" if i < 3 else "h_out"), p=f"p_{i}", w_ple_gate=f"w_ple_gate_{i}",
              w_ple=f"w_ple_{i}", g_ple=f"g_ple_{i}")
        if i < 3:
            pre_alias(i + 1)
        phase_pre(cx, i + 1 if i < 3 else None, i)
    P.barrier()
    P.emit()
    stack.close()
    return nc, cx


def kernel_fused(**inp):
    inp = {k: np.asarray(v) for k, v in inp.items()}
    if "fused" not in _PROGS:
        _PROGS["fused"] = build_fused()[0]
    nc = _PROGS["fused"]
    invf = np.tile((500000.0 ** (-np.arange(0, 16, 2, dtype=np.float32) / 16)).astype(np.float32)[None, :], (128, 1))
    shared = {"invf": invf}
    for i in range(4):
        shared[f"w_in_{i}"] = _w_in_perm(inp, i)
        shared[f"w_out_{i}"] = inp[_W_OUT[i]][0]
        for w in _LW:
            shared[f"{w}_{i}"] = inp[w][i]
        for g in _GAINS:
            shared[f"{g}_{i}"] = inp[g][i][None, :]
    shared = {k: np.ascontiguousarray(v) for k, v in shared.items()}
    in_maps = []
    for core in range(8):
        b, c = core // 4, core % 4
        d = dict(shared)
        d["cinfo"] = np.tile(np.array([[128.0 * c, float(c // 2), 0.0, 0.0]], np.float32), (128, 1))
        pos = _rows(inp["positions"][b][:, None].astype(np.int32), c)[:, 0]
        d["pos"] = np.ascontiguousarray(pos.reshape(NT, 128).T)
        d["x"] = _rows(inp["x"][b].astype(np.float32), c)
        for i in range(4):
            d[f"p_{i}"] = _rows(inp["p"][i, b], c)
        in_maps.append(d)
    res = run_bass_kernel_spmd(nc, in_maps, core_ids=list(range(8)))
    out = np.empty((2, 8192, D), np.float32)
    for core in range(8):
        b, c = core // 4, core % 4
        out[b].reshape(16, 4, 128, D)[:, c] = np.asarray(res.results[core]["h_out"]).reshape(16, 128, D)
    return out
```
